# Optimizing a Trainium2 kernel written in Bass

```python
import math
import jax, jax.numpy as jnp
from jax import lax
import numpy as np

D_MODEL = 1024
BATCH = 8
SEQ = 2048
DEPTH = 2
DEC_BATCH = 128
DEC_SEQ = 8
PAST_LEN = 16384
PAGE_SIZE = 128

N_MIXERS = 2
N_RET_LAYERS = (DEPTH + 1) // 2
N_SWA_LAYERS = DEPTH // 2

RET_HEADS = 4
RET_DK = D_MODEL // RET_HEADS
RET_DV = 2 * D_MODEL // RET_HEADS
RET_CHUNK = 128
ROPE_BASE = 10000.0

SWA_HEADS = 16
SWA_KV_HEADS = 2
SWA_HEAD_DIM = 64
SWA_GROUP = SWA_HEADS // SWA_KV_HEADS
WINDOW = 128
SWA_QKV_WIDTH = (SWA_HEADS + 2 * SWA_KV_HEADS) * SWA_HEAD_DIM

D_FF = -(-8 * D_MODEL // (3 * 256)) * 256
RMS_EPS = 1e-6
GN_EPS = 1e-6
NEG_INF = -1e30

kernel_name = "hybrid_retention_swa_sink_decoder_step"


def rms_norm(x, g):
    xf = x.astype(jnp.float32)
    y = xf * lax.rsqrt(jnp.mean(xf * xf, axis=-1, keepdims=True) + RMS_EPS)
    return (y * g.astype(jnp.float32)).astype(x.dtype)


def swiglu(x, w1, w3, w2):
    return (jax.nn.silu(x @ w1) * (x @ w3)) @ w2


def rotate(x, pos):
    half = x.shape[-1] // 2
    inv = 1.0 / (ROPE_BASE ** (jnp.arange(half, dtype=jnp.float32) / half))
    ang = pos.astype(jnp.float32)[:, None] * inv[None, :]
    cos = jnp.cos(ang)[None, :, None, :]
    sin = jnp.sin(ang)[None, :, None, :]
    x1, x2 = x[..., :half], x[..., half:]
    return jnp.concatenate([x1 * cos - x2 * sin, x1 * sin + x2 * cos], axis=-1)


def retention_chunkwise(q, k, v, s0, chunk):
    b, t, h, _ = q.shape
    dv = v.shape[-1]
    nc = t // chunk
    log_gamma = jnp.log1p(-jnp.exp2(-(5.0 + jnp.arange(h, dtype=jnp.float32))))
    idx = jnp.arange(chunk, dtype=jnp.float32)
    diff = idx[:, None] - idx[None, :]
    intra = jnp.where(diff >= 0, jnp.exp(log_gamma[:, None, None] * jnp.maximum(diff, 0.0)), 0.0)
    q_decay = jnp.exp(log_gamma[None, :] * (idx[:, None] + 1.0))
    k_decay = jnp.exp(log_gamma[None, :] * (chunk - 1.0 - idx[:, None]))
    chunk_decay = jnp.exp(log_gamma * chunk)

    def to_chunks(a):
        return jnp.moveaxis(a.reshape(b, nc, chunk, h, a.shape[-1]), 1, 0)

    def step(s, qkv):
        qc, kc, vc = qkv
        scores = jnp.einsum('bihd,bjhd->bhij', qc, kc) * intra[None]
        o = jnp.einsum('bhij,bjhe->bihe', scores, vc)
        o = o + jnp.einsum('bihd,bhde->bihe', qc, s) * q_decay[None, :, :, None]
        s = s * chunk_decay[None, :, None, None] + jnp.einsum(
            'bjhd,bjhe->bhde', kc * k_decay[None, :, :, None], vc)
        return s, o

    s, o = lax.scan(step, s0, (to_chunks(q), to_chunks(k), to_chunks(v)))
    o = jnp.moveaxis(o, 0, 1).reshape(b, t, h, dv)
    return o, s


def retention_mixer(h, pos, s0, chunk, w_q, w_k, w_v, w_g, w_o):
    b, t, _ = h.shape
    f32 = jnp.float32
    q = (h @ w_q).reshape(b, t, RET_HEADS, RET_DK).astype(f32)
    k = (h @ w_k).reshape(b, t, RET_HEADS, RET_DK).astype(f32)
    v = (h @ w_v).reshape(b, t, RET_HEADS, RET_DV).astype(f32)
    q = rotate(q, pos)
    k = rotate(k, pos) * (RET_DK ** -0.5)
    o, s = retention_chunkwise(q, k, v, s0.astype(f32), chunk)
    mu = jnp.mean(o, axis=-1, keepdims=True)
    var = jnp.mean(jnp.square(o - mu), axis=-1, keepdims=True)
    o = ((o - mu) * lax.rsqrt(var + GN_EPS)).reshape(b, t, RET_HEADS * RET_DV).astype(h.dtype)
    y = (jax.nn.silu(h @ w_g) * o) @ w_o
    return y, s


def swa_project(h, w_qkv, b_qkv):
    b, t, _ = h.shape
    qkv = h @ w_qkv + b_qkv
    nq = SWA_HEADS * SWA_HEAD_DIM
    nk = SWA_KV_HEADS * SWA_HEAD_DIM
    q = qkv[..., :nq].reshape(b, t, SWA_KV_HEADS, SWA_GROUP, SWA_HEAD_DIM)
    k = qkv[..., nq:nq + nk].reshape(b, t, SWA_KV_HEADS, SWA_HEAD_DIM)
    v = qkv[..., nq + nk:].reshape(b, t, SWA_KV_HEADS, SWA_HEAD_DIM)
    return q, k, v


def sink_attention(q, k, v, mask, sinks):
    s = jnp.einsum('bnqhgd,bnkhd->bnhgqk', q, k, preferred_element_type=jnp.float32) * (SWA_HEAD_DIM ** -0.5)
    s = jnp.where(mask[None, :, None, None, :, :], s, NEG_INF)
    sink = sinks.astype(jnp.float32).reshape(SWA_KV_HEADS, SWA_GROUP)[None, None, :, :, None, None]
    m = jnp.maximum(jnp.max(s, axis=-1, keepdims=True), sink)
    p = jnp.exp(s - m)
    p = p / (jnp.sum(p, axis=-1, keepdims=True) + jnp.exp(sink - m))
    return jnp.einsum('bnhgqk,bnkhd->bnqhgd', p.astype(v.dtype), v)


def swa_prompt(h, w_qkv, b_qkv, w_o, b_o, sinks):
    b, t, _ = h.shape
    nb = t // WINDOW
    q, k, v = swa_project(h, w_qkv, b_qkv)
    qb = q.reshape(b, nb, WINDOW, SWA_KV_HEADS, SWA_GROUP, SWA_HEAD_DIM)
    kb = k.reshape(b, nb, WINDOW, SWA_KV_HEADS, SWA_HEAD_DIM)
    vb = v.reshape(b, nb, WINDOW, SWA_KV_HEADS, SWA_HEAD_DIM)
    pad = ((0, 0), (1, 0), (0, 0), (0, 0), (0, 0))
    k_band = jnp.concatenate([jnp.pad(kb, pad)[:, :-1], kb], axis=2)
    v_band = jnp.concatenate([jnp.pad(vb, pad)[:, :-1], vb], axis=2)
    qpos = jnp.arange(t).reshape(nb, WINDOW)
    kpos = jnp.concatenate([qpos - WINDOW, qpos], axis=1)
    diff = qpos[:, :, None] - kpos[:, None, :]
    mask = (kpos[:, None, :] >= 0) & (diff >= 0) & (diff < WINDOW)
    o = sink_attention(qb, k_band, v_band, mask, sinks)
    y = o.reshape(b, t, SWA_HEADS * SWA_HEAD_DIM) @ w_o + b_o
    return y, k[:, t - WINDOW:], v[:, t - WINDOW:]


def swa_sample(h, ck, cv, w_qkv, b_qkv, w_o, b_o, sinks):
    b, t, _ = h.shape
    q, k, v = swa_project(h, w_qkv, b_qkv)
    k_all = jnp.concatenate([ck.astype(k.dtype), k], axis=1)
    v_all = jnp.concatenate([cv.astype(v.dtype), v], axis=1)
    qpos = PAST_LEN + jnp.arange(t)
    kpos = jnp.concatenate([PAST_LEN - WINDOW + jnp.arange(WINDOW), qpos])
    diff = qpos[:, None] - kpos[None, :]
    mask = ((diff >= 0) & (diff < WINDOW))[None]
    o = sink_attention(q[:, None], k_all[:, None], v_all[:, None], mask, sinks)
    y = o.reshape(b, t, SWA_HEADS * SWA_HEAD_DIM) @ w_o + b_o
    return y, k_all[:, -WINDOW:], v_all[:, -WINDOW:]


def setup_inputs(seed: int = 0) -> dict:
    key = jax.random.key(seed)
    ks = jax.random.split(key, 24)
    f32 = jnp.float32
    nr, ns = N_RET_LAYERS, N_SWA_LAYERS

    def w(k, shape, fan_in):
        return jax.random.normal(k, shape, f32) * (fan_in ** -0.5)

    return {
        "x_prompt": jax.random.normal(ks[0], (BATCH, SEQ, D_MODEL), f32),
        "x_sample": jax.random.normal(ks[1], (DEC_BATCH, DEC_SEQ, D_MODEL), f32),
        "state_ret": 0.5 * jax.random.normal(ks[2], (nr, DEC_BATCH, RET_HEADS, RET_DK, RET_DV), f32),
        "cache_swa_k": jax.random.normal(ks[3], (ns, DEC_BATCH, WINDOW, SWA_KV_HEADS, SWA_HEAD_DIM), f32),
        "cache_swa_v": jax.random.normal(ks[4], (ns, DEC_BATCH, WINDOW, SWA_KV_HEADS, SWA_HEAD_DIM), f32),
        "ret_w_q": w(ks[5], (nr, D_MODEL, RET_HEADS * RET_DK), D_MODEL),
        "ret_w_k": w(ks[6], (nr, D_MODEL, RET_HEADS * RET_DK), D_MODEL),
        "ret_w_v": w(ks[7], (nr, D_MODEL, RET_HEADS * RET_DV), D_MODEL),
        "ret_w_g": w(ks[8], (nr, D_MODEL, RET_HEADS * RET_DV), D_MODEL),
        "ret_w_o": w(ks[9], (nr, RET_HEADS * RET_DV, D_MODEL), RET_HEADS * RET_DV),
        "swa_w_qkv": w(ks[10], (ns, D_MODEL, SWA_QKV_WIDTH), D_MODEL),
        "swa_b_qkv": 0.02 * jax.random.normal(ks[11], (ns, SWA_QKV_WIDTH), f32),
        "swa_w_o": w(ks[12], (ns, SWA_HEADS * SWA_HEAD_DIM, D_MODEL), SWA_HEADS * SWA_HEAD_DIM),
        "swa_b_o": 0.02 * jax.random.normal(ks[13], (ns, D_MODEL), f32),
        "swa_sinks": 0.5 * jax.random.normal(ks[14], (ns, SWA_HEADS), f32),
        "norm_mix": 1.0 + 0.02 * jax.random.normal(ks[15], (DEPTH, D_MODEL), f32),
        "norm_ffn": 1.0 + 0.02 * jax.random.normal(ks[16], (DEPTH, D_MODEL), f32),
        "ffn_w1": w(ks[17], (DEPTH, D_MODEL, D_FF), D_MODEL),
        "ffn_w3": w(ks[18], (DEPTH, D_MODEL, D_FF), D_MODEL),
        "ffn_w2": w(ks[19], (DEPTH, D_FF, D_MODEL), D_FF),
        "norm_final": 1.0 + 0.02 * jax.random.normal(ks[20], (D_MODEL,), f32),
    }


def reference(x_prompt, x_sample, state_ret, cache_swa_k, cache_swa_v,
              ret_w_q, ret_w_k, ret_w_v, ret_w_g, ret_w_o,
              swa_w_qkv, swa_b_qkv, swa_w_o, swa_b_o, swa_sinks,
              norm_mix, norm_ffn, ffn_w1, ffn_w3, ffn_w2, norm_final):
    b_p, t_p, _ = x_prompt.shape
    pos_p = jnp.arange(t_p)
    pos_s = PAST_LEN + jnp.arange(x_sample.shape[1])
    xp, xs = x_prompt, x_sample
    ret_p, ret_s, kp_l, vp_l, ks_l, vs_l = [], [], [], [], [], []
    for layer in range(DEPTH):
        hp = rms_norm(xp, norm_mix[layer])
        hs = rms_norm(xs, norm_mix[layer])
        if layer % N_MIXERS == 0:
            r = layer // N_MIXERS
            rw = (ret_w_q[r], ret_w_k[r], ret_w_v[r], ret_w_g[r], ret_w_o[r])
            s0 = jnp.zeros((b_p, RET_HEADS, RET_DK, RET_DV), jnp.float32)
            yp, sp = retention_mixer(hp, pos_p, s0, RET_CHUNK, *rw)
            ys, ss = retention_mixer(hs, pos_s, state_ret[r], xs.shape[1], *rw)
            ret_p.append(sp.astype(x_prompt.dtype))
            ret_s.append(ss.astype(state_ret.dtype))
        else:
            a = layer // N_MIXERS
            aw = (swa_w_qkv[a], swa_b_qkv[a], swa_w_o[a], swa_b_o[a], swa_sinks[a])
            yp, kp, vp = swa_prompt(hp, *aw)
            ys, kn, vn = swa_sample(hs, cache_swa_k[a], cache_swa_v[a], *aw)
            kp_l.append(kp); vp_l.append(vp)
            ks_l.append(kn.astype(cache_swa_k.dtype)); vs_l.append(vn.astype(cache_swa_v.dtype))
        xp = xp + yp
        xs = xs + ys
        xp = xp + swiglu(rms_norm(xp, norm_ffn[layer]), ffn_w1[layer], ffn_w3[layer], ffn_w2[layer])
        xs = xs + swiglu(rms_norm(xs, norm_ffn[layer]), ffn_w1[layer], ffn_w3[layer], ffn_w2[layer])
    y_prompt = rms_norm(xp, norm_final)
    y_sample = rms_norm(xs, norm_final)
    state_ret_prompt = jnp.stack(ret_p)
    state_ret_sample = jnp.stack(ret_s)
    swa_k_prompt = jnp.stack(kp_l)
    swa_v_prompt = jnp.stack(vp_l)
    swa_k_sample = jnp.stack(ks_l)
    swa_v_sample = jnp.stack(vs_l)
    return (y_prompt, y_sample, state_ret_prompt, state_ret_sample,
            swa_k_prompt, swa_v_prompt, swa_k_sample, swa_v_sample)
```

```python
import contextlib
import numpy as np
import concourse.bass as bass
import concourse.mybir as mybir
from concourse.bass_utils import run_bass_kernel_spmd

F32 = mybir.dt.float32
BF16 = mybir.dt.bfloat16
AF = mybir.ActivationFunctionType
ALU = mybir.AluOpType

NCORES = 8
D = 1024
KC = 8
SEQ = 2048
TT = 512
NPT = SEQ // TT
NS = 128
NB = 16
DS = 8
PAST = 16384
DFF = 2816
NF = DFF // 128
RH = 4
EPS = 1e-6
WSLOT = 4096
NW = 4
DEBUG = False
DEBUG_TILE = 0


class Buf:
    __slots__ = ("name", "excl", "const", "w", "rs")

    def __init__(self, name, excl=False, const=False, pending=None):
        self.name = name
        self.excl = excl
        self.const = const
        self.w = None
        self.rs = dict(pending) if pending else {}


class Sched:
    ENG = ("pe", "act", "dve", "pool", "sp")

    def __init__(self):
        self.ops = []
        self.cnt = {e: 0 for e in self.ENG}
        self.dcount = {}
        self.sig = set()

    def op(self, eng, fn, r=(), w=(), dsem=None):
        self.cnt[eng] += 1
        idx = self.cnt[eng]
        deps = {}

        def need(key, val):
            if deps.get(key, -1) < val:
                deps[key] = val

        for b in r:
            if b.w is not None:
                need(*b.w)
            if b.excl:
                for k, v in b.rs.items():
                    if k != ("e", eng):
                        need(k, v)
        for b in w:
            if b.w is not None:
                need(*b.w)
            for k, v in b.rs.items():
                need(k, v)
        if dsem is not None:
            self.dcount[dsem] = self.dcount.get(dsem, 0) + 16
            ev = (("d", dsem), self.dcount[dsem])
        else:
            ev = (("e", eng), idx)
        waits = []
        for key, val in deps.items():
            if key[0] == "e":
                if key[1] == "pe" and eng == "pe":
                    continue
                if key == ev[0] and val >= idx:
                    continue
                waits.append((key, val))
                self.sig.add((key[1], val))
            else:
                v = self.dcount[key[1]]
                if key == ev[0]:
                    v -= 16
                if v > 0:
                    waits.append((key, v))
        self.ops.append((eng, idx, fn, waits, dsem))
        for b in w:
            b.w = ev
            b.rs = {}
        for b in r:
            if b.const or b in w:
                continue
            b.rs[ev[0]] = ev[1]

    def emit(self, nc, es):
        handles = {"pe": nc.tensor, "act": nc.scalar, "dve": nc.vector, "pool": nc.gpsimd, "sp": nc.sync}
        sems = {e: es.enter_context(nc.semaphore("s_" + e)) for e in self.ENG}
        dsems = {n: es.enter_context(nc.semaphore("d_" + n)) for n in self.dcount}
        rank = {}
        for e in self.ENG:
            ids = sorted(i for (en, i) in self.sig if en == e)
            for k, i in enumerate(ids):
                rank[(e, i)] = k + 1
        seen = {e: {} for e in self.ENG}
        for (eng, idx, fn, waits, dsem) in self.ops:
            h = handles[eng]
            for key, val in waits:
                if key[0] == "e":
                    v = rank[(key[1], val)]
                    sem = sems[key[1]]
                else:
                    v = val
                    sem = dsems[key[1]]
                if seen[eng].get(key, 0) >= v:
                    continue
                seen[eng][key] = v
                h.wait_ge(sem, v)
            ins = fn(h)
            if dsem is not None:
                ins.then_inc(dsems[dsem], 16)
            elif (eng, idx) in self.sig:
                ins.then_inc(sems[eng], 1)


def _gammas():
    h = np.arange(RH, dtype=np.float64)
    return 1.0 - np.exp2(-(5.0 + h))


def _ref_decay(chunk):
    import jax
    import jax.numpy as jnp
    with jax.default_device(jax.devices("cpu")[0]):
        h = RH
        log_gamma = jnp.log1p(-jnp.exp2(-(5.0 + jnp.arange(h, dtype=jnp.float32))))
        idx = jnp.arange(chunk, dtype=jnp.float32)
        diff = idx[:, None] - idx[None, :]
        intra = jnp.where(diff >= 0, jnp.exp(log_gamma[:, None, None] * jnp.maximum(diff, 0.0)), 0.0)
        q_decay = jnp.exp(log_gamma[None, :] * (idx[:, None] + 1.0))
        k_decay = jnp.exp(log_gamma[None, :] * (chunk - 1.0 - idx[:, None]))
        chunk_decay = jnp.exp(log_gamma * chunk)
        return (np.asarray(intra, np.float32), np.asarray(q_decay, np.float32),
                np.asarray(k_decay, np.float32), np.asarray(chunk_decay, np.float32))


def _ref_rope():
    import jax
    import jax.numpy as jnp
    with jax.default_device(jax.devices("cpu")[0]):
        half = 128
        inv = 1.0 / (10000.0 ** (jnp.arange(half, dtype=jnp.float32) / half))
        pos = jnp.concatenate([jnp.arange(SEQ), PAST + (jnp.arange(NS) % DS)])
        ang = pos.astype(jnp.float32)[:, None] * inv[None, :]
        return np.asarray(jnp.cos(ang), np.float32).T, np.asarray(jnp.sin(ang), np.float32).T


_CONST_CACHE = {}


def _consts():
    if "c" in _CONST_CACHE:
        return _CONST_CACHE["c"]
    i = np.arange(128)
    intra_p, qdec_p, kdec_p, cd_p = _ref_decay(128)
    intra_s, qdec_s, kdec_s, cd_s = _ref_decay(DS)
    sixteenth = np.float32(1.0 / 16.0)
    mt = np.zeros((128, RH, 128), np.float32)
    mts = np.zeros((128, RH, 128), np.float32)
    same = (i[:, None] // DS) == (i[None, :] // DS)
    difs = (i[None, :] % DS) - (i[:, None] % DS)
    for h in range(RH):
        mt[:, h, :] = intra_p[h].T * sixteenth
        blk = intra_s[h][(i[None, :] % DS), (i[:, None] % DS)]
        mts[:, h, :] = np.where(same, blk, 0.0) * sixteenth
    qd = np.zeros((128, RH, 128), np.float32)
    qds = np.zeros((128, RH, 128), np.float32)
    kd = np.zeros((128, RH), np.float32)
    kds = np.zeros((128, RH), np.float32)
    for h in range(RH):
        qd[:, h, :] = qdec_p[:, h][None, :]
        qds[:, h, :] = qdec_s[i % DS, h][None, :]
        kd[:, h] = kdec_p[:, h] * sixteenth
        kds[:, h] = kdec_s[i % DS, h] * sixteenth
    kzm = ((i[:, None] // DS) == np.arange(NB)[None, :]).astype(np.float32)
    ident = np.eye(128, dtype=np.float32)
    cf32 = np.concatenate([mt.reshape(128, -1), mts.reshape(128, -1), qd.reshape(128, -1), qds.reshape(128, -1),
                           kd, kds, kzm, ident], axis=1).astype(np.float32)
    ones = np.ones((128, 128))
    onesz = np.zeros((128, 2, 128))
    onesz[:, 0, :64] = 1.0
    onesz[:, 1, 64:] = 1.0
    m_cur = (i[:, None] <= i[None, :]).astype(np.float64)
    m_prev = (i[:, None] > i[None, :]).astype(np.float64)
    m_new = (same & (difs >= 0)).astype(np.float64)
    m_cache = np.zeros((128, 4, DS))
    for t in range(DS):
        m_cache[:, :, t] = (i > t)[:, None]
    blockmask = np.zeros((128, NB, 128))
    for b in range(NB):
        blockmask[:, b, b * DS:(b + 1) * DS] = 1.0
    cb = np.concatenate([ident, ones, onesz.reshape(128, -1), m_cur, m_prev, m_new, m_cache.reshape(128, -1),
                         blockmask.reshape(128, -1)], axis=1).astype(np.float32)
    cosT, sinT = _ref_rope()
    rope = np.stack([cosT, sinT]).astype(np.float32)
    _CONST_CACHE["c"] = (cf32, cb, rope, (cd_p, cd_s))
    return _CONST_CACHE["c"]


CF_MT = 0
CF_MTS = 512
CF_QD = 1024
CF_QDS = 1536
CF_KD = 2048
CF_KDS = 2052
CF_KZM = 2056
CF_ID = 2072
CF_N = 2200
CB_ID = 0
CB_ONES = 128
CB_ONESZ = 256
CB_MCUR = 512
CB_MPREV = 640
CB_MNEW = 768
CB_MCACHE = 896
CB_BLK = 928
CB_N = 928 + 2048
SP_G = 0
SP_BQ = 40
SP_BKZ = 48
SP_BK = 52
SP_BV = 53
SP_BO = 54
SP_SINK = 62
SP_N = 70


def build_program():
    nc = bass.Bass("TRN2", target_bir_lowering=False)
    S = Sched()

    def din(name, shape):
        return nc.dram_tensor(name, list(shape), F32, kind="ExternalInput").ap()

    def dout(name, shape):
        return nc.dram_tensor(name, list(shape), F32, kind="ExternalOutput").ap()

    xTp = din("xTp", [D, SEQ])
    xTs = din("xTs", [D, NS])
    st_in = din("st_in", [NB, RH, 256, 512])
    ck = din("ck", [NB, 128, 128])
    cv = din("cv", [NB, 128, 128])
    wqk = din("wqk", [D, RH * 512])
    wv = din("wv", [D, 2048])
    wg = din("wg", [D, 2048])
    wo = din("wo", [2048, D])
    wq1 = din("wq1", [D, 1024])
    wkz = din("wkz", [D, 512])
    wkv = din("wkv", [D, 256])
    wo1 = din("wo1", [D, D])
    w1 = din("w1", [2, D, DFF])
    w3 = din("w3", [2, D, DFF])
    w2 = din("w2", [2, DFF, D])
    spar = din("spar", [128, SP_N])
    cf_d = din("cf32", [128, CF_N])
    cb_d = din("cb", [128, CB_N])
    rope_d = din("rope", [2, 128, SEQ + NS])

    yTp = dout("yTp", [D, SEQ])
    yTs = dout("yTs", [D, NS])
    srp = dout("srp", [RH, 256, 512])
    srs = dout("srs", [NB, RH, 256, 512])
    kp_o = dout("kp", [128, 128])
    vp_o = dout("vp", [128, 128])
    ks_o = dout("ks", [NB, 128, 128])
    vs_o = dout("vs", [NB, 128, 128])
    if DEBUG:
        dbg_o = dout("dbg", [4, D, TT])

    es = contextlib.ExitStack()
    with es:
        def sb(name, shape, dt):
            return es.enter_context(nc.sbuf_tensor(name, list(shape), dt))

        class Ctx:
            pass

        hT_p = sb("hT", [128, KC, TT], BF16)
        hb_p = [Buf(f"h{k}") for k in range(KC)]
        pcx = []
        for i in range(2):
            c_ = Ctx()
            c_.xT = sb(f"xT{i}", [128, KC, TT], F32)
            c_.xb = [Buf(f"x{i}_{k}") for k in range(KC)]
            c_.hT = hT_p
            c_.hb = hb_p
            c_.NT = TT
            pcx.append(c_)
        scx = Ctx()
        scx.xT = sb("xTs_sb", [128, KC, NS], F32)
        scx.xb = [Buf(f"xs{k}") for k in range(KC)]
        scx.hT = sb("hTs_sb", [128, KC, NS], BF16)
        scx.hb = [Buf(f"hs{k}") for k in range(KC)]
        scx.NT = NS
        sq = sb("sq", [128, 2, TT], BF16)
        sqb = [Buf("sq0"), Buf("sq1")]
        rtmp = sb("rtmp", [128, TT], F32)
        rtmpb = Buf("rtmp")
        rstd = sb("rstd", [128, TT], F32)
        rstdb = Buf("rstd")
        wring = [sb(f"wr{i}", [128, WSLOT], BF16) for i in range(NW)]
        wrb = [Buf(f"wr{i}") for i in range(NW)]
        rope = [sb(f"rope{i}", [128, 2, TT], F32) for i in range(2)]
        ropeb = [Buf("rope0"), Buf("rope1")]
        cf = sb("cf", [128, CF_N], F32)
        cfb = Buf("cf", const=True)
        cb = sb("cbt", [128, CB_N], BF16)
        cbb = Buf("cb", const=True)
        sp = sb("sp", [128, SP_N], F32)
        spb = Buf("sp", const=True)
        esink = sb("esink", [128, 8], F32)
        esinkb = Buf("esink", const=True)
        Sst = sb("Sst", [128, RH, 2, 512], F32)
        Sbf = sb("Sbf", [128, RH, 2, 512], BF16)
        Sb = [[Buf(f"S{h}{a}") for a in range(2)] for h in range(RH)]
        Sbfb = [[Buf(f"Sbf{h}{a}") for a in range(2)] for h in range(RH)]
        ARENA_COLS = 34112
        arena = sb("arena", [128, ARENA_COLS], BF16)
        stat = sb("stat", [128, 4, 16], F32)
        statb = [Buf(f"stat{i}") for i in range(4)]
        kz = sb("kz", [128, 4, 128 + TT], BF16)
        kzb = [Buf(f"kz{i}") for i in range(5)]
        vz = sb("vz", [128, 5, 4, 128], BF16)
        vzb = [Buf(f"vz{i}") for i in range(5)]

        banks = [es.enter_context(nc.psum_tensor(f"pb{i}", [128, 512], F32)) for i in range(8)]
        bankb = [Buf(f"bank{i}", excl=True) for i in range(8)]
        held = [False] * 8
        rr = [0]

        def pb(hold=False):
            for _ in range(16):
                i = rr[0] % 8
                rr[0] += 1
                if not held[i]:
                    if hold:
                        held[i] = True
                    return banks[i], bankb[i], i
            raise RuntimeError("no psum bank")

        def release(i):
            held[i] = False

        class Arena:
            def __init__(self):
                self.off = 0
                self.cur = []
                self.pending = {}

            def reset(self):
                for b in self.cur:
                    if b.w is not None:
                        k, v = b.w
                        if self.pending.get(k, -1) < v:
                            self.pending[k] = v
                    for k, v in b.rs.items():
                        if self.pending.get(k, -1) < v:
                            self.pending[k] = v
                self.cur = []
                self.off = 0

            def buf(self, name):
                b = Buf(name, pending=self.pending)
                self.cur.append(b)
                return b

            def alloc(self, name, shape, dt):
                n = 1
                for s in shape[1:]:
                    n *= s
                nb = n * (2 if dt == F32 else 1)
                nb = (nb + 15) // 16 * 16
                assert self.off + nb <= ARENA_COLS, (name, self.off, nb)
                v = arena[:, self.off:self.off + nb]
                self.off += nb
                if dt == F32:
                    v = v.bitcast(F32)[:, :n]
                else:
                    v = v[:, :n]
                if len(shape) == 3:
                    v = v.rearrange("p (a b) -> p a b", a=shape[1])
                elif len(shape) == 4:
                    v = v.rearrange("p (a b c) -> p a b c", a=shape[1], b=shape[2])
                return v

        A = Arena()

        def mm(out, lhsT, rhs, start, stop, r, w):
            S.op("pe", lambda e: e.matmul(out, lhsT, rhs, start=start, stop=stop), r, w)

        def tr(out, in_, ident, r, w):
            S.op("pe", lambda e: e.transpose(out, in_, ident), r, w)

        def act(out, in_, func, r, w, bias=0.0, scale=1.0):
            S.op("act", lambda e: e.activation(out=out, in_=in_, func=func, bias=bias, scale=scale), r, w)

        def tt(out, in0, in1, op, r, w, eng="dve"):
            S.op(eng, lambda e: e.tensor_tensor(out=out, in0=in0, in1=in1, op=op), r, w)

        def stt(out, in0, scalar, in1, op0, op1, r, w):
            S.op("dve", lambda e: e.scalar_tensor_tensor(out=out, in0=in0, scalar=scalar, in1=in1, op0=op0, op1=op1), r, w)

        def ts(out, in0, s1, s2, op0, op1, r, w):
            S.op("dve", lambda e: e.tensor_scalar(out=out, in0=in0, scalar1=s1, scalar2=s2, op0=op0, op1=op1), r, w)

        def recip(out, in_, r, w):
            S.op("dve", lambda e: e.reciprocal(out=out, in_=in_), r, w)

        def dma(eng, out, in_, r, w, sem):
            S.op(eng, lambda e: e.dma_start(out=out, in_=in_), r, w, dsem=sem)

        def memset(eng, ap, val, w):
            S.op(eng, lambda e: e.memset(ap, val), (), w)

        def wview(ap2d, ncols):
            return ap2d.rearrange("(kc p) n -> p kc n", p=128), ncols

        units = {}
        wqk_v = wqk.rearrange("(kc p) n -> p kc n", p=128)
        wv_v = wv.rearrange("(kc p) n -> p kc n", p=128)
        wg_v = wg.rearrange("(kc p) n -> p kc n", p=128)
        wo_v = wo.rearrange("(kc p) n -> p kc n", p=128)
        wq1_v = wq1.rearrange("(kc p) n -> p kc n", p=128)
        wkz_v = wkz.rearrange("(kc p) n -> p kc n", p=128)
        wkv_v = wkv.rearrange("(kc p) n -> p kc n", p=128)
        wo1_v = wo1.rearrange("(kc p) n -> p kc n", p=128)
        for h in range(RH):
            units[("qk", h)] = (wqk_v[:, :, h * 512:(h + 1) * 512], (8, 512))
            units[("v", h)] = (wv_v[:, :, h * 512:(h + 1) * 512], (8, 512))
            units[("g", h)] = (wg_v[:, :, h * 512:(h + 1) * 512], (8, 512))
        for half in range(2):
            for ecg in range(2):
                units[("wo", half, ecg)] = (wo_v[:, ecg * 8:(ecg + 1) * 8, half * 512:(half + 1) * 512], (8, 512))
            units[("q1", half)] = (wq1_v[:, :, half * 512:(half + 1) * 512], (8, 512))
            units[("wo1", half)] = (wo1_v[:, :, half * 512:(half + 1) * 512], (8, 512))
        units[("kz",)] = (wkz_v, (8, 512))
        units[("kv",)] = (wkv_v, (8, 256))
        FG = [(0, 4), (4, 4), (8, 4), (12, 4), (16, 4), (20, 2)]
        UG = [(0, 8), (8, 8), (16, 6)]
        for l in range(2):
            w1_v = w1[l].rearrange("(kc p) n -> p kc n", p=128)
            w3_v = w3[l].rearrange("(kc p) n -> p kc n", p=128)
            w2_v = w2[l].rearrange("(f p) n -> p f n", p=128)
            for gi, (f0, nf) in enumerate(FG):
                units[("w1", l, gi)] = (w1_v[:, :, f0 * 128:(f0 + nf) * 128], (8, nf * 128))
                units[("w3", l, gi)] = (w3_v[:, :, f0 * 128:(f0 + nf) * 128], (8, nf * 128))
            for half in range(2):
                for ui, (f0, nf) in enumerate(UG):
                    units[("w2", l, half, ui)] = (w2_v[:, f0:f0 + nf, half * 512:(half + 1) * 512], (nf, 512))

        plan = []
        for t in range(NPT):
            nrep = 2 if t == NPT - 1 else 1
            for rep in range(nrep):
                if rep == 0:
                    for h in range(RH):
                        plan += [("qk", h), ("v", h), ("g", h)]
                plan += [("wo", 0, 0), ("wo", 0, 1), ("wo", 1, 0), ("wo", 1, 1)]
            for l in range(2):
                if l == 1:
                    for _ in range(nrep):
                        plan += [("q1", 0), ("q1", 1), ("kz",), ("kv",), ("wo1", 0), ("wo1", 1)]
                for gi in range(len(FG)):
                    plan += [("w1", l, gi), ("w3", l, gi)]
                for half in range(2):
                    for ui in range(len(UG)):
                        plan.append(("w2", l, half, ui))
        wstate = {"k": 0, "loaded": 0}

        def slot_view(slot, shp):
            a, b = shp
            return wring[slot][:, :a * b].rearrange("p (a b) -> p a b", a=a)

        def wget(key):
            u = wstate["k"]
            assert plan[u] == key, (u, plan[u], key)
            wstate["k"] += 1
            while wstate["loaded"] < min(len(plan), u + NW - 1):
                j = wstate["loaded"]
                src, shp = units[plan[j]]
                sl = j % NW
                dma("pool", slot_view(sl, shp), src, (), [wrb[sl]], f"w{sl}")
                wstate["loaded"] += 1
            src, shp = units[key]
            return slot_view(u % NW, shp), wrb[u % NW]

        dma("sp", cf[:, :], cf_d[:, :], (), [cfb], "c0")
        dma("sp", sp[:, :], spar[:, :], (), [spb], "c0")
        dma("pool", cb[:, :], cb_d[:, :], (), [cbb], "c1")
        act(esink[:, :], sp[:, SP_SINK:SP_SINK + 8], AF.Exp, [spb], [esinkb])
        for h in range(RH):
            for a in range(2):
                memset("dve", Sst[:, h, a, :], 0.0, [Sb[h][a]])
                memset("dve", Sbf[:, h, a, :], 0.0, [Sbfb[h][a]])
        for i in range(5):
            memset("dve", vz[:, i, :, :], 0.0, [vzb[i]])

        ident_bf = cb[:, CB_ID:CB_ID + 128]
        ones_bf = cb[:, CB_ONES:CB_ONES + 128]
        ident_f = cf[:, CF_ID:CF_ID + 128]

        def gain(gi, kc):
            return sp[:, SP_G + gi * 8 + kc:SP_G + gi * 8 + kc + 1]

        def rmsnorm(cx, gi, to_x=False):
            NT = cx.NT
            bk, bb, _ = pb()
            for kc in range(KC):
                s = kc % 2
                act(sq[:, s, :NT], cx.xT[:, kc, :NT], AF.Square, [cx.xb[kc]], [sqb[s]])
                mm(bk[:, :NT], ones_bf, sq[:, s, :NT], kc == 0, kc == KC - 1, [sqb[s], cbb], [bb])
            act(rtmp[:, :NT], bk[:, :NT], AF.Sqrt, [bb], [rtmpb], bias=EPS, scale=1.0 / D)
            recip(rstd[:, :NT], rtmp[:, :NT], [rtmpb], [rstdb])
            for kc in range(KC):
                if to_x:
                    stt(cx.xT[:, kc, :NT], cx.xT[:, kc, :NT], gain(gi, kc), rstd[:, :NT], ALU.mult, ALU.mult,
                        [cx.xb[kc], rstdb, spb], [cx.xb[kc]])
                else:
                    stt(cx.hT[:, kc, :NT], cx.xT[:, kc, :NT], gain(gi, kc), rstd[:, :NT], ALU.mult, ALU.mult,
                        [cx.xb[kc], rstdb, spb], [cx.hb[kc]])

        def ffn(cxs, l):
            A.reset()
            aTs, abs_, s1s, s1bs = [], [], [], []
            for ci, cx in enumerate(cxs):
                aTs.append(A.alloc(f"aT{ci}", [128, NF, cx.NT], BF16))
                abs_.append([A.buf(f"a{f}") for f in range(NF)])
                s1s.append(A.alloc(f"s1{ci}", [128, 2, cx.NT], F32))
                s1bs.append([A.buf("s1a"), A.buf("s1b")])
            for gi, (f0, nf) in enumerate(FG):
                w1u, w1b_ = wget(("w1", l, gi))
                w3u, w3b_ = wget(("w3", l, gi))
                for fi in range(nf):
                    f = f0 + fi
                    for ci, cx in enumerate(cxs):
                        NT = cx.NT
                        aT, ab, s1, s1b = aTs[ci], abs_[ci], s1s[ci], s1bs[ci]
                        b1, bb1, _ = pb()
                        b3, bb3, _ = pb()
                        for kc in range(KC):
                            mm(b1[:, :NT], w1u[:, kc, fi * 128:(fi + 1) * 128], cx.hT[:, kc, :NT], kc == 0, kc == KC - 1,
                               [w1b_, cx.hb[kc]], [bb1])
                        for kc in range(KC):
                            mm(b3[:, :NT], w3u[:, kc, fi * 128:(fi + 1) * 128], cx.hT[:, kc, :NT], kc == 0, kc == KC - 1,
                               [w3b_, cx.hb[kc]], [bb3])
                        act(s1[:, f % 2, :NT], b1[:, :NT], AF.Silu, [bb1], [s1b[f % 2]])
                        tt(aT[:, f, :NT], s1[:, f % 2, :NT], b3[:, :NT], ALU.mult, [s1b[f % 2], bb3], [ab[f]])
            for half in range(2):
                bss = [[pb(hold=True) for _ in range(4)] for _ in cxs]
                for ui, (f0, nf) in enumerate(UG):
                    w2u, w2b_ = wget(("w2", l, half, ui))
                    for dm in range(4):
                        for fi in range(nf):
                            f = f0 + fi
                            for ci, cx in enumerate(cxs):
                                mm(bss[ci][dm][0][:, :cx.NT], w2u[:, fi, dm * 128:(dm + 1) * 128], aTs[ci][:, f, :cx.NT],
                                   f == 0, f == NF - 1, [w2b_, abs_[ci][f]], [bss[ci][dm][1]])
                for ci, cx in enumerate(cxs):
                    NT = cx.NT
                    for dm in range(4):
                        kc = half * 4 + dm
                        tt(cx.xT[:, kc, :NT], bss[ci][dm][0][:, :NT], cx.xT[:, kc, :NT], ALU.add, [bss[ci][dm][1], cx.xb[kc]], [cx.xb[kc]])
                        release(bss[ci][dm][2])

        def gn_gate(ob, obb, g_ap, gbuf, u_ap, ubuf, on_ap, onbuf, si):
            st = stat[:, si, :]
            sbf = statb[si]
            S.op("dve", lambda e: e.bn_stats(out=st[:, 0:6], in_=ob), [obb], [sbf])
            S.op("dve", lambda e: e.bn_aggr(out=st[:, 6:8], in_=st[:, 0:6]), [sbf], [sbf])
            act(st[:, 8:9], st[:, 7:8], AF.Sqrt, [sbf], [sbf], bias=EPS, scale=1.0)
            recip(st[:, 9:10], st[:, 8:9], [sbf], [sbf])
            stt(st[:, 10:11], st[:, 6:7], -1.0, st[:, 9:10], ALU.mult, ALU.mult, [sbf], [sbf])
            act(on_ap, ob, AF.Identity, [obb, sbf], [onbuf], bias=st[:, 10:11], scale=st[:, 9:10])
            tt(u_ap, on_ap, g_ap, ALU.mult, [onbuf, gbuf], [ubuf])

        def layer0(cx, tile_i, sample, rope_i, rider=None, pre=None):
            NT = cx.NT
            nch = NT // 128
            A.reset()
            if pre is None:
                qT = [A.alloc(f"qT{i}", [128, 2, NT], BF16) for i in range(2)]
                qdT = [A.alloc(f"qdT{i}", [128, 2, NT], BF16) for i in range(2)]
                kT = [A.alloc(f"kT{i}", [128, 2, NT], BF16) for i in range(2)]
                qTb = [[A.buf("qT") for _ in range(2)] for _ in range(2)]
                qdTb = [[A.buf("qdT") for _ in range(2)] for _ in range(2)]
                kTb = [[A.buf("kT") for _ in range(2)] for _ in range(2)]
                kd = [A.alloc(f"kd{i}", [128, nch, 256], BF16) for i in range(2)]
                kdb = [[A.buf("kd") for _ in range(nch)] for _ in range(2)]
                vv = [A.alloc(f"v{i}", [128, nch, 512], BF16) for i in range(2)]
                vb = [[A.buf("v") for _ in range(nch)] for _ in range(2)]
                gg = [A.alloc(f"g{i}", [128, nch, 512], BF16) for i in range(2)]
                gb = [[A.buf("g") for _ in range(nch)] for _ in range(2)]
            else:
                qT, qdT, kT, kd, vv, gg = pre["qT"], pre["qdT"], pre["kT"], pre["kd"], pre["v"], pre["g"]
                qTb, qdTb, kTb, kdb, vb, gb = pre["qTb"], pre["qdTb"], pre["kTb"], pre["kdb"], pre["vb"], pre["gb"]
            npar = len(qT)
            R = None
            if rider is not None:
                rcx, r_rope, store, store_bufs = rider
                pend = {}
                for b_ in store_bufs:
                    if b_.w is not None and pend.get(b_.w[0], -1) < b_.w[1]:
                        pend[b_.w[0]] = b_.w[1]
                    for k_, v_ in b_.rs.items():
                        if pend.get(k_, -1) < v_:
                            pend[k_] = v_
                flat = store.rearrange("p a b -> p (a b)").bitcast(BF16)
                R = {k_: [] for k_ in ("qT", "qdT", "kT", "kd", "v", "g", "qTb", "qdTb", "kTb", "kdb", "vb", "gb")}
                for h_ in range(RH):
                    o_ = h_ * 2048
                    R["qT"].append(flat[:, o_:o_ + 256].rearrange("p (a t) -> p a t", a=2))
                    R["qdT"].append(flat[:, o_ + 256:o_ + 512].rearrange("p (a t) -> p a t", a=2))
                    R["kT"].append(flat[:, o_ + 512:o_ + 768].rearrange("p (a t) -> p a t", a=2))
                    R["kd"].append(flat[:, o_ + 768:o_ + 1024].rearrange("p (c d) -> p c d", c=1))
                    R["v"].append(flat[:, o_ + 1024:o_ + 1536].rearrange("p (c d) -> p c d", c=1))
                    R["g"].append(flat[:, o_ + 1536:o_ + 2048].rearrange("p (c d) -> p c d", c=1))
                    R["qTb"].append([Buf("rqT", pending=pend) for _ in range(2)])
                    R["qdTb"].append([Buf("rqdT", pending=pend) for _ in range(2)])
                    R["kTb"].append([Buf("rkT", pending=pend) for _ in range(2)])
                    R["kdb"].append([Buf("rkd", pending=pend)])
                    R["vb"].append([Buf("rv", pending=pend)])
                    R["gb"].append([Buf("rg", pending=pend)])
                r_cos = rope[r_rope][:, 0, :NS]
                r_sin = rope[r_rope][:, 1, :NS]
                r_rpb = ropeb[r_rope]
            if pre is None:
                tmp = A.alloc("rt", [128, 4, NT], F32)
                tmpb = [A.buf(f"rt{i}") for i in range(4)]
                tq = A.alloc("tq", [128, 2, NT], F32)
                tqb = [A.buf("tq0"), A.buf("tq1")]
            uT = A.alloc("uT", [128, 16, NT], BF16)
            uTb = [[A.buf("uT") for _ in range(nch)] for _ in range(RH)]
            on = A.alloc("on", [128, 2, 512], F32)
            onb = [A.buf("on0"), A.buf("on1")]
            uu = A.alloc("u", [128, 2, 512], BF16)
            ub = [A.buf("u0"), A.buf("u1")]
            sTm = A.alloc("sTm", [128, 2, 128], BF16)
            sTmb = [A.buf("sTm0"), A.buf("sTm1")]
            if sample:
                NSR = 4
                S0 = [A.alloc(f"S0{i}", [128, 2, 512], F32) for i in range(NSR)]
                S0b = [A.buf("S0") for i in range(NSR)]
                S0bf = [A.alloc(f"S0bf{i}", [128, 2, 512], BF16) for i in range(3)]
                S0bfb = [A.buf("S0bf") for i in range(3)]
                Sn = [A.alloc(f"Sn{i}", [128, 2, 512], F32) for i in range(NSR)]
                Snb = [A.buf("Sn") for i in range(NSR)]
                Qz = A.alloc("Qz", [128, 2, NB, 128], BF16)
                Qzb = A.buf("Qz")
                KZ = A.alloc("KZ", [128, 2, NB, 128], BF16)
                KZb = A.buf("KZ")
            rp = rope[rope_i]
            rpb = ropeb[rope_i]
            cosv = rp[:, 0, :NT]
            sinv = rp[:, 1, :NT]
            mt_off = CF_MTS if sample else CF_MT
            qd_off = CF_QDS if sample else CF_QD
            kd_off = CF_KDS if sample else CF_KD
            gam = _gammas()
            sidx = [0]
            s0_issued = [0]

            def issue_s0(u):
                if u >= RH * NB or u < s0_issued[0]:
                    return
                assert u == s0_issued[0]
                s0_issued[0] += 1
                dma("sp", S0[u % 4][:, :, :], st_in[u % NB, u // NB].rearrange("(a p) e -> p a e", p=128), (), [S0b[u % 4]], f"s0{u % 4}")
            wun = {}

            def p_qk(hd, which):
                def f():
                    p = hd % npar
                    if which == 0:
                        wun[hd] = wget(("qk", hd))
                    wu, wub = wun[hd]
                    base = which * 256
                    b1, bb1, _ = pb()
                    b2, bb2, _ = pb()
                    for kc in range(KC):
                        mm(b1[:, :NT], wu[:, kc, base:base + 128], cx.hT[:, kc, :NT], kc == 0, kc == KC - 1, [wub, cx.hb[kc]], [bb1])
                    for kc in range(KC):
                        mm(b2[:, :NT], wu[:, kc, base + 128:base + 256], cx.hT[:, kc, :NT], kc == 0, kc == KC - 1, [wub, cx.hb[kc]], [bb2])
                    tt(tmp[:, 0, :], b1[:, :NT], cosv, ALU.mult, [bb1, rpb], [tmpb[0]])
                    tt(tmp[:, 1, :], b2[:, :NT], sinv, ALU.mult, [bb2, rpb], [tmpb[1]])
                    tt(tmp[:, 2, :], b1[:, :NT], sinv, ALU.mult, [bb1, rpb], [tmpb[2]])
                    tt(tmp[:, 3, :], b2[:, :NT], cosv, ALU.mult, [bb2, rpb], [tmpb[3]])
                    if which == 0:
                        tt(tq[:, 0, :], tmp[:, 0, :], tmp[:, 1, :], ALU.subtract, [tmpb[0], tmpb[1]], [tqb[0]])
                        tt(tq[:, 1, :], tmp[:, 2, :], tmp[:, 3, :], ALU.add, [tmpb[2], tmpb[3]], [tqb[1]])
                        for a in range(2):
                            act(qT[p][:, a, :], tq[:, a, :], AF.Copy, [tqb[a]], [qTb[p][a]])
                            qdv = cf[:, qd_off + hd * 128:qd_off + (hd + 1) * 128].unsqueeze(1).broadcast_to([128, nch, 128])
                            tt(qdT[p][:, a, :].rearrange("p (c i) -> p c i", c=nch),
                               tq[:, a, :].rearrange("p (c i) -> p c i", c=nch), qdv, ALU.mult,
                               [tqb[a], cfb], [qdTb[p][a]])
                    else:
                        tt(kT[p][:, 0, :], tmp[:, 0, :], tmp[:, 1, :], ALU.subtract, [tmpb[0], tmpb[1]], [kTb[p][0]])
                        tt(kT[p][:, 1, :], tmp[:, 2, :], tmp[:, 3, :], ALU.add, [tmpb[2], tmpb[3]], [kTb[p][1]])
                    if R is not None:
                        b1, bb1, _ = pb()
                        b2, bb2, _ = pb()
                        for kc in range(KC):
                            mm(b1[:, :NS], wu[:, kc, base:base + 128], rcx.hT[:, kc, :NS], kc == 0, kc == KC - 1, [wub, rcx.hb[kc]], [bb1])
                        for kc in range(KC):
                            mm(b2[:, :NS], wu[:, kc, base + 128:base + 256], rcx.hT[:, kc, :NS], kc == 0, kc == KC - 1, [wub, rcx.hb[kc]], [bb2])
                        tt(tmp[:, 0, :NS], b1[:, :NS], r_cos, ALU.mult, [bb1, r_rpb], [tmpb[0]])
                        tt(tmp[:, 1, :NS], b2[:, :NS], r_sin, ALU.mult, [bb2, r_rpb], [tmpb[1]])
                        tt(tmp[:, 2, :NS], b1[:, :NS], r_sin, ALU.mult, [bb1, r_rpb], [tmpb[2]])
                        tt(tmp[:, 3, :NS], b2[:, :NS], r_cos, ALU.mult, [bb2, r_rpb], [tmpb[3]])
                        if which == 0:
                            tt(tq[:, 0, :NS], tmp[:, 0, :NS], tmp[:, 1, :NS], ALU.subtract, [tmpb[0], tmpb[1]], [tqb[0]])
                            tt(tq[:, 1, :NS], tmp[:, 2, :NS], tmp[:, 3, :NS], ALU.add, [tmpb[2], tmpb[3]], [tqb[1]])
                            for a in range(2):
                                act(R["qT"][hd][:, a, :], tq[:, a, :NS], AF.Copy, [tqb[a]], [R["qTb"][hd][a]])
                                tt(R["qdT"][hd][:, a, :], tq[:, a, :NS], cf[:, CF_QDS + hd * 128:CF_QDS + (hd + 1) * 128], ALU.mult,
                                   [tqb[a], cfb], [R["qdTb"][hd][a]])
                        else:
                            tt(R["kT"][hd][:, 0, :], tmp[:, 0, :NS], tmp[:, 1, :NS], ALU.subtract, [tmpb[0], tmpb[1]], [R["kTb"][hd][0]])
                            tt(R["kT"][hd][:, 1, :], tmp[:, 2, :NS], tmp[:, 3, :NS], ALU.add, [tmpb[2], tmpb[3]], [R["kTb"][hd][1]])
                return f

            def p_vg(hd, kind):
                def f():
                    p = hd % npar
                    wu_, wb_ = wget((kind, hd))
                    for c in range(nch):
                        bk, bb, _ = pb()
                        for kc in range(KC):
                            mm(bk[:, :], cx.hT[:, kc, c * 128:(c + 1) * 128], wu_[:, kc, :], kc == 0, kc == KC - 1, [wb_, cx.hb[kc]], [bb])
                        if kind == "v":
                            act(vv[p][:, c, :], bk[:, :], AF.Copy, [bb], [vb[p][c]])
                        else:
                            act(gg[p][:, c, :], bk[:, :], AF.Silu, [bb], [gb[p][c]])
                    if R is not None:
                        bk, bb, _ = pb()
                        for kc in range(KC):
                            mm(bk[:, :], rcx.hT[:, kc, 0:NS], wu_[:, kc, :], kc == 0, kc == KC - 1, [wb_, rcx.hb[kc]], [bb])
                        if kind == "v":
                            act(R["v"][hd][:, 0, :], bk[:, :], AF.Copy, [bb], [R["vb"][hd][0]])
                        else:
                            act(R["g"][hd][:, 0, :], bk[:, :], AF.Silu, [bb], [R["gb"][hd][0]])
                return f

            def p_kd(hd):
                def f():
                    p = hd % npar
                    for c in range(nch):
                        tb, tbb, _ = pb()
                        tbv = tb[:, :].bitcast(BF16)
                        for a in range(2):
                            tr(tbv[:, a * 128:(a + 1) * 128], kT[p][:, a, c * 128:(c + 1) * 128], ident_bf, [kTb[p][a], cbb], [tbb])
                        act(kd[p][:, c, :], tbv[:, 0:256], AF.Copy, [tbb, cfb], [kdb[p][c]],
                            scale=cf[:, kd_off + hd:kd_off + hd + 1])
                    if R is not None:
                        tb, tbb, _ = pb()
                        tbv = tb[:, :].bitcast(BF16)
                        for a in range(2):
                            tr(tbv[:, a * 128:(a + 1) * 128], R["kT"][hd][:, a, :], ident_bf, [R["kTb"][hd][a], cbb], [tbb])
                        act(R["kd"][hd][:, 0, :], tbv[:, 0:256], AF.Copy, [tbb, cfb], [R["kdb"][hd][0]],
                            scale=cf[:, CF_KDS + hd:CF_KDS + hd + 1])
                return f

            def proj_pieces(hd):
                if pre is not None:
                    return []
                return [p_qk(hd, 0), p_qk(hd, 1), p_vg(hd, "v"), p_vg(hd, "g"), p_kd(hd)]

            cst = {}

            def c_main(hd, c):
                p = hd % npar
                cd = float(_consts()[3][1 if sample else 0][hd])
                cs = slice(c * 128, (c + 1) * 128)
                if sample:
                    for a in range(2):
                        tt(Qz[:, a, :, :], qdT[p][:, a, :].unsqueeze(1).broadcast_to([128, NB, 128]),
                           cb[:, CB_BLK:CB_BLK + NB * 128].rearrange("p (b i) -> p b i", b=NB), ALU.mult,
                           [qdTb[p][a], cbb], [Qzb])
                        tt(KZ[:, a, :, :], kd[p][:, 0, a * 128:(a + 1) * 128].unsqueeze(1).broadcast_to([128, NB, 128]),
                           cf[:, CF_KZM:CF_KZM + NB].unsqueeze(2).broadcast_to([128, NB, 128]), ALU.mult,
                           [kdb[p][0], cfb], [KZb])
                sbk, sbb, _ = pb()
                for a in range(2):
                    mm(sbk[:, :128], kT[p][:, a, cs], qT[p][:, a, cs], a == 0, a == 1, [kTb[p][a], qTb[p][a]], [sbb])
                si = sidx[0] % 2
                sidx[0] += 1
                tt(sTm[:, si, :], sbk[:, :128], cf[:, mt_off + hd * 128:mt_off + (hd + 1) * 128], ALU.mult,
                   [sbb, cfb], [sTmb[si]])
                if not sample:
                    pbs = []
                    for a in range(2):
                        pk, pkb, _ = pb()
                        mm(pk[:, :], kd[p][:, c, a * 128:(a + 1) * 128], vv[p][:, c, :], True, True, [kdb[p][c], vb[p][c]], [pkb])
                        pbs.append((pk, pkb))
                    ob, obb, _ = pb()
                    mm(ob[:, :], sTm[:, si, :], vv[p][:, c, :], True, False, [sTmb[si], vb[p][c]], [obb])
                    for a in range(2):
                        mm(ob[:, :], qdT[p][:, a, cs], Sbf[:, hd, a, :], False, a == 1, [qdTb[p][a], Sbfb[hd][a]], [obb])
                    for a in range(2):
                        stt(Sst[:, hd, a, :], Sst[:, hd, a, :], cd, pbs[a][0][:, :], ALU.mult, ALU.add,
                            [Sb[hd][a], pbs[a][1]], [Sb[hd][a]])
                        act(Sbf[:, hd, a, :], Sst[:, hd, a, :], AF.Copy, [Sb[hd][a]], [Sbfb[hd][a]])
                else:
                    ob, obb, obi = pb(hold=True)
                    mm(ob[:, :], sTm[:, si, :], vv[p][:, c, :], True, False, [sTmb[si], vb[p][c]], [obb])
                    for b in range(NB):
                        u = hd * NB + b
                        ui = u % 4
                        u2 = u % 3
                        issue_s0(u)
                        issue_s0(u + 1)
                        issue_s0(u + 2)
                        issue_s0(u + 3)
                        for a in range(2):
                            act(S0bf[u2][:, a, :], S0[ui][:, a, :], AF.Copy, [S0b[ui]], [S0bfb[u2]])
                        for a in range(2):
                            mm(ob[:, :], Qz[:, a, b, :], S0bf[u2][:, a, :], False, (b == NB - 1 and a == 1), [Qzb, S0bfb[u2]], [obb])
                        for a in range(2):
                            pk, pkb, _ = pb()
                            mm(pk[:, :], KZ[:, a, b, :], vv[p][:, c, :], True, True, [KZb, vb[p][c]], [pkb])
                            stt(Sn[ui][:, a, :], S0[ui][:, a, :], cd, pk[:, :], ALU.mult, ALU.add, [S0b[ui], pkb], [Snb[ui]])
                        dma("pool", srs[b, hd].rearrange("(a p) e -> p a e", p=128), Sn[ui][:, :, :], [Snb[ui]], [], f"sn{ui}")
                    release(obi)
                gs = gidx[0] % 2
                gidx[0] += 1
                gn_gate(ob[:, :], obb, gg[p][:, c, :], gb[p][c], uu[:, gs, :], ub[gs], on[:, gs, :], onb[gs], (hd * nch + c) % 4)
                cst[(hd, c)] = gs

            def c_tail(hd, c):
                gs = cst[(hd, c)]
                cs = slice(c * 128, (c + 1) * 128)
                tb, tbb, _ = pb()
                tbv = tb[:, :].bitcast(BF16)
                for ec in range(4):
                    tr(tbv[:, ec * 128:(ec + 1) * 128], uu[:, gs, ec * 128:(ec + 1) * 128], ident_bf, [ub[gs], cbb], [tbb])
                act(uT[:, hd * 4:(hd + 1) * 4, cs], tbv[:, 0:512].rearrange("p (a b) -> p a b", a=4), AF.Copy, [tbb], [uTb[hd][c]])

            gidx = [0]
            for pc in proj_pieces(0):
                pc()
            for hd in range(RH):
                q_ = proj_pieces(hd + 1) if hd + 1 < RH else []
                for c in range(nch):
                    c_main(hd, c)
                    if q_:
                        q_.pop(0)()
                    if c > 0:
                        c_tail(hd, c - 1)
                while len(q_) > 1:
                    q_.pop(0)()
                c_tail(hd, nch - 1)
                while q_:
                    q_.pop(0)()
            for half in range(2):
                bs = [pb(hold=True) for _ in range(4)]
                for ecg in range(2):
                    wou, wob = wget(("wo", half, ecg))
                    for dm in range(4):
                        for ec in range(8):
                            e_ = ecg * 8 + ec
                            mm(bs[dm][0][:, :NT], wou[:, ec, dm * 128:(dm + 1) * 128], uT[:, e_, :NT], e_ == 0, e_ == 15,
                               [wob] + uTb[e_ // 4], [bs[dm][1]])
                for dm in range(4):
                    kc = half * 4 + dm
                    tt(cx.xT[:, kc, :NT], bs[dm][0][:, :NT], cx.xT[:, kc, :NT], ALU.add, [bs[dm][1], cx.xb[kc]], [cx.xb[kc]])
                    release(bs[dm][2])
            if (not sample) and tile_i == NPT - 1:
                dma("sp", srp.rearrange("h (a p) e -> p h a e", p=128), Sst[:, :, :, :],
                    [Sb[h][a] for h in range(RH) for a in range(2)], [], "srp")
            return R

        def layer1(cx, tile_i, sample):
            NT = cx.NT
            nblk = NT // 128
            A.reset()
            qT1 = A.alloc("qT1", [128, 8, NT], BF16)
            qT1b = [A.buf("qT1") for _ in range(8)]
            kvT = A.alloc("kvT", [128, 2, NT], F32)
            kvTb = [A.buf("kT1"), A.buf("vT1")]
            oT = A.alloc("oT", [128, 8, NT], BF16)
            oTb2 = [[A.buf("oT") for _ in range(nblk)] for _ in range(8)]
            ee = A.alloc("ee", [128, 8, 128], BF16)
            eeb = [A.buf("ee") for _ in range(8)]
            if sample:
                pT5 = A.alloc("pT", [128, 4, NB, 32], BF16)
                qS = A.alloc("qS", [128, 2, NB, 32], BF16)
                qSb = A.buf("qS")
            else:
                pT = A.alloc("pT", [128, 12, 128], BF16)
            pTb = [A.buf("pT") for _ in range(16 if sample else 12)]
            rec = A.alloc("rec", [128, 2, 512], F32)
            recb = [A.buf("rec0"), A.buf("rec1")]
            tok = A.alloc("tok", [128, 2, 128], F32)
            tokb = [A.buf("tok0"), A.buf("tok1")]
            if sample:
                Kc = [A.alloc(f"Kc{i}", [128, 2, 128], F32) for i in range(2)]
                Kcb = [A.buf("Kc0"), A.buf("Kc1")]
                Kcz = [A.alloc(f"Kcz{i}", [128, 4, 128], BF16) for i in range(2)]
                Kczb = [A.buf("Kcz0"), A.buf("Kcz1")]
                Vcz = [A.alloc(f"Vcz{i}", [128, 4, 128], BF16) for i in range(2)]
                Vczb = [A.buf("Vcz0"), A.buf("Vcz1")]
                kzc = [A.alloc(f"kzc{i}", [128, 4, 128], BF16) for i in range(2)]
                kzcb = [A.buf("kzc0"), A.buf("kzc1")]
                e32 = A.alloc("e32", [128, 4, 32], BF16)
                e32b = [A.buf("e32") for _ in range(4)]
                pTc = [A.alloc(f"pTc{i}", [128, 4, 32], BF16) for i in range(2)]
                pTcb = [A.buf("pTc0"), A.buf("pTc1")]
            for qu in range(2):
                wu, wub = wget(("q1", qu))
                for j in range(4):
                    hp = qu * 4 + j
                    bk, bb, _ = pb()
                    for kc in range(KC):
                        mm(bk[:, :NT], wu[:, kc, j * 128:(j + 1) * 128], cx.hT[:, kc, :NT], kc == 0, kc == KC - 1, [wub, cx.hb[kc]], [bb])
                    act(qT1[:, hp, :], bk[:, :NT], AF.Identity, [bb, spb], [qT1b[hp]], bias=sp[:, SP_BQ + hp:SP_BQ + hp + 1])
            wu, wub = wget(("kz",))
            for var in range(4):
                bk, bb, _ = pb()
                for kc in range(KC):
                    mm(bk[:, :NT], wu[:, kc, var * 128:(var + 1) * 128], cx.hT[:, kc, :NT], kc == 0, kc == KC - 1, [wub, cx.hb[kc]], [bb])
                act(kz[:, var, 128:128 + NT], bk[:, :NT], AF.Identity, [bb, spb], [kzb[1 + i] for i in range(nblk)],
                    bias=sp[:, SP_BKZ + var:SP_BKZ + var + 1])
            wu, wub = wget(("kv",))
            for j in range(2):
                bk, bb, _ = pb()
                for kc in range(KC):
                    mm(bk[:, :NT], wu[:, kc, j * 128:(j + 1) * 128], cx.hT[:, kc, :NT], kc == 0, kc == KC - 1, [wub, cx.hb[kc]], [bb])
                act(kvT[:, j, :], bk[:, :NT], AF.Identity, [bb, spb], [kvTb[j]], bias=sp[:, SP_BK + j:SP_BK + j + 1])
            for blk in range(nblk):
                cs = slice(blk * 128, (blk + 1) * 128)
                tb, tbb, _ = pb()
                tr(tb[:, 0:128], kvT[:, 1, cs], ident_f, [kvTb[1], cfb], [tbb])
                for var in range(4):
                    kvh, par = var // 2, var % 2
                    act(vz[:, 1 + blk, var, par * 64:(par + 1) * 64], tb[:, kvh * 64:(kvh + 1) * 64], AF.Copy, [tbb], [vzb[1 + blk]])
                last = sample or (tile_i == NPT - 1 and blk == nblk - 1)
                if last:
                    tt_i = 0
                    act(tok[:, 1, :], tb[:, 0:128], AF.Copy, [tbb], [tokb[1]])
                    tb2, tbb2, _ = pb()
                    tr(tb2[:, 0:128], kvT[:, 0, cs], ident_f, [kvTb[0], cfb], [tbb2])
                    act(tok[:, 0, :], tb2[:, 0:128], AF.Copy, [tbb2], [tokb[0]])
                    if sample:
                        for b in range(NB):
                            dma("sp", ks_o[b, 128 - DS:128, :], tok[b * DS:(b + 1) * DS, 0, :], [tokb[0]], [], "ko")
                            dma("sp", vs_o[b, 128 - DS:128, :], tok[b * DS:(b + 1) * DS, 1, :], [tokb[1]], [], "ko")
                        dma("sp", ks_o[:, 0:128 - DS, :], ck[:, DS:128, :], [], [], "ko")
                        dma("sp", vs_o[:, 0:128 - DS, :], cv[:, DS:128, :], [], [], "ko")
                    else:
                        dma("sp", kp_o[:, :], tok[:, 0, :], [tokb[0]], [], "ko")
                        dma("sp", vp_o[:, :], tok[:, 1, :], [tokb[1]], [], "ko")
            m_cur = cb[:, CB_MCUR:CB_MCUR + 128]
            m_prev = cb[:, CB_MPREV:CB_MPREV + 128]
            m_new = cb[:, CB_MNEW:CB_MNEW + 128]
            ei = [0]

            if not sample:
                its = [(blk, hp) for blk in range(nblk) for hp in range(8)]
                sres = {}

                def a_scores(i):
                    blk, hp = its[i]
                    gblk = tile_i * (TT // 128) + blk
                    qs = slice(blk * 128, (blk + 1) * 128)
                    kbs = [blk + 1] + ([blk] if gblk > 0 else [])
                    nk = len(kbs)
                    kvh = hp // 4
                    items = []
                    sbk, sbb, _ = pb()
                    col = 0
                    for par in range(2):
                        var = kvh * 2 + par
                        for kbi, kblk in enumerate(kbs):
                            pi = (i % 3) * 4 + par * 2 + kbi
                            mm(sbk[:, col * 128:(col + 1) * 128], kz[:, var, kblk * 128:(kblk + 1) * 128], qT1[:, hp, qs], True, True,
                               [kzb[kblk], qT1b[hp]], [sbb])
                            col += 1
                            items.append((par, var, kblk, pi))
                    w_ = 2 * nk * 128
                    e_i = i % 2
                    act(ee[:, e_i * 4:e_i * 4 + 2 * nk, :], sbk[:, :w_].rearrange("p (a c) -> p a c", c=128), AF.Exp, [sbb], [eeb[e_i]], scale=0.125)
                    base = (i % 3) * 4
                    dst = pT[:, base:base + 4, :].rearrange("p (a b) c -> p a b c", a=2)[:, :, 0:nk, :]
                    src = ee[:, e_i * 4:e_i * 4 + 2 * nk, :].rearrange("p (a b) c -> p a b c", a=2)
                    msk = cb[:, CB_MCUR:CB_MCUR + nk * 128].rearrange("p (b c) -> p b c", c=128).unsqueeze(1).broadcast_to([128, 2, nk, 128])
                    tt(dst, src, msk, ALU.mult, [eeb[e_i], cbb], [pTb[i % 3]])
                    sres[i] = items

                def a_nd(i):
                    blk, hp = its[i]
                    qs = slice(blk * 128, (blk + 1) * 128)
                    items = sres.pop(i)
                    nb_, nbb, _ = pb()
                    for n, (par, var, kblk, pi) in enumerate(items):
                        mm(nb_[:, :128], vz[:, kblk, var, :], pT[:, pi, :], n == 0, n == len(items) - 1, [vzb[kblk], pTb[i % 3]], [nbb])
                    db_, dbb, _ = pb()
                    for n, (par, var, kblk, pi) in enumerate(items):
                        mm(db_[:, :128], cb[:, CB_ONESZ + par * 128:CB_ONESZ + (par + 1) * 128], pT[:, pi, :], n == 0, n == len(items) - 1,
                           [cbb, pTb[i % 3]], [dbb])
                    ri = i % 2
                    act(rec[:, ri, :128], db_[:, :128], AF.Identity, [dbb, esinkb], [recb[ri]], bias=esink[:, hp:hp + 1])
                    recip(rec[:, ri, 128:256], rec[:, ri, :128], [recb[ri]], [recb[ri]])
                    tt(oT[:, hp, qs], nb_[:, :128], rec[:, ri, 128:256], ALU.mult, [nbb, recb[ri]], [oTb2[hp][blk]])

                a_scores(0)
                a_scores(1)
                for i in range(len(its)):
                    if i + 2 < len(its):
                        a_scores(i + 2)
                    a_nd(i)
                S.op("act", lambda e: e.activation(out=kz[:, :, 0:128], in_=kz[:, :, NT:NT + 128], func=AF.Copy), [kzb[nblk]], [kzb[0]])
                S.op("dve", lambda e: e.tensor_copy(out=vz[:, 0, :, :], in_=vz[:, nblk, :, :]), [vzb[nblk]], [vzb[0]])
            else:
                qs = slice(0, 128)
                for gi_ in range(4):
                    par, kvh = gi_ // 2, gi_ % 2
                    var = kvh * 2 + par
                    sbk, sbb, _ = pb()
                    for m in range(4):
                        hp = kvh * 4 + m
                        mm(sbk[:, m * 128:(m + 1) * 128], kz[:, var, 128:256], qT1[:, hp, qs], True, True, [kzb[1], qT1b[hp]], [sbb])
                    e_i = gi_ % 2
                    act(ee[:, e_i * 4:e_i * 4 + 4, :], sbk[:, :].rearrange("p (a c) -> p a c", c=128), AF.Exp, [sbb], [eeb[e_i]], scale=0.125)
                    tt(pT5[:, par * 2 + kvh, :, :].rearrange("p b (m i) -> p b m i", m=4),
                       ee[:, e_i * 4:e_i * 4 + 4, :].rearrange("p m (b i) -> p b m i", b=NB),
                       m_new.rearrange("p (b i) -> p b i", b=NB).unsqueeze(2).broadcast_to([128, NB, 4, DS]), ALU.mult,
                       [eeb[e_i], cbb], [pTb[gi_]])
                for kvh in range(2):
                    S.op("dve", (lambda o_, i_: (lambda e: e.tensor_copy(out=o_, in_=i_)))(
                        qS[:, kvh, :, :].rearrange("p b (m i) -> p b m i", m=4),
                        qT1[:, kvh * 4:(kvh + 1) * 4, :].rearrange("p m (b i) -> p b m i", b=NB)),
                        qT1b[kvh * 4:(kvh + 1) * 4], [qSb])
                nbk = [pb(hold=True) for _ in range(2)]
                dbk = [pb(hold=True) for _ in range(2)]
                m_cache = cb[:, CB_MCACHE:CB_MCACHE + 32].rearrange("p (m i) -> p m i", m=4)
                def issue_kc(b):
                    if b < NB:
                        dma("sp", Kc[b % 2][:, 0, :], ck[b], [], [Kcb[b % 2]], f"kc{b % 2}")
                        dma("sp", Kc[b % 2][:, 1, :], cv[b], [], [Kcb[b % 2]], f"kc{b % 2}")
                memset("dve", Kcz[0][:, :, :], 0.0, [Kczb[0]])
                memset("dve", Kcz[1][:, :, :], 0.0, [Kczb[1]])
                memset("dve", Vcz[0][:, :, :], 0.0, [Vczb[0]])
                memset("dve", Vcz[1][:, :, :], 0.0, [Vczb[1]])

                def s1(b):
                    ui = b % 2
                    for par in range(2):
                        kdst = Kcz[ui][:, par:4:2, par * 64:(par + 1) * 64] if False else \
                            Kcz[ui][:, :, :].rearrange("p (k r) c -> p k r c", k=2)[:, :, par, par * 64:(par + 1) * 64]
                        vdst = Vcz[ui][:, :, :].rearrange("p (k r) c -> p k r c", k=2)[:, :, par, par * 64:(par + 1) * 64]
                        ksrc = Kc[ui][:, 0, :].rearrange("p (k d) -> p k d", k=2)
                        vsrc = Kc[ui][:, 1, :].rearrange("p (k d) -> p k d", k=2)
                        S.op("dve", (lambda o_, i_: (lambda e: e.tensor_copy(out=o_, in_=i_)))(kdst, ksrc), [Kcb[ui]], [Kczb[ui]])
                        act(vdst, vsrc, AF.Copy, [Kcb[ui]], [Vczb[ui]])
                    tb, tbb, _ = pb()
                    tbv = tb[:, :].bitcast(BF16)
                    for var in range(4):
                        tr(tbv[:, var * 128:(var + 1) * 128], Kcz[ui][:, var, :], ident_bf, [Kczb[ui], cbb], [tbb])
                    act(kzc[ui][:, :, :], tbv[:, 0:512].rearrange("p (a b) -> p a b", a=4), AF.Copy, [tbb], [kzcb[ui]])

                def s2(b):
                    ui = b % 2
                    sbk, sbb, _ = pb()
                    for var in range(4):
                        kvh, par = var // 2, var % 2
                        mm(sbk[:, var * 32:(var + 1) * 32], kzc[ui][:, var, :], qS[:, kvh, b, :], True, True, [kzcb[ui], qSb], [sbb])
                    act(e32[:, :, :], sbk[:, 0:128].rearrange("p (v c) -> p v c", v=4), AF.Exp, [sbb], [e32b[0]], scale=0.125)
                    tt(pTc[ui][:, :, :].rearrange("p v (m i) -> p v m i", m=4), e32[:, :, :].rearrange("p v (m i) -> p v m i", m=4),
                       m_cache.unsqueeze(1).broadcast_to([128, 4, 4, DS]), ALU.mult, [e32b[0], cbb], [pTcb[ui]])

                def s3(b):
                    ui = b % 2
                    for kvh in range(2):
                        for (bk3, lhs_kind) in ((nbk[kvh], "v"), (dbk[kvh], "o")):
                            outv = bk3[0][:, b * 32:(b + 1) * 32]
                            n = 0
                            for par in range(2):
                                var = kvh * 2 + par
                                lhs = Vcz[ui][:, var, :] if lhs_kind == "v" else cb[:, CB_ONESZ + par * 128:CB_ONESZ + (par + 1) * 128]
                                rds = [Vczb[ui] if lhs_kind == "v" else cbb, pTcb[ui]]
                                mm(outv, lhs, pTc[ui][:, var, :], n == 0, False, rds, [bk3[1]])
                                n += 1
                            for par in range(2):
                                var = kvh * 2 + par
                                lhs = vz[:, 1, var, :] if lhs_kind == "v" else cb[:, CB_ONESZ + par * 128:CB_ONESZ + (par + 1) * 128]
                                rhs = pT5[:, par * 2 + kvh, b, :]
                                rds = [vzb[1] if lhs_kind == "v" else cbb, pTb[par * 2 + kvh]]
                                mm(outv, lhs, rhs, False, par == 1, rds, [bk3[1]])

                issue_kc(0)
                issue_kc(1)
                s1(0)
                for b in range(NB):
                    s2(b)
                    if b >= 1:
                        s3(b - 1)
                    issue_kc(b + 2)
                    if b + 1 < NB:
                        s1(b + 1)
                s3(NB - 1)
                for kvh in range(2):
                    dv = dbk[kvh][0][:, :].rearrange("p (b m i) -> p b m i", b=NB, m=4)
                    nv = nbk[kvh][0][:, :].rearrange("p (b m i) -> p b m i", b=NB, m=4)
                    r0 = rec[:, 0, :].rearrange("p (b m i) -> p b m i", b=NB, m=4)
                    r1 = rec[:, 1, :].rearrange("p (b m i) -> p b m i", b=NB, m=4)
                    for m in range(4):
                        hp = kvh * 4 + m
                        ts(r0[:, :, m, :], dv[:, :, m, :], esink[:, hp:hp + 1], None, ALU.add, ALU.bypass,
                           [dbk[kvh][1], esinkb], [recb[0]])
                    recip(rec[:, 1, :], rec[:, 0, :], [recb[0]], [recb[1]])
                    for m in range(4):
                        hp = kvh * 4 + m
                        tt(oT[:, hp, :].rearrange("p (b i) -> p b i", b=NB), nv[:, :, m, :], r1[:, :, m, :], ALU.mult,
                           [nbk[kvh][1], recb[1]], oTb2[hp])
                    release(nbk[kvh][2])
                    release(dbk[kvh][2])
            for half in range(2):
                wu, wub = wget(("wo1", half))
                for dm in range(4):
                    kc = half * 4 + dm
                    bk, bb, _ = pb()
                    for hp in range(8):
                        mm(bk[:, :NT], wu[:, hp, dm * 128:(dm + 1) * 128], oT[:, hp, :], hp == 0, hp == 7, [wub] + oTb2[hp], [bb])
                    stt(cx.xT[:, kc, :NT], bk[:, :NT], sp[:, SP_BO + kc:SP_BO + kc + 1], cx.xT[:, kc, :NT], ALU.add, ALU.add,
                        [bb, spb, cx.xb[kc]], [cx.xb[kc]])

        xin_p = xTp.rearrange("(kc p) t -> p kc t", p=128)
        yout_p = yTp.rearrange("(kc p) t -> p kc t", p=128)
        xin_s = xTs.rearrange("(kc p) t -> p kc t", p=128)
        yout_s = yTs.rearrange("(kc p) t -> p kc t", p=128)

        def load_rope(i, c0, n):
            dma("sp", rope[i][:, :, :n], rope_d[:, :, c0:c0 + n].rearrange("c p t -> p c t"), [], [ropeb[i]], f"rope{i}")

        dma("sp", pcx[0].xT[:, :, :], xin_p[:, :, 0:TT], [], pcx[0].xb, "xin0")
        load_rope(0, 0, TT)
        dma("sp", scx.xT[:, :, :], xin_s[:, :, :], [], scx.xb, "xins")
        for t in range(NPT):
            cx = pcx[t % 2]
            last = t == NPT - 1
            if not last:
                load_rope((t + 1) % 2, (t + 1) * TT, TT)
                dma("sp", pcx[(t + 1) % 2].xT[:, :, :], xin_p[:, :, (t + 1) * TT:(t + 2) * TT], [], pcx[(t + 1) % 2].xb, f"xin{(t + 1) % 2}")
            cxs = [cx, scx] if last else [cx]

            def dbg(i):
                if DEBUG and t == DEBUG_TILE:
                    dma("sp", dbg_o[i].rearrange("(kc p) t -> p kc t", p=128)[:, :, :TT], cx.xT[:, :, :TT], cx.xb, [], "dbg")
            rmsnorm(cx, 0)
            if last:
                load_rope((t + 1) % 2, SEQ, NS)
                rmsnorm(scx, 0)
                idle = pcx[(t + 1) % 2]
                R_ = layer0(cx, t, False, t % 2, rider=(scx, (t + 1) % 2, idle.xT, idle.xb))
                layer0(scx, NPT, True, (t + 1) % 2, pre=R_)
            else:
                layer0(cx, t, False, t % 2)
            dbg(0)
            for c_ in cxs:
                rmsnorm(c_, 1)
            ffn(cxs, 0)
            dbg(1)
            rmsnorm(cx, 2)
            layer1(cx, t, False)
            if last:
                rmsnorm(scx, 2)
                layer1(scx, NPT, True)
            dbg(2)
            for c_ in cxs:
                rmsnorm(c_, 3)
            ffn(cxs, 1)
            dbg(3)
            rmsnorm(cx, 4, to_x=True)
            dma("sp", yout_p[:, :, t * TT:(t + 1) * TT], cx.xT[:, :, :], cx.xb, [], "yout")
            if last:
                rmsnorm(scx, 4, to_x=True)
                dma("sp", yout_s[:, :, :], scx.xT[:, :, :], scx.xb, [], "yout")
        outs = Buf("outs")
        for name in list(S.dcount):
            if name in ("yout", "srp", "ko", "dbg") or name.startswith("sn"):
                outs.rs[("d", name)] = S.dcount[name]
        S.op("sp", lambda e: e.nop(), (), [outs])
        assert wstate["k"] == len(plan)
        S.emit(nc, es)
    return nc


_CACHE = {}


def _host_layout(inp):
    f = np.float32
    g = lambda k: np.ascontiguousarray(np.asarray(inp[k], dtype=f))
    x_prompt, x_sample = g("x_prompt"), g("x_sample")
    state_ret, ck, cv = g("state_ret")[0], g("cache_swa_k")[0], g("cache_swa_v")[0]
    wq, wk = g("ret_w_q")[0], g("ret_w_k")[0]
    wqk = np.concatenate([np.concatenate([wq[:, h * 256:(h + 1) * 256], wk[:, h * 256:(h + 1) * 256]], axis=1) for h in range(RH)], axis=1)
    wqkv = g("swa_w_qkv")[0]
    bqkv = g("swa_b_qkv")[0]
    wq1 = wqkv[:, :1024]
    wk1 = wqkv[:, 1024:1152]
    wv1 = wqkv[:, 1152:1280]
    wkz = np.zeros((D, 4, 128), f)
    bkz = np.zeros((4, 128), f)
    for kvh in range(2):
        for par in range(2):
            wkz[:, kvh * 2 + par, par * 64:(par + 1) * 64] = wk1[:, kvh * 64:(kvh + 1) * 64]
            bkz[kvh * 2 + par, par * 64:(par + 1) * 64] = bqkv[1024 + kvh * 64:1024 + (kvh + 1) * 64]
    wkz = wkz.reshape(D, 512)
    wkv = np.concatenate([wk1, wv1], axis=1)
    spar = np.zeros((128, SP_N), f)
    gains = [g("norm_mix")[0], g("norm_ffn")[0], g("norm_mix")[1], g("norm_ffn")[1], g("norm_final")]
    for i, v in enumerate(gains):
        spar[:, SP_G + i * 8:SP_G + (i + 1) * 8] = v.reshape(8, 128).T
    spar[:, SP_BQ:SP_BQ + 8] = bqkv[:1024].reshape(8, 128).T
    spar[:, SP_BKZ:SP_BKZ + 4] = bkz.T
    spar[:, SP_BK] = bqkv[1024:1152]
    spar[:, SP_BV] = bqkv[1152:1280]
    spar[:, SP_BO:SP_BO + 8] = g("swa_b_o")[0].reshape(8, 128).T
    sinks = g("swa_sinks")[0]
    spar[:, SP_SINK:SP_SINK + 8] = np.repeat(sinks.reshape(8, 2), 64, axis=1).T
    cf32, cb, rope, _ = _consts()
    shared = {
        "wqk": np.ascontiguousarray(wqk), "wv": g("ret_w_v")[0], "wg": g("ret_w_g")[0], "wo": g("ret_w_o")[0],
        "wq1": np.ascontiguousarray(wq1), "wkz": wkz, "wkv": np.ascontiguousarray(wkv), "wo1": g("swa_w_o")[0],
        "w1": g("ffn_w1"), "w3": g("ffn_w3"), "w2": g("ffn_w2"), "spar": spar, "cf32": cf32, "cb": cb, "rope": rope,
    }
    in_maps = []
    for c in range(NCORES):
        m = dict(shared)
        m["xTp"] = np.ascontiguousarray(x_prompt[c].T)
        m["xTs"] = np.ascontiguousarray(x_sample[c * NB:(c + 1) * NB].reshape(NS, D).T)
        m["st_in"] = np.ascontiguousarray(state_ret[c * NB:(c + 1) * NB])
        m["ck"] = np.ascontiguousarray(ck[c * NB:(c + 1) * NB].reshape(NB, 128, 128))
        m["cv"] = np.ascontiguousarray(cv[c * NB:(c + 1) * NB].reshape(NB, 128, 128))
        in_maps.append(m)
    return in_maps


def kernel(**inputs):
    if "nc" not in _CACHE:
        _CACHE["nc"] = build_program()
    nc = _CACHE["nc"]
    in_maps = _host_layout(inputs)
    res = run_bass_kernel_spmd(nc, in_maps, core_ids=list(range(NCORES)))
    R = res.results
    f = np.float32
    y_prompt = np.stack([R[c]["yTp"].T for c in range(NCORES)]).astype(f)
    y_sample = np.concatenate([R[c]["yTs"].T.reshape(NB, DS, D) for c in range(NCORES)]).astype(f)
    srp = np.stack([R[c]["srp"] for c in range(NCORES)])[None].astype(f)
    srs = np.concatenate([R[c]["srs"] for c in range(NCORES)])[None].astype(f)
    kp = np.stack([R[c]["kp"].reshape(128, 2, 64) for c in range(NCORES)])[None].astype(f)
    vp = np.stack([R[c]["vp"].reshape(128, 2, 64) for c in range(NCORES)])[None].astype(f)
    ks = np.concatenate([R[c]["ks"].reshape(NB, 128, 2, 64) for c in range(NCORES)])[None].astype(f)
    vs = np.concatenate([R[c]["vs"].reshape(NB, 128, 2, 64) for c in range(NCORES)])[None].astype(f)
    return (y_prompt, y_sample, srp, srs, kp, vp, ks, vs)
```

```python
import contextlib
import numpy as np
import concourse.bass as bass
import concourse.mybir as mybir
from concourse.bass_utils import run_bass_kernel_spmd

F32 = mybir.dt.float32
BF16 = mybir.dt.bfloat16
AF = mybir.ActivationFunctionType
ALU = mybir.AluOpType

NCORES = 8
D = 1024
KC = 8
SEQ = 2048
TT = 512
NPT = SEQ // TT
NS = 128
NB = 16
DS = 8
PAST = 16384
DFF = 2816
NF = DFF // 128
RH = 4
EPS = 1e-6
WSLOT = 4096
NW = 4
DEBUG = False
DEBUG_TILE = 0


class Buf:
    __slots__ = ("name", "excl", "const", "w", "rs")

    def __init__(self, name, excl=False, const=False, pending=None):
        self.name = name
        self.excl = excl
        self.const = const
        self.w = None
        self.rs = dict(pending) if pending else {}


class Sched:
    ENG = ("pe", "act", "dve", "pool", "sp")

    def __init__(self):
        self.ops = []
        self.cnt = {e: 0 for e in self.ENG}
        self.dcount = {}
        self.sig = set()

    def op(self, eng, fn, r=(), w=(), dsem=None):
        self.cnt[eng] += 1
        idx = self.cnt[eng]
        deps = {}

        def need(key, val):
            if deps.get(key, -1) < val:
                deps[key] = val

        for b in r:
            if b.w is not None:
                need(*b.w)
            if b.excl:
                for k, v in b.rs.items():
                    if k != ("e", eng):
                        need(k, v)
        for b in w:
            if b.w is not None:
                need(*b.w)
            for k, v in b.rs.items():
                need(k, v)
        if dsem is not None:
            self.dcount[dsem] = self.dcount.get(dsem, 0) + 16
            ev = (("d", dsem), self.dcount[dsem])
        else:
            ev = (("e", eng), idx)
        waits = []
        for key, val in deps.items():
            if key[0] == "e":
                if key[1] == "pe" and eng == "pe":
                    continue
                if key == ev[0] and val >= idx:
                    continue
                waits.append((key, val))
                self.sig.add((key[1], val))
            else:
                v = self.dcount[key[1]]
                if key == ev[0]:
                    v -= 16
                if v > 0:
                    waits.append((key, v))
        self.ops.append((eng, idx, fn, waits, dsem))
        for b in w:
            b.w = ev
            b.rs = {}
        for b in r:
            if b.const or b in w:
                continue
            b.rs[ev[0]] = ev[1]

    def emit(self, nc, es):
        handles = {"pe": nc.tensor, "act": nc.scalar, "dve": nc.vector, "pool": nc.gpsimd, "sp": nc.sync}
        sems = {e: es.enter_context(nc.semaphore("s_" + e)) for e in self.ENG}
        dsems = {n: es.enter_context(nc.semaphore("d_" + n)) for n in self.dcount}
        rank = {}
        for e in self.ENG:
            ids = sorted(i for (en, i) in self.sig if en == e)
            for k, i in enumerate(ids):
                rank[(e, i)] = k + 1
        seen = {e: {} for e in self.ENG}
        for (eng, idx, fn, waits, dsem) in self.ops:
            h = handles[eng]
            for key, val in waits:
                if key[0] == "e":
                    v = rank[(key[1], val)]
                    sem = sems[key[1]]
                else:
                    v = val
                    sem = dsems[key[1]]
                if seen[eng].get(key, 0) >= v:
                    continue
                seen[eng][key] = v
                h.wait_ge(sem, v)
            ins = fn(h)
            if dsem is not None:
                ins.then_inc(dsems[dsem], 16)
            elif (eng, idx) in self.sig:
                ins.then_inc(sems[eng], 1)


def _gammas():
    h = np.arange(RH, dtype=np.float64)
    return 1.0 - np.exp2(-(5.0 + h))


def _ref_decay(chunk):
    import jax
    import jax.numpy as jnp
    with jax.default_device(jax.devices("cpu")[0]):
        h = RH
        log_gamma = jnp.log1p(-jnp.exp2(-(5.0 + jnp.arange(h, dtype=jnp.float32))))
        idx = jnp.arange(chunk, dtype=jnp.float32)
        diff = idx[:, None] - idx[None, :]
        intra = jnp.where(diff >= 0, jnp.exp(log_gamma[:, None, None] * jnp.maximum(diff, 0.0)), 0.0)
        q_decay = jnp.exp(log_gamma[None, :] * (idx[:, None] + 1.0))
        k_decay = jnp.exp(log_gamma[None, :] * (chunk - 1.0 - idx[:, None]))
        chunk_decay = jnp.exp(log_gamma * chunk)
        return (np.asarray(intra, np.float32), np.asarray(q_decay, np.float32),
                np.asarray(k_decay, np.float32), np.asarray(chunk_decay, np.float32))


def _ref_rope():
    import jax
    import jax.numpy as jnp
    with jax.default_device(jax.devices("cpu")[0]):
        half = 128
        inv = 1.0 / (10000.0 ** (jnp.arange(half, dtype=jnp.float32) / half))
        pos = jnp.concatenate([jnp.arange(SEQ), PAST + (jnp.arange(NS) % DS)])
        ang = pos.astype(jnp.float32)[:, None] * inv[None, :]
        return np.asarray(jnp.cos(ang), np.float32).T, np.asarray(jnp.sin(ang), np.float32).T


_CONST_CACHE = {}


def _consts():
    if "c" in _CONST_CACHE:
        return _CONST_CACHE["c"]
    i = np.arange(128)
    intra_p, qdec_p, kdec_p, cd_p = _ref_decay(128)
    intra_s, qdec_s, kdec_s, cd_s = _ref_decay(DS)
    sixteenth = np.float32(1.0 / 16.0)
    mt = np.zeros((128, RH, 128), np.float32)
    mts = np.zeros((128, RH, 128), np.float32)
    same = (i[:, None] // DS) == (i[None, :] // DS)
    difs = (i[None, :] % DS) - (i[:, None] % DS)
    for h in range(RH):
        mt[:, h, :] = intra_p[h].T * sixteenth
        blk = intra_s[h][(i[None, :] % DS), (i[:, None] % DS)]
        mts[:, h, :] = np.where(same, blk, 0.0) * sixteenth
    qd = np.zeros((128, RH, 128), np.float32)
    qds = np.zeros((128, RH, 128), np.float32)
    kd = np.zeros((128, RH), np.float32)
    kds = np.zeros((128, RH), np.float32)
    for h in range(RH):
        qd[:, h, :] = qdec_p[:, h][None, :]
        qds[:, h, :] = qdec_s[i % DS, h][None, :]
        kd[:, h] = kdec_p[:, h] * sixteenth
        kds[:, h] = kdec_s[i % DS, h] * sixteenth
    kzm = ((i[:, None] // DS) == np.arange(NB)[None, :]).astype(np.float32)
    ident = np.eye(128, dtype=np.float32)
    cf32 = np.concatenate([mt.reshape(128, -1), mts.reshape(128, -1), qd.reshape(128, -1), qds.reshape(128, -1),
                           kd, kds, kzm, ident], axis=1).astype(np.float32)
    ones = np.ones((128, 128))
    onesz = np.zeros((128, 2, 128))
    onesz[:, 0, :64] = 1.0
    onesz[:, 1, 64:] = 1.0
    m_cur = (i[:, None] <= i[None, :]).astype(np.float64)
    m_prev = (i[:, None] > i[None, :]).astype(np.float64)
    m_new = (same & (difs >= 0)).astype(np.float64)
    m_cache = np.zeros((128, 4, DS))
    for t in range(DS):
        m_cache[:, :, t] = (i > t)[:, None]
    blockmask = np.zeros((128, NB, 128))
    for b in range(NB):
        blockmask[:, b, b * DS:(b + 1) * DS] = 1.0
    cb = np.concatenate([ident, ones, onesz.reshape(128, -1), m_cur, m_prev, m_new, m_cache.reshape(128, -1),
                         blockmask.reshape(128, -1)], axis=1).astype(np.float32)
    cosT, sinT = _ref_rope()
    rope = np.stack([cosT, sinT]).astype(np.float32)
    _CONST_CACHE["c"] = (cf32, cb, rope, (cd_p, cd_s))
    return _CONST_CACHE["c"]


CF_MT = 0
CF_MTS = 512
CF_QD = 1024
CF_QDS = 1536
CF_KD = 2048
CF_KDS = 2052
CF_KZM = 2056
CF_ID = 2072
CF_N = 2200
CB_ID = 0
CB_ONES = 128
CB_ONESZ = 256
CB_MCUR = 512
CB_MPREV = 640
CB_MNEW = 768
CB_MCACHE = 896
CB_BLK = 928
CB_N = 928 + 2048
SP_G = 0
SP_BQ = 40
SP_BKZ = 48
SP_BK = 52
SP_BV = 53
SP_BO = 54
SP_SINK = 62
SP_N = 70


def build_program():
    nc = bass.Bass("TRN2", target_bir_lowering=False)
    S = Sched()

    def din(name, shape):
        return nc.dram_tensor(name, list(shape), F32, kind="ExternalInput").ap()

    def dout(name, shape):
        return nc.dram_tensor(name, list(shape), F32, kind="ExternalOutput").ap()

    xTp = din("xTp", [D, SEQ])
    xTs = din("xTs", [D, NS])
    st_in = din("st_in", [NB, RH, 256, 512])
    ck = din("ck", [NB, 128, 128])
    cv = din("cv", [NB, 128, 128])
    wqk = din("wqk", [D, RH * 512])
    wv = din("wv", [D, 2048])
    wg = din("wg", [D, 2048])
    wo = din("wo", [2048, D])
    wq1 = din("wq1", [D, 1024])
    wkz = din("wkz", [D, 512])
    wkv = din("wkv", [D, 256])
    wo1 = din("wo1", [D, D])
    w1 = din("w1", [2, D, DFF])
    w3 = din("w3", [2, D, DFF])
    w2 = din("w2", [2, DFF, D])
    spar = din("spar", [128, SP_N])
    cf_d = din("cf32", [128, CF_N])
    cb_d = din("cb", [128, CB_N])
    rope_d = din("rope", [2, 128, SEQ + NS])

    yTp = dout("yTp", [D, SEQ])
    yTs = dout("yTs", [D, NS])
    srp = dout("srp", [RH, 256, 512])
    srs = dout("srs", [NB, RH, 256, 512])
    kp_o = dout("kp", [128, 128])
    vp_o = dout("vp", [128, 128])
    ks_o = dout("ks", [NB, 128, 128])
    vs_o = dout("vs", [NB, 128, 128])
    if DEBUG:
        dbg_o = dout("dbg", [4, D, TT])

    es = contextlib.ExitStack()
    with es:
        def sb(name, shape, dt):
            return es.enter_context(nc.sbuf_tensor(name, list(shape), dt))

        class Ctx:
            pass

        hT_p = sb("hT", [128, KC, TT], BF16)
        hb_p = [Buf(f"h{k}") for k in range(KC)]
        pcx = []
        for i in range(2):
            c_ = Ctx()
            c_.xT = sb(f"xT{i}", [128, KC, TT], F32)
            c_.xb = [Buf(f"x{i}_{k}") for k in range(KC)]
            c_.hT = hT_p
            c_.hb = hb_p
            c_.NT = TT
            pcx.append(c_)
        scx = Ctx()
        scx.xT = sb("xTs_sb", [128, KC, NS], F32)
        scx.xb = [Buf(f"xs{k}") for k in range(KC)]
        scx.hT = sb("hTs_sb", [128, KC, NS], BF16)
        scx.hb = [Buf(f"hs{k}") for k in range(KC)]
        scx.NT = NS
        sq = sb("sq", [128, 2, TT], BF16)
        sqb = [Buf("sq0"), Buf("sq1")]
        rtmp = sb("rtmp", [128, TT], F32)
        rtmpb = Buf("rtmp")
        rstd = sb("rstd", [128, TT], F32)
        rstdb = Buf("rstd")
        wring = [sb(f"wr{i}", [128, WSLOT], BF16) for i in range(NW)]
        wrb = [Buf(f"wr{i}") for i in range(NW)]
        rope = [sb(f"rope{i}", [128, 2, TT], F32) for i in range(2)]
        ropeb = [Buf("rope0"), Buf("rope1")]
        cf = sb("cf", [128, CF_N], F32)
        cfb = Buf("cf", const=True)
        cb = sb("cbt", [128, CB_N], BF16)
        cbb = Buf("cb", const=True)
        sp = sb("sp", [128, SP_N], F32)
        spb = Buf("sp", const=True)
        esink = sb("esink", [128, 8], F32)
        esinkb = Buf("esink", const=True)
        Sst = sb("Sst", [128, RH, 2, 512], F32)
        Sbf = sb("Sbf", [128, RH, 2, 512], BF16)
        Sb = [[Buf(f"S{h}{a}") for a in range(2)] for h in range(RH)]
        Sbfb = [[Buf(f"Sbf{h}{a}") for a in range(2)] for h in range(RH)]
        ARENA_COLS = 34112
        arena = sb("arena", [128, ARENA_COLS], BF16)
        stat = sb("stat", [128, 4, 16], F32)
        statb = [Buf(f"stat{i}") for i in range(4)]
        kz = sb("kz", [128, 4, 128 + TT], BF16)
        kzb = [Buf(f"kz{i}") for i in range(5)]
        vz = sb("vz", [128, 5, 4, 128], BF16)
        vzb = [Buf(f"vz{i}") for i in range(5)]

        banks = [es.enter_context(nc.psum_tensor(f"pb{i}", [128, 512], F32)) for i in range(8)]
        bankb = [Buf(f"bank{i}", excl=True) for i in range(8)]
        held = [False] * 8
        rr = [0]

        def pb(hold=False):
            for _ in range(16):
                i = rr[0] % 8
                rr[0] += 1
                if not held[i]:
                    if hold:
                        held[i] = True
                    return banks[i], bankb[i], i
            raise RuntimeError("no psum bank")

        def release(i):
            held[i] = False

        class Arena:
            def __init__(self):
                self.off = 0
                self.cur = []
                self.pending = {}

            def reset(self):
                for b in self.cur:
                    if b.w is not None:
                        k, v = b.w
                        if self.pending.get(k, -1) < v:
                            self.pending[k] = v
                    for k, v in b.rs.items():
                        if self.pending.get(k, -1) < v:
                            self.pending[k] = v
                self.cur = []
                self.off = 0

            def buf(self, name):
                b = Buf(name, pending=self.pending)
                self.cur.append(b)
                return b

            def alloc(self, name, shape, dt):
                n = 1
                for s in shape[1:]:
                    n *= s
                nb = n * (2 if dt == F32 else 1)
                nb = (nb + 15) // 16 * 16
                assert self.off + nb <= ARENA_COLS, (name, self.off, nb)
                v = arena[:, self.off:self.off + nb]
                self.off += nb
                if dt == F32:
                    v = v.bitcast(F32)[:, :n]
                else:
                    v = v[:, :n]
                if len(shape) == 3:
                    v = v.rearrange("p (a b) -> p a b", a=shape[1])
                elif len(shape) == 4:
                    v = v.rearrange("p (a b c) -> p a b c", a=shape[1], b=shape[2])
                return v

        A = Arena()

        def mm(out, lhsT, rhs, start, stop, r, w):
            S.op("pe", lambda e: e.matmul(out, lhsT, rhs, start=start, stop=stop), r, w)

        def tr(out, in_, ident, r, w):
            S.op("pe", lambda e: e.transpose(out, in_, ident), r, w)

        def act(out, in_, func, r, w, bias=0.0, scale=1.0):
            S.op("act", lambda e: e.activation(out=out, in_=in_, func=func, bias=bias, scale=scale), r, w)

        def tt(out, in0, in1, op, r, w, eng="dve"):
            S.op(eng, lambda e: e.tensor_tensor(out=out, in0=in0, in1=in1, op=op), r, w)

        def stt(out, in0, scalar, in1, op0, op1, r, w):
            S.op("dve", lambda e: e.scalar_tensor_tensor(out=out, in0=in0, scalar=scalar, in1=in1, op0=op0, op1=op1), r, w)

        def ts(out, in0, s1, s2, op0, op1, r, w):
            S.op("dve", lambda e: e.tensor_scalar(out=out, in0=in0, scalar1=s1, scalar2=s2, op0=op0, op1=op1), r, w)

        def recip(out, in_, r, w):
            S.op("dve", lambda e: e.reciprocal(out=out, in_=in_), r, w)

        def dma(eng, out, in_, r, w, sem):
            S.op(eng, lambda e: e.dma_start(out=out, in_=in_), r, w, dsem=sem)

        def memset(eng, ap, val, w):
            S.op(eng, lambda e: e.memset(ap, val), (), w)

        def wview(ap2d, ncols):
            return ap2d.rearrange("(kc p) n -> p kc n", p=128), ncols

        units = {}
        wqk_v = wqk.rearrange("(kc p) n -> p kc n", p=128)
        wv_v = wv.rearrange("(kc p) n -> p kc n", p=128)
        wg_v = wg.rearrange("(kc p) n -> p kc n", p=128)
        wo_v = wo.rearrange("(kc p) n -> p kc n", p=128)
        wq1_v = wq1.rearrange("(kc p) n -> p kc n", p=128)
        wkz_v = wkz.rearrange("(kc p) n -> p kc n", p=128)
        wkv_v = wkv.rearrange("(kc p) n -> p kc n", p=128)
        wo1_v = wo1.rearrange("(kc p) n -> p kc n", p=128)
        for h in range(RH):
            units[("qk", h)] = (wqk_v[:, :, h * 512:(h + 1) * 512], (8, 512))
            units[("v", h)] = (wv_v[:, :, h * 512:(h + 1) * 512], (8, 512))
            units[("g", h)] = (wg_v[:, :, h * 512:(h + 1) * 512], (8, 512))
        for half in range(2):
            for ecg in range(2):
                units[("wo", half, ecg)] = (wo_v[:, ecg * 8:(ecg + 1) * 8, half * 512:(half + 1) * 512], (8, 512))
            units[("q1", half)] = (wq1_v[:, :, half * 512:(half + 1) * 512], (8, 512))
            units[("wo1", half)] = (wo1_v[:, :, half * 512:(half + 1) * 512], (8, 512))
        units[("kz",)] = (wkz_v, (8, 512))
        units[("kv",)] = (wkv_v, (8, 256))
        FG = [(0, 4), (4, 4), (8, 4), (12, 4), (16, 4), (20, 2)]
        UG = [(0, 8), (8, 8), (16, 6)]
        for l in range(2):
            w1_v = w1[l].rearrange("(kc p) n -> p kc n", p=128)
            w3_v = w3[l].rearrange("(kc p) n -> p kc n", p=128)
            w2_v = w2[l].rearrange("(f p) n -> p f n", p=128)
            for gi, (f0, nf) in enumerate(FG):
                units[("w1", l, gi)] = (w1_v[:, :, f0 * 128:(f0 + nf) * 128], (8, nf * 128))
                units[("w3", l, gi)] = (w3_v[:, :, f0 * 128:(f0 + nf) * 128], (8, nf * 128))
            for half in range(2):
                for ui, (f0, nf) in enumerate(UG):
                    units[("w2", l, half, ui)] = (w2_v[:, f0:f0 + nf, half * 512:(half + 1) * 512], (nf, 512))

        plan = []
        for t in range(NPT):
            nrep = 2 if t == NPT - 1 else 1
            for rep in range(nrep):
                if rep == 0:
                    for h in range(RH):
                        plan += [("qk", h), ("v", h), ("g", h)]
                plan += [("wo", 0, 0), ("wo", 0, 1), ("wo", 1, 0), ("wo", 1, 1)]
            for l in range(2):
                if l == 1:
                    for _ in range(nrep):
                        plan += [("q1", 0), ("q1", 1), ("kz",), ("kv",), ("wo1", 0), ("wo1", 1)]
                for gi in range(len(FG)):
                    plan += [("w1", l, gi), ("w3", l, gi)]
                for half in range(2):
                    for ui in range(len(UG)):
                        plan.append(("w2", l, half, ui))
        wstate = {"k": 0, "loaded": 0}

        def slot_view(slot, shp):
            a, b = shp
            return wring[slot][:, :a * b].rearrange("p (a b) -> p a b", a=a)

        def wget(key):
            u = wstate["k"]
            assert plan[u] == key, (u, plan[u], key)
            wstate["k"] += 1
            while wstate["loaded"] < min(len(plan), u + NW - 1):
                j = wstate["loaded"]
                src, shp = units[plan[j]]
                sl = j % NW
                dma("pool", slot_view(sl, shp), src, (), [wrb[sl]], f"w{sl}")
                wstate["loaded"] += 1
            src, shp = units[key]
            return slot_view(u % NW, shp), wrb[u % NW]

        dma("sp", cf[:, :], cf_d[:, :], (), [cfb], "c0")
        dma("sp", sp[:, :], spar[:, :], (), [spb], "c0")
        dma("pool", cb[:, :], cb_d[:, :], (), [cbb], "c1")
        act(esink[:, :], sp[:, SP_SINK:SP_SINK + 8], AF.Exp, [spb], [esinkb])
        for h in range(RH):
            for a in range(2):
                memset("dve", Sst[:, h, a, :], 0.0, [Sb[h][a]])
                memset("dve", Sbf[:, h, a, :], 0.0, [Sbfb[h][a]])
        for i in range(5):
            memset("dve", vz[:, i, :, :], 0.0, [vzb[i]])

        ident_bf = cb[:, CB_ID:CB_ID + 128]
        ones_bf = cb[:, CB_ONES:CB_ONES + 128]
        ident_f = cf[:, CF_ID:CF_ID + 128]

        def gain(gi, kc):
            return sp[:, SP_G + gi * 8 + kc:SP_G + gi * 8 + kc + 1]

        def rmsnorm(cx, gi, to_x=False):
            NT = cx.NT
            bk, bb, _ = pb()
            for kc in range(KC):
                s = kc % 2
                act(sq[:, s, :NT], cx.xT[:, kc, :NT], AF.Square, [cx.xb[kc]], [sqb[s]])
                mm(bk[:, :NT], ones_bf, sq[:, s, :NT], kc == 0, kc == KC - 1, [sqb[s], cbb], [bb])
            act(rtmp[:, :NT], bk[:, :NT], AF.Sqrt, [bb], [rtmpb], bias=EPS, scale=1.0 / D)
            recip(rstd[:, :NT], rtmp[:, :NT], [rtmpb], [rstdb])
            for kc in range(KC):
                if to_x:
                    stt(cx.xT[:, kc, :NT], cx.xT[:, kc, :NT], gain(gi, kc), rstd[:, :NT], ALU.mult, ALU.mult,
                        [cx.xb[kc], rstdb, spb], [cx.xb[kc]])
                else:
                    stt(cx.hT[:, kc, :NT], cx.xT[:, kc, :NT], gain(gi, kc), rstd[:, :NT], ALU.mult, ALU.mult,
                        [cx.xb[kc], rstdb, spb], [cx.hb[kc]])

        def ffn(cxs, l):
            A.reset()
            aTs, abs_, s1s, s1bs = [], [], [], []
            for ci, cx in enumerate(cxs):
                aTs.append(A.alloc(f"aT{ci}", [128, NF, cx.NT], BF16))
                abs_.append([A.buf(f"a{f}") for f in range(NF)])
                s1s.append(A.alloc(f"s1{ci}", [128, 2, cx.NT], F32))
                s1bs.append([A.buf("s1a"), A.buf("s1b")])
            for gi, (f0, nf) in enumerate(FG):
                w1u, w1b_ = wget(("w1", l, gi))
                w3u, w3b_ = wget(("w3", l, gi))
                for fi in range(nf):
                    f = f0 + fi
                    for ci, cx in enumerate(cxs):
                        NT = cx.NT
                        aT, ab, s1, s1b = aTs[ci], abs_[ci], s1s[ci], s1bs[ci]
                        b1, bb1, _ = pb()
                        b3, bb3, _ = pb()
                        for kc in range(KC):
                            mm(b1[:, :NT], w1u[:, kc, fi * 128:(fi + 1) * 128], cx.hT[:, kc, :NT], kc == 0, kc == KC - 1,
                               [w1b_, cx.hb[kc]], [bb1])
                        for kc in range(KC):
                            mm(b3[:, :NT], w3u[:, kc, fi * 128:(fi + 1) * 128], cx.hT[:, kc, :NT], kc == 0, kc == KC - 1,
                               [w3b_, cx.hb[kc]], [bb3])
                        act(s1[:, f % 2, :NT], b1[:, :NT], AF.Silu, [bb1], [s1b[f % 2]])
                        tt(aT[:, f, :NT], s1[:, f % 2, :NT], b3[:, :NT], ALU.mult, [s1b[f % 2], bb3], [ab[f]])
            for half in range(2):
                bss = [[pb(hold=True) for _ in range(4)] for _ in cxs]
                for ui, (f0, nf) in enumerate(UG):
                    w2u, w2b_ = wget(("w2", l, half, ui))
                    for dm in range(4):
                        for fi in range(nf):
                            f = f0 + fi
                            for ci, cx in enumerate(cxs):
                                mm(bss[ci][dm][0][:, :cx.NT], w2u[:, fi, dm * 128:(dm + 1) * 128], aTs[ci][:, f, :cx.NT],
                                   f == 0, f == NF - 1, [w2b_, abs_[ci][f]], [bss[ci][dm][1]])
                for ci, cx in enumerate(cxs):
                    NT = cx.NT
                    for dm in range(4):
                        kc = half * 4 + dm
                        tt(cx.xT[:, kc, :NT], bss[ci][dm][0][:, :NT], cx.xT[:, kc, :NT], ALU.add, [bss[ci][dm][1], cx.xb[kc]], [cx.xb[kc]])
                        release(bss[ci][dm][2])

        def gn_gate(ob, obb, g_ap, gbuf, u_ap, ubuf, on_ap, onbuf, si):
            st = stat[:, si, :]
            sbf = statb[si]
            S.op("dve", lambda e: e.bn_stats(out=st[:, 0:6], in_=ob), [obb], [sbf])
            S.op("dve", lambda e: e.bn_aggr(out=st[:, 6:8], in_=st[:, 0:6]), [sbf], [sbf])
            act(st[:, 8:9], st[:, 7:8], AF.Sqrt, [sbf], [sbf], bias=EPS, scale=1.0)
            recip(st[:, 9:10], st[:, 8:9], [sbf], [sbf])
            stt(st[:, 10:11], st[:, 6:7], -1.0, st[:, 9:10], ALU.mult, ALU.mult, [sbf], [sbf])
            act(on_ap, ob, AF.Identity, [obb, sbf], [onbuf], bias=st[:, 10:11], scale=st[:, 9:10])
            tt(u_ap, on_ap, g_ap, ALU.mult, [onbuf, gbuf], [ubuf])

        def layer0(cx, tile_i, sample, rope_i, rider=None, pre=None):
            NT = cx.NT
            nch = NT // 128
            A.reset()
            if pre is None:
                qT = [A.alloc(f"qT{i}", [128, 2, NT], BF16) for i in range(2)]
                qdT = [A.alloc(f"qdT{i}", [128, 2, NT], BF16) for i in range(2)]
                kT = [A.alloc(f"kT{i}", [128, 2, NT], BF16) for i in range(2)]
                qTb = [[A.buf("qT") for _ in range(2)] for _ in range(2)]
                qdTb = [[A.buf("qdT") for _ in range(2)] for _ in range(2)]
                kTb = [[A.buf("kT") for _ in range(2)] for _ in range(2)]
                kd = [A.alloc(f"kd{i}", [128, nch, 256], BF16) for i in range(2)]
                kdb = [[A.buf("kd") for _ in range(nch)] for _ in range(2)]
                vv = [A.alloc(f"v{i}", [128, nch, 512], BF16) for i in range(2)]
                vb = [[A.buf("v") for _ in range(nch)] for _ in range(2)]
                gg = [A.alloc(f"g{i}", [128, nch, 512], BF16) for i in range(2)]
                gb = [[A.buf("g") for _ in range(nch)] for _ in range(2)]
            else:
                qT, qdT, kT, kd, vv, gg = pre["qT"], pre["qdT"], pre["kT"], pre["kd"], pre["v"], pre["g"]
                qTb, qdTb, kTb, kdb, vb, gb = pre["qTb"], pre["qdTb"], pre["kTb"], pre["kdb"], pre["vb"], pre["gb"]
            npar = len(qT)
            R = None
            if rider is not None:
                rcx, r_rope, store, store_bufs = rider
                pend = {}
                for b_ in store_bufs:
                    if b_.w is not None and pend.get(b_.w[0], -1) < b_.w[1]:
                        pend[b_.w[0]] = b_.w[1]
                    for k_, v_ in b_.rs.items():
                        if pend.get(k_, -1) < v_:
                            pend[k_] = v_
                flat = store.rearrange("p a b -> p (a b)").bitcast(BF16)
                R = {k_: [] for k_ in ("qT", "qdT", "kT", "kd", "v", "g", "qTb", "qdTb", "kTb", "kdb", "vb", "gb")}
                for h_ in range(RH):
                    o_ = h_ * 2048
                    R["qT"].append(flat[:, o_:o_ + 256].rearrange("p (a t) -> p a t", a=2))
                    R["qdT"].append(flat[:, o_ + 256:o_ + 512].rearrange("p (a t) -> p a t", a=2))
                    R["kT"].append(flat[:, o_ + 512:o_ + 768].rearrange("p (a t) -> p a t", a=2))
                    R["kd"].append(flat[:, o_ + 768:o_ + 1024].rearrange("p (c d) -> p c d", c=1))
                    R["v"].append(flat[:, o_ + 1024:o_ + 1536].rearrange("p (c d) -> p c d", c=1))
                    R["g"].append(flat[:, o_ + 1536:o_ + 2048].rearrange("p (c d) -> p c d", c=1))
                    R["qTb"].append([Buf("rqT", pending=pend) for _ in range(2)])
                    R["qdTb"].append([Buf("rqdT", pending=pend) for _ in range(2)])
                    R["kTb"].append([Buf("rkT", pending=pend) for _ in range(2)])
                    R["kdb"].append([Buf("rkd", pending=pend)])
                    R["vb"].append([Buf("rv", pending=pend)])
                    R["gb"].append([Buf("rg", pending=pend)])
                r_cos = rope[r_rope][:, 0, :NS]
                r_sin = rope[r_rope][:, 1, :NS]
                r_rpb = ropeb[r_rope]
            if pre is None:
                tmp = A.alloc("rt", [128, 4, NT], F32)
                tmpb = [A.buf(f"rt{i}") for i in range(4)]
                tq = A.alloc("tq", [128, 2, NT], F32)
                tqb = [A.buf("tq0"), A.buf("tq1")]
            uT = A.alloc("uT", [128, 16, NT], BF16)
            uTb = [[A.buf("uT") for _ in range(nch)] for _ in range(RH)]
            on = A.alloc("on", [128, 2, 512], F32)
            onb = [A.buf("on0"), A.buf("on1")]
            uu = A.alloc("u", [128, 2, 512], BF16)
            ub = [A.buf("u0"), A.buf("u1")]
            sTm = A.alloc("sTm", [128, 2, 128], BF16)
            sTmb = [A.buf("sTm0"), A.buf("sTm1")]
            if sample:
                NSR = 4
                S0 = [A.alloc(f"S0{i}", [128, 2, 512], F32) for i in range(NSR)]
                S0b = [A.buf("S0") for i in range(NSR)]
                S0bf = [A.alloc(f"S0bf{i}", [128, 2, 512], BF16) for i in range(3)]
                S0bfb = [A.buf("S0bf") for i in range(3)]
                Sn = [A.alloc(f"Sn{i}", [128, 2, 512], F32) for i in range(NSR)]
                Snb = [A.buf("Sn") for i in range(NSR)]
                Qz = A.alloc("Qz", [128, 2, NB, 128], BF16)
                Qzb = A.buf("Qz")
                KZ = A.alloc("KZ", [128, 2, NB, 128], BF16)
                KZb = A.buf("KZ")
            rp = rope[rope_i]
            rpb = ropeb[rope_i]
            cosv = rp[:, 0, :NT]
            sinv = rp[:, 1, :NT]
            mt_off = CF_MTS if sample else CF_MT
            qd_off = CF_QDS if sample else CF_QD
            kd_off = CF_KDS if sample else CF_KD
            gam = _gammas()
            sidx = [0]
            s0_issued = [0]

            def issue_s0(u):
                if u >= RH * NB or u < s0_issued[0]:
                    return
                assert u == s0_issued[0]
                s0_issued[0] += 1
                dma("sp", S0[u % 4][:, :, :], st_in[u % NB, u // NB].rearrange("(a p) e -> p a e", p=128), (), [S0b[u % 4]], f"s0{u % 4}")

            if sample:
                for u_ in range(4):
                    issue_s0(u_)
            wun = {}

            def p_qk(hd, which):
                def f():
                    p = hd % npar
                    if which == 0:
                        wun[hd] = wget(("qk", hd))
                    wu, wub = wun[hd]
                    base = which * 256
                    b1, bb1, _ = pb()
                    b2, bb2, _ = pb()
                    for kc in range(KC):
                        mm(b1[:, :NT], wu[:, kc, base:base + 128], cx.hT[:, kc, :NT], kc == 0, kc == KC - 1, [wub, cx.hb[kc]], [bb1])
                    for kc in range(KC):
                        mm(b2[:, :NT], wu[:, kc, base + 128:base + 256], cx.hT[:, kc, :NT], kc == 0, kc == KC - 1, [wub, cx.hb[kc]], [bb2])
                    tt(tmp[:, 0, :], b1[:, :NT], cosv, ALU.mult, [bb1, rpb], [tmpb[0]])
                    tt(tmp[:, 1, :], b2[:, :NT], sinv, ALU.mult, [bb2, rpb], [tmpb[1]])
                    tt(tmp[:, 2, :], b1[:, :NT], sinv, ALU.mult, [bb1, rpb], [tmpb[2]])
                    tt(tmp[:, 3, :], b2[:, :NT], cosv, ALU.mult, [bb2, rpb], [tmpb[3]])
                    if which == 0:
                        tt(tq[:, 0, :], tmp[:, 0, :], tmp[:, 1, :], ALU.subtract, [tmpb[0], tmpb[1]], [tqb[0]])
                        tt(tq[:, 1, :], tmp[:, 2, :], tmp[:, 3, :], ALU.add, [tmpb[2], tmpb[3]], [tqb[1]])
                        for a in range(2):
                            act(qT[p][:, a, :], tq[:, a, :], AF.Copy, [tqb[a]], [qTb[p][a]])
                            qdv = cf[:, qd_off + hd * 128:qd_off + (hd + 1) * 128].unsqueeze(1).broadcast_to([128, nch, 128])
                            tt(qdT[p][:, a, :].rearrange("p (c i) -> p c i", c=nch),
                               tq[:, a, :].rearrange("p (c i) -> p c i", c=nch), qdv, ALU.mult,
                               [tqb[a], cfb], [qdTb[p][a]])
                    else:
                        tt(kT[p][:, 0, :], tmp[:, 0, :], tmp[:, 1, :], ALU.subtract, [tmpb[0], tmpb[1]], [kTb[p][0]])
                        tt(kT[p][:, 1, :], tmp[:, 2, :], tmp[:, 3, :], ALU.add, [tmpb[2], tmpb[3]], [kTb[p][1]])
                    if R is not None:
                        b1, bb1, _ = pb()
                        b2, bb2, _ = pb()
                        for kc in range(KC):
                            mm(b1[:, :NS], wu[:, kc, base:base + 128], rcx.hT[:, kc, :NS], kc == 0, kc == KC - 1, [wub, rcx.hb[kc]], [bb1])
                        for kc in range(KC):
                            mm(b2[:, :NS], wu[:, kc, base + 128:base + 256], rcx.hT[:, kc, :NS], kc == 0, kc == KC - 1, [wub, rcx.hb[kc]], [bb2])
                        tt(tmp[:, 0, :NS], b1[:, :NS], r_cos, ALU.mult, [bb1, r_rpb], [tmpb[0]])
                        tt(tmp[:, 1, :NS], b2[:, :NS], r_sin, ALU.mult, [bb2, r_rpb], [tmpb[1]])
                        tt(tmp[:, 2, :NS], b1[:, :NS], r_sin, ALU.mult, [bb1, r_rpb], [tmpb[2]])
                        tt(tmp[:, 3, :NS], b2[:, :NS], r_cos, ALU.mult, [bb2, r_rpb], [tmpb[3]])
                        if which == 0:
                            tt(tq[:, 0, :NS], tmp[:, 0, :NS], tmp[:, 1, :NS], ALU.subtract, [tmpb[0], tmpb[1]], [tqb[0]])
                            tt(tq[:, 1, :NS], tmp[:, 2, :NS], tmp[:, 3, :NS], ALU.add, [tmpb[2], tmpb[3]], [tqb[1]])
                            for a in range(2):
                                act(R["qT"][hd][:, a, :], tq[:, a, :NS], AF.Copy, [tqb[a]], [R["qTb"][hd][a]])
                                tt(R["qdT"][hd][:, a, :], tq[:, a, :NS], cf[:, CF_QDS + hd * 128:CF_QDS + (hd + 1) * 128], ALU.mult,
                                   [tqb[a], cfb], [R["qdTb"][hd][a]])
                        else:
                            tt(R["kT"][hd][:, 0, :], tmp[:, 0, :NS], tmp[:, 1, :NS], ALU.subtract, [tmpb[0], tmpb[1]], [R["kTb"][hd][0]])
                            tt(R["kT"][hd][:, 1, :], tmp[:, 2, :NS], tmp[:, 3, :NS], ALU.add, [tmpb[2], tmpb[3]], [R["kTb"][hd][1]])
                return f

            def p_vg(hd, kind):
                def f():
                    p = hd % npar
                    wu_, wb_ = wget((kind, hd))
                    for c in range(nch):
                        bk, bb, _ = pb()
                        for kc in range(KC):
                            mm(bk[:, :], cx.hT[:, kc, c * 128:(c + 1) * 128], wu_[:, kc, :], kc == 0, kc == KC - 1, [wb_, cx.hb[kc]], [bb])
                        if kind == "v":
                            act(vv[p][:, c, :], bk[:, :], AF.Copy, [bb], [vb[p][c]])
                        else:
                            act(gg[p][:, c, :], bk[:, :], AF.Silu, [bb], [gb[p][c]])
                    if R is not None:
                        bk, bb, _ = pb()
                        for kc in range(KC):
                            mm(bk[:, :], rcx.hT[:, kc, 0:NS], wu_[:, kc, :], kc == 0, kc == KC - 1, [wb_, rcx.hb[kc]], [bb])
                        if kind == "v":
                            act(R["v"][hd][:, 0, :], bk[:, :], AF.Copy, [bb], [R["vb"][hd][0]])
                        else:
                            act(R["g"][hd][:, 0, :], bk[:, :], AF.Silu, [bb], [R["gb"][hd][0]])
                return f

            def p_kd(hd):
                def f():
                    p = hd % npar
                    for c in range(nch):
                        tb, tbb, _ = pb()
                        tbv = tb[:, :].bitcast(BF16)
                        for a in range(2):
                            tr(tbv[:, a * 128:(a + 1) * 128], kT[p][:, a, c * 128:(c + 1) * 128], ident_bf, [kTb[p][a], cbb], [tbb])
                        act(kd[p][:, c, :], tbv[:, 0:256], AF.Copy, [tbb, cfb], [kdb[p][c]],
                            scale=cf[:, kd_off + hd:kd_off + hd + 1])
                    if R is not None:
                        tb, tbb, _ = pb()
                        tbv = tb[:, :].bitcast(BF16)
                        for a in range(2):
                            tr(tbv[:, a * 128:(a + 1) * 128], R["kT"][hd][:, a, :], ident_bf, [R["kTb"][hd][a], cbb], [tbb])
                        act(R["kd"][hd][:, 0, :], tbv[:, 0:256], AF.Copy, [tbb, cfb], [R["kdb"][hd][0]],
                            scale=cf[:, CF_KDS + hd:CF_KDS + hd + 1])
                return f

            def proj_pieces(hd):
                if pre is not None:
                    return []
                return [p_qk(hd, 0), p_qk(hd, 1), p_vg(hd, "v"), p_vg(hd, "g"), p_kd(hd)]

            cst = {}

            def c_main(hd, c):
                p = hd % npar
                cd = float(_consts()[3][1 if sample else 0][hd])
                cs = slice(c * 128, (c + 1) * 128)
                if sample:
                    for a in range(2):
                        tt(Qz[:, a, :, :], qdT[p][:, a, :].unsqueeze(1).broadcast_to([128, NB, 128]),
                           cb[:, CB_BLK:CB_BLK + NB * 128].rearrange("p (b i) -> p b i", b=NB), ALU.mult,
                           [qdTb[p][a], cbb], [Qzb])
                        tt(KZ[:, a, :, :], kd[p][:, 0, a * 128:(a + 1) * 128].unsqueeze(1).broadcast_to([128, NB, 128]),
                           cf[:, CF_KZM:CF_KZM + NB].unsqueeze(2).broadcast_to([128, NB, 128]), ALU.mult,
                           [kdb[p][0], cfb], [KZb])
                sbk, sbb, _ = pb()
                for a in range(2):
                    mm(sbk[:, :128], kT[p][:, a, cs], qT[p][:, a, cs], a == 0, a == 1, [kTb[p][a], qTb[p][a]], [sbb])
                si = sidx[0] % 2
                sidx[0] += 1
                tt(sTm[:, si, :], sbk[:, :128], cf[:, mt_off + hd * 128:mt_off + (hd + 1) * 128], ALU.mult,
                   [sbb, cfb], [sTmb[si]])
                if not sample:
                    pbs = []
                    for a in range(2):
                        pk, pkb, _ = pb()
                        mm(pk[:, :], kd[p][:, c, a * 128:(a + 1) * 128], vv[p][:, c, :], True, True, [kdb[p][c], vb[p][c]], [pkb])
                        pbs.append((pk, pkb))
                    ob, obb, _ = pb()
                    mm(ob[:, :], sTm[:, si, :], vv[p][:, c, :], True, False, [sTmb[si], vb[p][c]], [obb])
                    for a in range(2):
                        mm(ob[:, :], qdT[p][:, a, cs], Sbf[:, hd, a, :], False, a == 1, [qdTb[p][a], Sbfb[hd][a]], [obb])
                    for a in range(2):
                        stt(Sst[:, hd, a, :], Sst[:, hd, a, :], cd, pbs[a][0][:, :], ALU.mult, ALU.add,
                            [Sb[hd][a], pbs[a][1]], [Sb[hd][a]])
                        act(Sbf[:, hd, a, :], Sst[:, hd, a, :], AF.Copy, [Sb[hd][a]], [Sbfb[hd][a]])
                else:
                    ob, obb, obi = pb(hold=True)
                    mm(ob[:, :], sTm[:, si, :], vv[p][:, c, :], True, False, [sTmb[si], vb[p][c]], [obb])
                    for b in range(NB):
                        u = hd * NB + b
                        ui = u % 4
                        u2 = u % 3
                        issue_s0(u)
                        issue_s0(u + 1)
                        issue_s0(u + 2)
                        issue_s0(u + 3)
                        for a in range(2):
                            act(S0bf[u2][:, a, :], S0[ui][:, a, :], AF.Copy, [S0b[ui]], [S0bfb[u2]])
                        for a in range(2):
                            mm(ob[:, :], Qz[:, a, b, :], S0bf[u2][:, a, :], False, (b == NB - 1 and a == 1), [Qzb, S0bfb[u2]], [obb])
                        for a in range(2):
                            pk, pkb, _ = pb()
                            mm(pk[:, :], KZ[:, a, b, :], vv[p][:, c, :], True, True, [KZb, vb[p][c]], [pkb])
                            stt(Sn[ui][:, a, :], S0[ui][:, a, :], cd, pk[:, :], ALU.mult, ALU.add, [S0b[ui], pkb], [Snb[ui]])
                        dma("pool", srs[b, hd].rearrange("(a p) e -> p a e", p=128), Sn[ui][:, :, :], [Snb[ui]], [], f"sn{ui}")
                    release(obi)
                gs = gidx[0] % 2
                gidx[0] += 1
                gn_gate(ob[:, :], obb, gg[p][:, c, :], gb[p][c], uu[:, gs, :], ub[gs], on[:, gs, :], onb[gs], (hd * nch + c) % 4)
                cst[(hd, c)] = gs

            def c_tail(hd, c):
                gs = cst[(hd, c)]
                cs = slice(c * 128, (c + 1) * 128)
                tb, tbb, _ = pb()
                tbv = tb[:, :].bitcast(BF16)
                for ec in range(4):
                    tr(tbv[:, ec * 128:(ec + 1) * 128], uu[:, gs, ec * 128:(ec + 1) * 128], ident_bf, [ub[gs], cbb], [tbb])
                act(uT[:, hd * 4:(hd + 1) * 4, cs], tbv[:, 0:512].rearrange("p (a b) -> p a b", a=4), AF.Copy, [tbb], [uTb[hd][c]])

            gidx = [0]
            for pc in proj_pieces(0):
                pc()
            for hd in range(RH):
                q_ = proj_pieces(hd + 1) if hd + 1 < RH else []
                for c in range(nch):
                    c_main(hd, c)
                    if q_:
                        q_.pop(0)()
                    if c > 0:
                        c_tail(hd, c - 1)
                while len(q_) > 1:
                    q_.pop(0)()
                c_tail(hd, nch - 1)
                while q_:
                    q_.pop(0)()
            for half in range(2):
                bs = [pb(hold=True) for _ in range(4)]
                for ecg in range(2):
                    wou, wob = wget(("wo", half, ecg))
                    for dm in range(4):
                        for ec in range(8):
                            e_ = ecg * 8 + ec
                            mm(bs[dm][0][:, :NT], wou[:, ec, dm * 128:(dm + 1) * 128], uT[:, e_, :NT], e_ == 0, e_ == 15,
                               [wob] + uTb[e_ // 4], [bs[dm][1]])
                for dm in range(4):
                    kc = half * 4 + dm
                    tt(cx.xT[:, kc, :NT], bs[dm][0][:, :NT], cx.xT[:, kc, :NT], ALU.add, [bs[dm][1], cx.xb[kc]], [cx.xb[kc]])
                    release(bs[dm][2])
            if (not sample) and tile_i == NPT - 1:
                dma("sp", srp.rearrange("h (a p) e -> p h a e", p=128), Sst[:, :, :, :],
                    [Sb[h][a] for h in range(RH) for a in range(2)], [], "srp")
            return R

        def layer1(cx, tile_i, sample):
            NT = cx.NT
            nblk = NT // 128
            A.reset()
            qT1 = A.alloc("qT1", [128, 8, NT], BF16)
            qT1b = [A.buf("qT1") for _ in range(8)]
            kvT = A.alloc("kvT", [128, 2, NT], F32)
            kvTb = [A.buf("kT1"), A.buf("vT1")]
            oT = A.alloc("oT", [128, 8, NT], BF16)
            oTb2 = [[A.buf("oT") for _ in range(nblk)] for _ in range(8)]
            ee = A.alloc("ee", [128, 8, 128], BF16)
            eeb = [A.buf("ee") for _ in range(8)]
            if sample:
                pT5 = A.alloc("pT", [128, 4, NB, 32], BF16)
                qS = A.alloc("qS", [128, 2, NB, 32], BF16)
                qSb = A.buf("qS")
            else:
                pT = A.alloc("pT", [128, 12, 128], BF16)
            pTb = [A.buf("pT") for _ in range(16 if sample else 12)]
            rec = A.alloc("rec", [128, 2, 512], F32)
            recb = [A.buf("rec0"), A.buf("rec1")]
            tok = A.alloc("tok", [128, 2, 128], F32)
            tokb = [A.buf("tok0"), A.buf("tok1")]
            if sample:
                Kc = [A.alloc(f"Kc{i}", [128, 2, 128], F32) for i in range(2)]
                Kcb = [A.buf("Kc0"), A.buf("Kc1")]
                Kcz = [A.alloc(f"Kcz{i}", [128, 4, 128], BF16) for i in range(2)]
                Kczb = [A.buf("Kcz0"), A.buf("Kcz1")]
                Vcz = [A.alloc(f"Vcz{i}", [128, 4, 128], BF16) for i in range(2)]
                Vczb = [A.buf("Vcz0"), A.buf("Vcz1")]
                kzc = [A.alloc(f"kzc{i}", [128, 4, 128], BF16) for i in range(2)]
                kzcb = [A.buf("kzc0"), A.buf("kzc1")]
                e32 = A.alloc("e32", [128, 4, 32], BF16)
                e32b = [A.buf("e32") for _ in range(4)]
                pTc = [A.alloc(f"pTc{i}", [128, 4, 32], BF16) for i in range(2)]
                pTcb = [A.buf("pTc0"), A.buf("pTc1")]

                def issue_kc(b):
                    if b < NB:
                        dma("sp", Kc[b % 2][:, 0, :], ck[b], [], [Kcb[b % 2]], f"kc{b % 2}")
                        dma("sp", Kc[b % 2][:, 1, :], cv[b], [], [Kcb[b % 2]], f"kc{b % 2}")
                issue_kc(0)
                issue_kc(1)
            for qu in range(2):
                wu, wub = wget(("q1", qu))
                for j in range(4):
                    hp = qu * 4 + j
                    bk, bb, _ = pb()
                    for kc in range(KC):
                        mm(bk[:, :NT], wu[:, kc, j * 128:(j + 1) * 128], cx.hT[:, kc, :NT], kc == 0, kc == KC - 1, [wub, cx.hb[kc]], [bb])
                    act(qT1[:, hp, :], bk[:, :NT], AF.Identity, [bb, spb], [qT1b[hp]], bias=sp[:, SP_BQ + hp:SP_BQ + hp + 1])
            wu, wub = wget(("kz",))
            for var in range(4):
                bk, bb, _ = pb()
                for kc in range(KC):
                    mm(bk[:, :NT], wu[:, kc, var * 128:(var + 1) * 128], cx.hT[:, kc, :NT], kc == 0, kc == KC - 1, [wub, cx.hb[kc]], [bb])
                act(kz[:, var, 128:128 + NT], bk[:, :NT], AF.Identity, [bb, spb], [kzb[1 + i] for i in range(nblk)],
                    bias=sp[:, SP_BKZ + var:SP_BKZ + var + 1])
            wu, wub = wget(("kv",))
            for j in range(2):
                bk, bb, _ = pb()
                for kc in range(KC):
                    mm(bk[:, :NT], wu[:, kc, j * 128:(j + 1) * 128], cx.hT[:, kc, :NT], kc == 0, kc == KC - 1, [wub, cx.hb[kc]], [bb])
                act(kvT[:, j, :], bk[:, :NT], AF.Identity, [bb, spb], [kvTb[j]], bias=sp[:, SP_BK + j:SP_BK + j + 1])
            for blk in range(nblk):
                cs = slice(blk * 128, (blk + 1) * 128)
                tb, tbb, _ = pb()
                tr(tb[:, 0:128], kvT[:, 1, cs], ident_f, [kvTb[1], cfb], [tbb])
                for var in range(4):
                    kvh, par = var // 2, var % 2
                    act(vz[:, 1 + blk, var, par * 64:(par + 1) * 64], tb[:, kvh * 64:(kvh + 1) * 64], AF.Copy, [tbb], [vzb[1 + blk]])
                last = sample or (tile_i == NPT - 1 and blk == nblk - 1)
                if last:
                    tt_i = 0
                    act(tok[:, 1, :], tb[:, 0:128], AF.Copy, [tbb], [tokb[1]])
                    tb2, tbb2, _ = pb()
                    tr(tb2[:, 0:128], kvT[:, 0, cs], ident_f, [kvTb[0], cfb], [tbb2])
                    act(tok[:, 0, :], tb2[:, 0:128], AF.Copy, [tbb2], [tokb[0]])
                    if sample:
                        for b in range(NB):
                            dma("sp", ks_o[b, 128 - DS:128, :], tok[b * DS:(b + 1) * DS, 0, :], [tokb[0]], [], "ko")
                            dma("sp", vs_o[b, 128 - DS:128, :], tok[b * DS:(b + 1) * DS, 1, :], [tokb[1]], [], "ko")
                        dma("sp", ks_o[:, 0:128 - DS, :], ck[:, DS:128, :], [], [], "ko")
                        dma("sp", vs_o[:, 0:128 - DS, :], cv[:, DS:128, :], [], [], "ko")
                    else:
                        dma("sp", kp_o[:, :], tok[:, 0, :], [tokb[0]], [], "ko")
                        dma("sp", vp_o[:, :], tok[:, 1, :], [tokb[1]], [], "ko")
            m_cur = cb[:, CB_MCUR:CB_MCUR + 128]
            m_prev = cb[:, CB_MPREV:CB_MPREV + 128]
            m_new = cb[:, CB_MNEW:CB_MNEW + 128]
            ei = [0]

            if not sample:
                its = [(blk, hp) for blk in range(nblk) for hp in range(8)]
                sres = {}

                def a_scores(i):
                    blk, hp = its[i]
                    gblk = tile_i * (TT // 128) + blk
                    qs = slice(blk * 128, (blk + 1) * 128)
                    kbs = [blk + 1] + ([blk] if gblk > 0 else [])
                    nk = len(kbs)
                    kvh = hp // 4
                    items = []
                    sbk, sbb, _ = pb()
                    col = 0
                    for par in range(2):
                        var = kvh * 2 + par
                        for kbi, kblk in enumerate(kbs):
                            pi = (i % 3) * 4 + par * 2 + kbi
                            mm(sbk[:, col * 128:(col + 1) * 128], kz[:, var, kblk * 128:(kblk + 1) * 128], qT1[:, hp, qs], True, True,
                               [kzb[kblk], qT1b[hp]], [sbb])
                            col += 1
                            items.append((par, var, kblk, pi))
                    w_ = 2 * nk * 128
                    e_i = i % 2
                    act(ee[:, e_i * 4:e_i * 4 + 2 * nk, :], sbk[:, :w_].rearrange("p (a c) -> p a c", c=128), AF.Exp, [sbb], [eeb[e_i]], scale=0.125)
                    base = (i % 3) * 4
                    dst = pT[:, base:base + 4, :].rearrange("p (a b) c -> p a b c", a=2)[:, :, 0:nk, :]
                    src = ee[:, e_i * 4:e_i * 4 + 2 * nk, :].rearrange("p (a b) c -> p a b c", a=2)
                    msk = cb[:, CB_MCUR:CB_MCUR + nk * 128].rearrange("p (b c) -> p b c", c=128).unsqueeze(1).broadcast_to([128, 2, nk, 128])
                    tt(dst, src, msk, ALU.mult, [eeb[e_i], cbb], [pTb[i % 3]])
                    sres[i] = items

                def a_nd(i):
                    blk, hp = its[i]
                    qs = slice(blk * 128, (blk + 1) * 128)
                    items = sres.pop(i)
                    nb_, nbb, _ = pb()
                    for n, (par, var, kblk, pi) in enumerate(items):
                        mm(nb_[:, :128], vz[:, kblk, var, :], pT[:, pi, :], n == 0, n == len(items) - 1, [vzb[kblk], pTb[i % 3]], [nbb])
                    db_, dbb, _ = pb()
                    for n, (par, var, kblk, pi) in enumerate(items):
                        mm(db_[:, :128], cb[:, CB_ONESZ + par * 128:CB_ONESZ + (par + 1) * 128], pT[:, pi, :], n == 0, n == len(items) - 1,
                           [cbb, pTb[i % 3]], [dbb])
                    ri = i % 2
                    act(rec[:, ri, :128], db_[:, :128], AF.Identity, [dbb, esinkb], [recb[ri]], bias=esink[:, hp:hp + 1])
                    recip(rec[:, ri, 128:256], rec[:, ri, :128], [recb[ri]], [recb[ri]])
                    tt(oT[:, hp, qs], nb_[:, :128], rec[:, ri, 128:256], ALU.mult, [nbb, recb[ri]], [oTb2[hp][blk]])

                a_scores(0)
                a_scores(1)
                for i in range(len(its)):
                    if i + 2 < len(its):
                        a_scores(i + 2)
                    a_nd(i)
                S.op("act", lambda e: e.activation(out=kz[:, :, 0:128], in_=kz[:, :, NT:NT + 128], func=AF.Copy), [kzb[nblk]], [kzb[0]])
                S.op("dve", lambda e: e.tensor_copy(out=vz[:, 0, :, :], in_=vz[:, nblk, :, :]), [vzb[nblk]], [vzb[0]])
            else:
                qs = slice(0, 128)
                for gi_ in range(4):
                    par, kvh = gi_ // 2, gi_ % 2
                    var = kvh * 2 + par
                    sbk, sbb, _ = pb()
                    for m in range(4):
                        hp = kvh * 4 + m
                        mm(sbk[:, m * 128:(m + 1) * 128], kz[:, var, 128:256], qT1[:, hp, qs], True, True, [kzb[1], qT1b[hp]], [sbb])
                    e_i = gi_ % 2
                    act(ee[:, e_i * 4:e_i * 4 + 4, :], sbk[:, :].rearrange("p (a c) -> p a c", c=128), AF.Exp, [sbb], [eeb[e_i]], scale=0.125)
                    tt(pT5[:, par * 2 + kvh, :, :].rearrange("p b (m i) -> p b m i", m=4),
                       ee[:, e_i * 4:e_i * 4 + 4, :].rearrange("p m (b i) -> p b m i", b=NB),
                       m_new.rearrange("p (b i) -> p b i", b=NB).unsqueeze(2).broadcast_to([128, NB, 4, DS]), ALU.mult,
                       [eeb[e_i], cbb], [pTb[gi_]])
                for kvh in range(2):
                    S.op("dve", (lambda o_, i_: (lambda e: e.tensor_copy(out=o_, in_=i_)))(
                        qS[:, kvh, :, :].rearrange("p b (m i) -> p b m i", m=4),
                        qT1[:, kvh * 4:(kvh + 1) * 4, :].rearrange("p m (b i) -> p b m i", b=NB)),
                        qT1b[kvh * 4:(kvh + 1) * 4], [qSb])
                nbk = [pb(hold=True) for _ in range(2)]
                dbk = [pb(hold=True) for _ in range(2)]
                m_cache = cb[:, CB_MCACHE:CB_MCACHE + 32].rearrange("p (m i) -> p m i", m=4)
                memset("dve", Kcz[0][:, :, :], 0.0, [Kczb[0]])
                memset("dve", Kcz[1][:, :, :], 0.0, [Kczb[1]])
                memset("dve", Vcz[0][:, :, :], 0.0, [Vczb[0]])
                memset("dve", Vcz[1][:, :, :], 0.0, [Vczb[1]])

                def s1(b):
                    ui = b % 2
                    for par in range(2):
                        kdst = Kcz[ui][:, par:4:2, par * 64:(par + 1) * 64] if False else \
                            Kcz[ui][:, :, :].rearrange("p (k r) c -> p k r c", k=2)[:, :, par, par * 64:(par + 1) * 64]
                        vdst = Vcz[ui][:, :, :].rearrange("p (k r) c -> p k r c", k=2)[:, :, par, par * 64:(par + 1) * 64]
                        ksrc = Kc[ui][:, 0, :].rearrange("p (k d) -> p k d", k=2)
                        vsrc = Kc[ui][:, 1, :].rearrange("p (k d) -> p k d", k=2)
                        S.op("dve", (lambda o_, i_: (lambda e: e.tensor_copy(out=o_, in_=i_)))(kdst, ksrc), [Kcb[ui]], [Kczb[ui]])
                        act(vdst, vsrc, AF.Copy, [Kcb[ui]], [Vczb[ui]])
                    tb, tbb, _ = pb()
                    tbv = tb[:, :].bitcast(BF16)
                    for var in range(4):
                        tr(tbv[:, var * 128:(var + 1) * 128], Kcz[ui][:, var, :], ident_bf, [Kczb[ui], cbb], [tbb])
                    act(kzc[ui][:, :, :], tbv[:, 0:512].rearrange("p (a b) -> p a b", a=4), AF.Copy, [tbb], [kzcb[ui]])

                def s2(b):
                    ui = b % 2
                    sbk, sbb, _ = pb()
                    for var in range(4):
                        kvh, par = var // 2, var % 2
                        mm(sbk[:, var * 32:(var + 1) * 32], kzc[ui][:, var, :], qS[:, kvh, b, :], True, True, [kzcb[ui], qSb], [sbb])
                    act(e32[:, :, :], sbk[:, 0:128].rearrange("p (v c) -> p v c", v=4), AF.Exp, [sbb], [e32b[0]], scale=0.125)
                    tt(pTc[ui][:, :, :].rearrange("p v (m i) -> p v m i", m=4), e32[:, :, :].rearrange("p v (m i) -> p v m i", m=4),
                       m_cache.unsqueeze(1).broadcast_to([128, 4, 4, DS]), ALU.mult, [e32b[0], cbb], [pTcb[ui]])

                def s3(b):
                    ui = b % 2
                    for kvh in range(2):
                        for (bk3, lhs_kind) in ((nbk[kvh], "v"), (dbk[kvh], "o")):
                            outv = bk3[0][:, b * 32:(b + 1) * 32]
                            n = 0
                            for par in range(2):
                                var = kvh * 2 + par
                                lhs = Vcz[ui][:, var, :] if lhs_kind == "v" else cb[:, CB_ONESZ + par * 128:CB_ONESZ + (par + 1) * 128]
                                rds = [Vczb[ui] if lhs_kind == "v" else cbb, pTcb[ui]]
                                mm(outv, lhs, pTc[ui][:, var, :], n == 0, False, rds, [bk3[1]])
                                n += 1
                            for par in range(2):
                                var = kvh * 2 + par
                                lhs = vz[:, 1, var, :] if lhs_kind == "v" else cb[:, CB_ONESZ + par * 128:CB_ONESZ + (par + 1) * 128]
                                rhs = pT5[:, par * 2 + kvh, b, :]
                                rds = [vzb[1] if lhs_kind == "v" else cbb, pTb[par * 2 + kvh]]
                                mm(outv, lhs, rhs, False, par == 1, rds, [bk3[1]])

                s1(0)
                for b in range(NB):
                    s2(b)
                    if b >= 1:
                        s3(b - 1)
                    issue_kc(b + 2)
                    if b + 1 < NB:
                        s1(b + 1)
                s3(NB - 1)
                for kvh in range(2):
                    dv = dbk[kvh][0][:, :].rearrange("p (b m i) -> p b m i", b=NB, m=4)
                    nv = nbk[kvh][0][:, :].rearrange("p (b m i) -> p b m i", b=NB, m=4)
                    r0 = rec[:, 0, :].rearrange("p (b m i) -> p b m i", b=NB, m=4)
                    r1 = rec[:, 1, :].rearrange("p (b m i) -> p b m i", b=NB, m=4)
                    for m in range(4):
                        hp = kvh * 4 + m
                        ts(r0[:, :, m, :], dv[:, :, m, :], esink[:, hp:hp + 1], None, ALU.add, ALU.bypass,
                           [dbk[kvh][1], esinkb], [recb[0]])
                    recip(rec[:, 1, :], rec[:, 0, :], [recb[0]], [recb[1]])
                    for m in range(4):
                        hp = kvh * 4 + m
                        tt(oT[:, hp, :].rearrange("p (b i) -> p b i", b=NB), nv[:, :, m, :], r1[:, :, m, :], ALU.mult,
                           [nbk[kvh][1], recb[1]], oTb2[hp])
                    release(nbk[kvh][2])
                    release(dbk[kvh][2])
            for half in range(2):
                wu, wub = wget(("wo1", half))
                for dm in range(4):
                    kc = half * 4 + dm
                    bk, bb, _ = pb()
                    for hp in range(8):
                        mm(bk[:, :NT], wu[:, hp, dm * 128:(dm + 1) * 128], oT[:, hp, :], hp == 0, hp == 7, [wub] + oTb2[hp], [bb])
                    stt(cx.xT[:, kc, :NT], bk[:, :NT], sp[:, SP_BO + kc:SP_BO + kc + 1], cx.xT[:, kc, :NT], ALU.add, ALU.add,
                        [bb, spb, cx.xb[kc]], [cx.xb[kc]])

        xin_p = xTp.rearrange("(kc p) t -> p kc t", p=128)
        yout_p = yTp.rearrange("(kc p) t -> p kc t", p=128)
        xin_s = xTs.rearrange("(kc p) t -> p kc t", p=128)
        yout_s = yTs.rearrange("(kc p) t -> p kc t", p=128)

        def load_rope(i, c0, n):
            dma("sp", rope[i][:, :, :n], rope_d[:, :, c0:c0 + n].rearrange("c p t -> p c t"), [], [ropeb[i]], f"rope{i}")

        dma("sp", pcx[0].xT[:, :, :], xin_p[:, :, 0:TT], [], pcx[0].xb, "xin0")
        load_rope(0, 0, TT)
        dma("sp", scx.xT[:, :, :], xin_s[:, :, :], [], scx.xb, "xins")
        for t in range(NPT):
            cx = pcx[t % 2]
            last = t == NPT - 1
            if not last:
                load_rope((t + 1) % 2, (t + 1) * TT, TT)
                dma("sp", pcx[(t + 1) % 2].xT[:, :, :], xin_p[:, :, (t + 1) * TT:(t + 2) * TT], [], pcx[(t + 1) % 2].xb, f"xin{(t + 1) % 2}")
            cxs = [cx, scx] if last else [cx]

            def dbg(i):
                if DEBUG and t == DEBUG_TILE:
                    dma("sp", dbg_o[i].rearrange("(kc p) t -> p kc t", p=128)[:, :, :TT], cx.xT[:, :, :TT], cx.xb, [], "dbg")
            rmsnorm(cx, 0)
            if last:
                load_rope((t + 1) % 2, SEQ, NS)
                rmsnorm(scx, 0)
                idle = pcx[(t + 1) % 2]
                R_ = layer0(cx, t, False, t % 2, rider=(scx, (t + 1) % 2, idle.xT, idle.xb))
                layer0(scx, NPT, True, (t + 1) % 2, pre=R_)
            else:
                layer0(cx, t, False, t % 2)
            dbg(0)
            for c_ in cxs:
                rmsnorm(c_, 1)
            ffn(cxs, 0)
            dbg(1)
            rmsnorm(cx, 2)
            layer1(cx, t, False)
            if last:
                rmsnorm(scx, 2)
                layer1(scx, NPT, True)
            dbg(2)
            for c_ in cxs:
                rmsnorm(c_, 3)
            ffn(cxs, 1)
            dbg(3)
            rmsnorm(cx, 4, to_x=True)
            dma("sp", yout_p[:, :, t * TT:(t + 1) * TT], cx.xT[:, :, :], cx.xb, [], "yout")
            if last:
                rmsnorm(scx, 4, to_x=True)
                dma("sp", yout_s[:, :, :], scx.xT[:, :, :], scx.xb, [], "yout")
        outs = Buf("outs")
        for name in list(S.dcount):
            if name in ("yout", "srp", "ko", "dbg") or name.startswith("sn"):
                outs.rs[("d", name)] = S.dcount[name]
        S.op("sp", lambda e: e.nop(), (), [outs])
        assert wstate["k"] == len(plan)
        S.emit(nc, es)
    return nc


_CACHE = {}


def _host_layout(inp):
    f = np.float32
    g = lambda k: np.ascontiguousarray(np.asarray(inp[k], dtype=f))
    x_prompt, x_sample = g("x_prompt"), g("x_sample")
    state_ret, ck, cv = g("state_ret")[0], g("cache_swa_k")[0], g("cache_swa_v")[0]
    wq, wk = g("ret_w_q")[0], g("ret_w_k")[0]
    wqk = np.concatenate([np.concatenate([wq[:, h * 256:(h + 1) * 256], wk[:, h * 256:(h + 1) * 256]], axis=1) for h in range(RH)], axis=1)
    wqkv = g("swa_w_qkv")[0]
    bqkv = g("swa_b_qkv")[0]
    wq1 = wqkv[:, :1024]
    wk1 = wqkv[:, 1024:1152]
    wv1 = wqkv[:, 1152:1280]
    wkz = np.zeros((D, 4, 128), f)
    bkz = np.zeros((4, 128), f)
    for kvh in range(2):
        for par in range(2):
            wkz[:, kvh * 2 + par, par * 64:(par + 1) * 64] = wk1[:, kvh * 64:(kvh + 1) * 64]
            bkz[kvh * 2 + par, par * 64:(par + 1) * 64] = bqkv[1024 + kvh * 64:1024 + (kvh + 1) * 64]
    wkz = wkz.reshape(D, 512)
    wkv = np.concatenate([wk1, wv1], axis=1)
    spar = np.zeros((128, SP_N), f)
    gains = [g("norm_mix")[0], g("norm_ffn")[0], g("norm_mix")[1], g("norm_ffn")[1], g("norm_final")]
    for i, v in enumerate(gains):
        spar[:, SP_G + i * 8:SP_G + (i + 1) * 8] = v.reshape(8, 128).T
    spar[:, SP_BQ:SP_BQ + 8] = bqkv[:1024].reshape(8, 128).T
    spar[:, SP_BKZ:SP_BKZ + 4] = bkz.T
    spar[:, SP_BK] = bqkv[1024:1152]
    spar[:, SP_BV] = bqkv[1152:1280]
    spar[:, SP_BO:SP_BO + 8] = g("swa_b_o")[0].reshape(8, 128).T
    sinks = g("swa_sinks")[0]
    spar[:, SP_SINK:SP_SINK + 8] = np.repeat(sinks.reshape(8, 2), 64, axis=1).T
    cf32, cb, rope, _ = _consts()
    shared = {
        "wqk": np.ascontiguousarray(wqk), "wv": g("ret_w_v")[0], "wg": g("ret_w_g")[0], "wo": g("ret_w_o")[0],
        "wq1": np.ascontiguousarray(wq1), "wkz": wkz, "wkv": np.ascontiguousarray(wkv), "wo1": g("swa_w_o")[0],
        "w1": g("ffn_w1"), "w3": g("ffn_w3"), "w2": g("ffn_w2"), "spar": spar, "cf32": cf32, "cb": cb, "rope": rope,
    }
    in_maps = []
    for c in range(NCORES):
        m = dict(shared)
        m["xTp"] = np.ascontiguousarray(x_prompt[c].T)
        m["xTs"] = np.ascontiguousarray(x_sample[c * NB:(c + 1) * NB].reshape(NS, D).T)
        m["st_in"] = np.ascontiguousarray(state_ret[c * NB:(c + 1) * NB])
        m["ck"] = np.ascontiguousarray(ck[c * NB:(c + 1) * NB].reshape(NB, 128, 128))
        m["cv"] = np.ascontiguousarray(cv[c * NB:(c + 1) * NB].reshape(NB, 128, 128))
        in_maps.append(m)
    return in_maps


def kernel(**inputs):
    if "nc" not in _CACHE:
        _CACHE["nc"] = build_program()
    nc = _CACHE["nc"]
    in_maps = _host_layout(inputs)
    res = run_bass_kernel_spmd(nc, in_maps, core_ids=list(range(NCORES)))
    R = res.results
    f = np.float32
    y_prompt = np.stack([R[c]["yTp"].T for c in range(NCORES)]).astype(f)
    y_sample = np.concatenate([R[c]["yTs"].T.reshape(NB, DS, D) for c in range(NCORES)]).astype(f)
    srp = np.stack([R[c]["srp"] for c in range(NCORES)])[None].astype(f)
    srs = np.concatenate([R[c]["srs"] for c in range(NCORES)])[None].astype(f)
    kp = np.stack([R[c]["kp"].reshape(128, 2, 64) for c in range(NCORES)])[None].astype(f)
    vp = np.stack([R[c]["vp"].reshape(128, 2, 64) for c in range(NCORES)])[None].astype(f)
    ks = np.concatenate([R[c]["ks"].reshape(NB, 128, 2, 64) for c in range(NCORES)])[None].astype(f)
    vs = np.concatenate([R[c]["vs"].reshape(NB, 128, 2, 64) for c in range(NCORES)])[None].astype(f)
    return (y_prompt, y_sample, srp, srs, kp, vp, ks, vs)
```

```python
import contextlib
import numpy as np
import concourse.bass as bass
import concourse.mybir as mybir
from concourse.bass_utils import run_bass_kernel_spmd

F32 = mybir.dt.float32
BF16 = mybir.dt.bfloat16
AF = mybir.ActivationFunctionType
ALU = mybir.AluOpType

NCORES = 8
D = 1024
KC = 8
SEQ = 2048
TT = 512
NPT = SEQ // TT
NS = 128
NB = 16
DS = 8
PAST = 16384
DFF = 2816
NF = DFF // 128
RH = 4
EPS = 1e-6
WSLOT = 4096
NW = 4
DEBUG = False
DEBUG_TILE = 0


class Buf:
    __slots__ = ("name", "excl", "const", "w", "rs")

    def __init__(self, name, excl=False, const=False, pending=None):
        self.name = name
        self.excl = excl
        self.const = const
        self.w = None
        self.rs = dict(pending) if pending else {}


class Sched:
    ENG = ("pe", "act", "dve", "pool", "sp")

    def __init__(self):
        self.ops = []
        self.cnt = {e: 0 for e in self.ENG}
        self.dcount = {}
        self.sig = set()

    def op(self, eng, fn, r=(), w=(), dsem=None):
        self.cnt[eng] += 1
        idx = self.cnt[eng]
        deps = {}

        def need(key, val):
            if deps.get(key, -1) < val:
                deps[key] = val

        for b in r:
            if b.w is not None:
                need(*b.w)
            if b.excl:
                for k, v in b.rs.items():
                    if k != ("e", eng):
                        need(k, v)
        for b in w:
            if b.w is not None:
                need(*b.w)
            for k, v in b.rs.items():
                need(k, v)
        if dsem is not None:
            self.dcount[dsem] = self.dcount.get(dsem, 0) + 16
            ev = (("d", dsem), self.dcount[dsem])
        else:
            ev = (("e", eng), idx)
        waits = []
        for key, val in deps.items():
            if key[0] == "e":
                if key[1] == "pe" and eng == "pe":
                    continue
                if key == ev[0] and val >= idx:
                    continue
                waits.append((key, val))
                self.sig.add((key[1], val))
            else:
                v = self.dcount[key[1]]
                if key == ev[0]:
                    v -= 16
                if v > 0:
                    waits.append((key, v))
        self.ops.append((eng, idx, fn, waits, dsem))
        for b in w:
            b.w = ev
            b.rs = {}
        for b in r:
            if b.const or b in w:
                continue
            b.rs[ev[0]] = ev[1]

    def emit(self, nc, es):
        handles = {"pe": nc.tensor, "act": nc.scalar, "dve": nc.vector, "pool": nc.gpsimd, "sp": nc.sync}
        sems = {e: es.enter_context(nc.semaphore("s_" + e)) for e in self.ENG}
        dsems = {n: es.enter_context(nc.semaphore("d_" + n)) for n in self.dcount}
        rank = {}
        for e in self.ENG:
            ids = sorted(i for (en, i) in self.sig if en == e)
            for k, i in enumerate(ids):
                rank[(e, i)] = k + 1
        seen = {e: {} for e in self.ENG}
        for (eng, idx, fn, waits, dsem) in self.ops:
            h = handles[eng]
            for key, val in waits:
                if key[0] == "e":
                    v = rank[(key[1], val)]
                    sem = sems[key[1]]
                else:
                    v = val
                    sem = dsems[key[1]]
                if seen[eng].get(key, 0) >= v:
                    continue
                seen[eng][key] = v
                h.wait_ge(sem, v)
            ins = fn(h)
            if dsem is not None:
                ins.then_inc(dsems[dsem], 16)
            elif (eng, idx) in self.sig:
                ins.then_inc(sems[eng], 1)


def _gammas():
    h = np.arange(RH, dtype=np.float64)
    return 1.0 - np.exp2(-(5.0 + h))


def _ref_decay(chunk):
    try:
        import jax
        import jax.numpy as jnp
        with jax.default_device(jax.devices("cpu")[0]):
            h = RH
            log_gamma = jnp.log1p(-jnp.exp2(-(5.0 + jnp.arange(h, dtype=jnp.float32))))
            idx = jnp.arange(chunk, dtype=jnp.float32)
            diff = idx[:, None] - idx[None, :]
            intra = jnp.where(diff >= 0, jnp.exp(log_gamma[:, None, None] * jnp.maximum(diff, 0.0)), 0.0)
            q_decay = jnp.exp(log_gamma[None, :] * (idx[:, None] + 1.0))
            k_decay = jnp.exp(log_gamma[None, :] * (chunk - 1.0 - idx[:, None]))
            chunk_decay = jnp.exp(log_gamma * chunk)
            return (np.asarray(intra, np.float32), np.asarray(q_decay, np.float32),
                    np.asarray(k_decay, np.float32), np.asarray(chunk_decay, np.float32))
    except Exception:
        f = np.float32
        log_gamma = np.log1p(-np.exp2(-(f(5.0) + np.arange(RH, dtype=f)))).astype(f)
        idx = np.arange(chunk, dtype=f)
        diff = idx[:, None] - idx[None, :]
        intra = np.where(diff >= 0, np.exp(log_gamma[:, None, None] * np.maximum(diff, f(0.0))), f(0.0)).astype(f)
        q_decay = np.exp(log_gamma[None, :] * (idx[:, None] + f(1.0))).astype(f)
        k_decay = np.exp(log_gamma[None, :] * (f(chunk) - f(1.0) - idx[:, None])).astype(f)
        chunk_decay = np.exp(log_gamma * f(chunk)).astype(f)
        return intra, q_decay, k_decay, chunk_decay


def _ref_rope():
    try:
        import jax
        import jax.numpy as jnp
        with jax.default_device(jax.devices("cpu")[0]):
            half = 128
            inv = 1.0 / (10000.0 ** (jnp.arange(half, dtype=jnp.float32) / half))
            pos = jnp.concatenate([jnp.arange(SEQ), PAST + (jnp.arange(NS) % DS)])
            ang = pos.astype(jnp.float32)[:, None] * inv[None, :]
            return np.asarray(jnp.cos(ang), np.float32).T, np.asarray(jnp.sin(ang), np.float32).T
    except Exception:
        f = np.float32
        half = 128
        inv = (f(1.0) / (f(10000.0) ** (np.arange(half, dtype=f) / f(half)))).astype(f)
        pos = np.concatenate([np.arange(SEQ), PAST + (np.arange(NS) % DS)]).astype(f)
        ang = (pos[:, None] * inv[None, :]).astype(f).astype(np.float64)
        return np.cos(ang).astype(f).T, np.sin(ang).astype(f).T


_CONST_CACHE = {}


def _consts():
    if "c" in _CONST_CACHE:
        return _CONST_CACHE["c"]
    i = np.arange(128)
    intra_p, qdec_p, kdec_p, cd_p = _ref_decay(128)
    intra_s, qdec_s, kdec_s, cd_s = _ref_decay(DS)
    sixteenth = np.float32(1.0 / 16.0)
    mt = np.zeros((128, RH, 128), np.float32)
    mts = np.zeros((128, RH, 128), np.float32)
    same = (i[:, None] // DS) == (i[None, :] // DS)
    difs = (i[None, :] % DS) - (i[:, None] % DS)
    for h in range(RH):
        mt[:, h, :] = intra_p[h].T * sixteenth
        blk = intra_s[h][(i[None, :] % DS), (i[:, None] % DS)]
        mts[:, h, :] = np.where(same, blk, 0.0) * sixteenth
    qd = np.zeros((128, RH, 128), np.float32)
    qds = np.zeros((128, RH, 128), np.float32)
    kd = np.zeros((128, RH), np.float32)
    kds = np.zeros((128, RH), np.float32)
    for h in range(RH):
        qd[:, h, :] = qdec_p[:, h][None, :]
        qds[:, h, :] = qdec_s[i % DS, h][None, :]
        kd[:, h] = kdec_p[:, h] * sixteenth
        kds[:, h] = kdec_s[i % DS, h] * sixteenth
    kzm = ((i[:, None] // DS) == np.arange(NB)[None, :]).astype(np.float32)
    ident = np.eye(128, dtype=np.float32)
    cf32 = np.concatenate([mt.reshape(128, -1), mts.reshape(128, -1), qd.reshape(128, -1), qds.reshape(128, -1),
                           kd, kds, kzm, ident], axis=1).astype(np.float32)
    ones = np.ones((128, 128))
    onesz = np.zeros((128, 2, 128))
    onesz[:, 0, :64] = 1.0
    onesz[:, 1, 64:] = 1.0
    m_cur = (i[:, None] <= i[None, :]).astype(np.float64)
    m_prev = (i[:, None] > i[None, :]).astype(np.float64)
    m_new = (same & (difs >= 0)).astype(np.float64)
    m_cache = np.zeros((128, 4, DS))
    for t in range(DS):
        m_cache[:, :, t] = (i > t)[:, None]
    blockmask = np.zeros((128, NB, 128))
    for b in range(NB):
        blockmask[:, b, b * DS:(b + 1) * DS] = 1.0
    cb = np.concatenate([ident, ones, onesz.reshape(128, -1), m_cur, m_prev, m_new, m_cache.reshape(128, -1),
                         blockmask.reshape(128, -1)], axis=1).astype(np.float32)
    cosT, sinT = _ref_rope()
    rope = np.stack([cosT, sinT]).astype(np.float32)
    _CONST_CACHE["c"] = (cf32, cb, rope, (cd_p, cd_s))
    return _CONST_CACHE["c"]


CF_MT = 0
CF_MTS = 512
CF_QD = 1024
CF_QDS = 1536
CF_KD = 2048
CF_KDS = 2052
CF_KZM = 2056
CF_ID = 2072
CF_N = 2200
CB_ID = 0
CB_ONES = 128
CB_ONESZ = 256
CB_MCUR = 512
CB_MPREV = 640
CB_MNEW = 768
CB_MCACHE = 896
CB_BLK = 928
CB_N = 928 + 2048
SP_G = 0
SP_BQ = 40
SP_BKZ = 48
SP_BK = 52
SP_BV = 53
SP_BO = 54
SP_SINK = 62
SP_N = 70


def build_program():
    nc = bass.Bass("TRN2", target_bir_lowering=False)
    S = Sched()

    def din(name, shape):
        return nc.dram_tensor(name, list(shape), F32, kind="ExternalInput").ap()

    def dout(name, shape):
        return nc.dram_tensor(name, list(shape), F32, kind="ExternalOutput").ap()

    xTp = din("xTp", [D, SEQ])
    xTs = din("xTs", [D, NS])
    st_in = din("st_in", [NB, RH, 256, 512])
    ck = din("ck", [NB, 128, 128])
    cv = din("cv", [NB, 128, 128])
    wqk = din("wqk", [D, RH * 512])
    wv = din("wv", [D, 2048])
    wg = din("wg", [D, 2048])
    wo = din("wo", [2048, D])
    wq1 = din("wq1", [D, 1024])
    wkz = din("wkz", [D, 512])
    wkv = din("wkv", [D, 256])
    wo1 = din("wo1", [D, D])
    w1 = din("w1", [2, D, DFF])
    w3 = din("w3", [2, D, DFF])
    w2 = din("w2", [2, DFF, D])
    spar = din("spar", [128, SP_N])
    cf_d = din("cf32", [128, CF_N])
    cb_d = din("cb", [128, CB_N])
    rope_d = din("rope", [2, 128, SEQ + NS])

    yTp = dout("yTp", [D, SEQ])
    yTs = dout("yTs", [D, NS])
    srp = dout("srp", [RH, 256, 512])
    srs = dout("srs", [NB, RH, 256, 512])
    kp_o = dout("kp", [128, 128])
    vp_o = dout("vp", [128, 128])
    ks_o = dout("ks", [NB, 128, 128])
    vs_o = dout("vs", [NB, 128, 128])
    if DEBUG:
        dbg_o = dout("dbg", [4, D, TT])

    es = contextlib.ExitStack()
    with es:
        def sb(name, shape, dt):
            return es.enter_context(nc.sbuf_tensor(name, list(shape), dt))

        class Ctx:
            pass

        hT_p = sb("hT", [128, KC, TT], BF16)
        hb_p = [Buf(f"h{k}") for k in range(KC)]
        pcx = []
        for i in range(2):
            c_ = Ctx()
            c_.xT = sb(f"xT{i}", [128, KC, TT], F32)
            c_.xb = [Buf(f"x{i}_{k}") for k in range(KC)]
            c_.hT = hT_p
            c_.hb = hb_p
            c_.NT = TT
            pcx.append(c_)
        scx = Ctx()
        scx.xT = sb("xTs_sb", [128, KC, NS], F32)
        scx.xb = [Buf(f"xs{k}") for k in range(KC)]
        scx.hT = sb("hTs_sb", [128, KC, NS], BF16)
        scx.hb = [Buf(f"hs{k}") for k in range(KC)]
        scx.NT = NS
        sq = sb("sq", [128, 2, TT], BF16)
        sqb = [Buf("sq0"), Buf("sq1")]
        rtmp = sb("rtmp", [128, TT], F32)
        rtmpb = Buf("rtmp")
        rstd = sb("rstd", [128, TT], F32)
        rstdb = Buf("rstd")
        wring = [sb(f"wr{i}", [128, WSLOT], BF16) for i in range(NW)]
        wrb = [Buf(f"wr{i}") for i in range(NW)]
        rope = [sb(f"rope{i}", [128, 2, TT], F32) for i in range(2)]
        ropeb = [Buf("rope0"), Buf("rope1")]
        cf = sb("cf", [128, CF_N], F32)
        cfb = Buf("cf", const=True)
        cb = sb("cbt", [128, CB_N], BF16)
        cbb = Buf("cb", const=True)
        sp = sb("sp", [128, SP_N], F32)
        spb = Buf("sp", const=True)
        esink = sb("esink", [128, 8], F32)
        esinkb = Buf("esink", const=True)
        Sst = sb("Sst", [128, RH, 2, 512], F32)
        Sbf = sb("Sbf", [128, RH, 2, 512], BF16)
        Sb = [[Buf(f"S{h}{a}") for a in range(2)] for h in range(RH)]
        Sbfb = [[Buf(f"Sbf{h}{a}") for a in range(2)] for h in range(RH)]
        ARENA_COLS = 34112
        arena = sb("arena", [128, ARENA_COLS], BF16)
        stat = sb("stat", [128, 4, 16], F32)
        statb = [Buf(f"stat{i}") for i in range(4)]
        kz = sb("kz", [128, 4, 128 + TT], BF16)
        kzb = [Buf(f"kz{i}") for i in range(5)]
        vz = sb("vz", [128, 5, 4, 128], BF16)
        vzb = [Buf(f"vz{i}") for i in range(5)]

        banks = [es.enter_context(nc.psum_tensor(f"pb{i}", [128, 512], F32)) for i in range(8)]
        bankb = [Buf(f"bank{i}", excl=True) for i in range(8)]
        held = [False] * 8
        rr = [0]

        def pb(hold=False):
            for _ in range(16):
                i = rr[0] % 8
                rr[0] += 1
                if not held[i]:
                    if hold:
                        held[i] = True
                    return banks[i], bankb[i], i
            raise RuntimeError("no psum bank")

        def release(i):
            held[i] = False

        class Arena:
            def __init__(self):
                self.off = 0
                self.cur = []
                self.pending = {}

            def reset(self):
                for b in self.cur:
                    if b.w is not None:
                        k, v = b.w
                        if self.pending.get(k, -1) < v:
                            self.pending[k] = v
                    for k, v in b.rs.items():
                        if self.pending.get(k, -1) < v:
                            self.pending[k] = v
                self.cur = []
                self.off = 0

            def buf(self, name):
                b = Buf(name, pending=self.pending)
                self.cur.append(b)
                return b

            def alloc(self, name, shape, dt):
                n = 1
                for s in shape[1:]:
                    n *= s
                nb = n * (2 if dt == F32 else 1)
                nb = (nb + 15) // 16 * 16
                assert self.off + nb <= ARENA_COLS, (name, self.off, nb)
                v = arena[:, self.off:self.off + nb]
                self.off += nb
                if dt == F32:
                    v = v.bitcast(F32)[:, :n]
                else:
                    v = v[:, :n]
                if len(shape) == 3:
                    v = v.rearrange("p (a b) -> p a b", a=shape[1])
                elif len(shape) == 4:
                    v = v.rearrange("p (a b c) -> p a b c", a=shape[1], b=shape[2])
                return v

        A = Arena()

        def mm(out, lhsT, rhs, start, stop, r, w):
            S.op("pe", lambda e: e.matmul(out, lhsT, rhs, start=start, stop=stop), r, w)

        def tr(out, in_, ident, r, w):
            S.op("pe", lambda e: e.transpose(out, in_, ident), r, w)

        def act(out, in_, func, r, w, bias=0.0, scale=1.0):
            S.op("act", lambda e: e.activation(out=out, in_=in_, func=func, bias=bias, scale=scale), r, w)

        def tt(out, in0, in1, op, r, w, eng="dve"):
            S.op(eng, lambda e: e.tensor_tensor(out=out, in0=in0, in1=in1, op=op), r, w)

        def stt(out, in0, scalar, in1, op0, op1, r, w):
            S.op("dve", lambda e: e.scalar_tensor_tensor(out=out, in0=in0, scalar=scalar, in1=in1, op0=op0, op1=op1), r, w)

        def ts(out, in0, s1, s2, op0, op1, r, w):
            S.op("dve", lambda e: e.tensor_scalar(out=out, in0=in0, scalar1=s1, scalar2=s2, op0=op0, op1=op1), r, w)

        def recip(out, in_, r, w):
            S.op("dve", lambda e: e.reciprocal(out=out, in_=in_), r, w)

        def dma(eng, out, in_, r, w, sem):
            S.op(eng, lambda e: e.dma_start(out=out, in_=in_), r, w, dsem=sem)

        def memset(eng, ap, val, w):
            S.op(eng, lambda e: e.memset(ap, val), (), w)

        def wview(ap2d, ncols):
            return ap2d.rearrange("(kc p) n -> p kc n", p=128), ncols

        units = {}
        wqk_v = wqk.rearrange("(kc p) n -> p kc n", p=128)
        wv_v = wv.rearrange("(kc p) n -> p kc n", p=128)
        wg_v = wg.rearrange("(kc p) n -> p kc n", p=128)
        wo_v = wo.rearrange("(kc p) n -> p kc n", p=128)
        wq1_v = wq1.rearrange("(kc p) n -> p kc n", p=128)
        wkz_v = wkz.rearrange("(kc p) n -> p kc n", p=128)
        wkv_v = wkv.rearrange("(kc p) n -> p kc n", p=128)
        wo1_v = wo1.rearrange("(kc p) n -> p kc n", p=128)
        for h in range(RH):
            units[("qk", h)] = (wqk_v[:, :, h * 512:(h + 1) * 512], (8, 512))
            units[("v", h)] = (wv_v[:, :, h * 512:(h + 1) * 512], (8, 512))
            units[("g", h)] = (wg_v[:, :, h * 512:(h + 1) * 512], (8, 512))
        for half in range(2):
            for ecg in range(2):
                units[("wo", half, ecg)] = (wo_v[:, ecg * 8:(ecg + 1) * 8, half * 512:(half + 1) * 512], (8, 512))
            units[("q1", half)] = (wq1_v[:, :, half * 512:(half + 1) * 512], (8, 512))
            units[("wo1", half)] = (wo1_v[:, :, half * 512:(half + 1) * 512], (8, 512))
        units[("kz",)] = (wkz_v, (8, 512))
        units[("kv",)] = (wkv_v, (8, 256))
        FG = [(0, 4), (4, 4), (8, 4), (12, 4), (16, 4), (20, 2)]
        UG = [(0, 8), (8, 8), (16, 6)]
        for l in range(2):
            w1_v = w1[l].rearrange("(kc p) n -> p kc n", p=128)
            w3_v = w3[l].rearrange("(kc p) n -> p kc n", p=128)
            w2_v = w2[l].rearrange("(f p) n -> p f n", p=128)
            for gi, (f0, nf) in enumerate(FG):
                units[("w1", l, gi)] = (w1_v[:, :, f0 * 128:(f0 + nf) * 128], (8, nf * 128))
                units[("w3", l, gi)] = (w3_v[:, :, f0 * 128:(f0 + nf) * 128], (8, nf * 128))
            for half in range(2):
                for ui, (f0, nf) in enumerate(UG):
                    units[("w2", l, half, ui)] = (w2_v[:, f0:f0 + nf, half * 512:(half + 1) * 512], (nf, 512))

        plan = []
        for t in range(NPT):
            nrep = 2 if t == NPT - 1 else 1
            for rep in range(nrep):
                if rep == 0:
                    for h in range(RH):
                        plan += [("qk", h), ("v", h), ("g", h)]
                plan += [("wo", 0, 0), ("wo", 0, 1), ("wo", 1, 0), ("wo", 1, 1)]
            for l in range(2):
                if l == 1:
                    for _ in range(nrep):
                        plan += [("q1", 0), ("q1", 1), ("kz",), ("kv",), ("wo1", 0), ("wo1", 1)]
                for gi in range(len(FG)):
                    plan += [("w1", l, gi), ("w3", l, gi)]
                for half in range(2):
                    for ui in range(len(UG)):
                        plan.append(("w2", l, half, ui))
        wstate = {"k": 0, "loaded": 0}

        def slot_view(slot, shp):
            a, b = shp
            return wring[slot][:, :a * b].rearrange("p (a b) -> p a b", a=a)

        def wget(key):
            u = wstate["k"]
            assert plan[u] == key, (u, plan[u], key)
            wstate["k"] += 1
            while wstate["loaded"] < min(len(plan), u + NW - 1):
                j = wstate["loaded"]
                src, shp = units[plan[j]]
                sl = j % NW
                dma("pool", slot_view(sl, shp), src, (), [wrb[sl]], f"w{sl}")
                wstate["loaded"] += 1
            src, shp = units[key]
            return slot_view(u % NW, shp), wrb[u % NW]

        dma("sp", cf[:, :], cf_d[:, :], (), [cfb], "c0")
        dma("sp", sp[:, :], spar[:, :], (), [spb], "c0")
        dma("pool", cb[:, :], cb_d[:, :], (), [cbb], "c1")
        act(esink[:, :], sp[:, SP_SINK:SP_SINK + 8], AF.Exp, [spb], [esinkb])
        for h in range(RH):
            for a in range(2):
                memset("dve", Sst[:, h, a, :], 0.0, [Sb[h][a]])
                memset("dve", Sbf[:, h, a, :], 0.0, [Sbfb[h][a]])
        for i in range(5):
            memset("dve", vz[:, i, :, :], 0.0, [vzb[i]])

        ident_bf = cb[:, CB_ID:CB_ID + 128]
        ones_bf = cb[:, CB_ONES:CB_ONES + 128]
        ident_f = cf[:, CF_ID:CF_ID + 128]

        def gain(gi, kc):
            return sp[:, SP_G + gi * 8 + kc:SP_G + gi * 8 + kc + 1]

        def rmsnorm(cx, gi, to_x=False):
            NT = cx.NT
            bk, bb, _ = pb()
            for kc in range(KC):
                s = kc % 2
                act(sq[:, s, :NT], cx.xT[:, kc, :NT], AF.Square, [cx.xb[kc]], [sqb[s]])
                mm(bk[:, :NT], ones_bf, sq[:, s, :NT], kc == 0, kc == KC - 1, [sqb[s], cbb], [bb])
            act(rtmp[:, :NT], bk[:, :NT], AF.Sqrt, [bb], [rtmpb], bias=EPS, scale=1.0 / D)
            recip(rstd[:, :NT], rtmp[:, :NT], [rtmpb], [rstdb])
            for kc in range(KC):
                if to_x:
                    stt(cx.xT[:, kc, :NT], cx.xT[:, kc, :NT], gain(gi, kc), rstd[:, :NT], ALU.mult, ALU.mult,
                        [cx.xb[kc], rstdb, spb], [cx.xb[kc]])
                else:
                    stt(cx.hT[:, kc, :NT], cx.xT[:, kc, :NT], gain(gi, kc), rstd[:, :NT], ALU.mult, ALU.mult,
                        [cx.xb[kc], rstdb, spb], [cx.hb[kc]])

        def ffn(cxs, l):
            A.reset()
            aTs, abs_, s1s, s1bs = [], [], [], []
            for ci, cx in enumerate(cxs):
                aTs.append(A.alloc(f"aT{ci}", [128, NF, cx.NT], BF16))
                abs_.append([A.buf(f"a{f}") for f in range(NF)])
                s1s.append(A.alloc(f"s1{ci}", [128, 2, cx.NT], F32))
                s1bs.append([A.buf("s1a"), A.buf("s1b")])
            for gi, (f0, nf) in enumerate(FG):
                w1u, w1b_ = wget(("w1", l, gi))
                w3u, w3b_ = wget(("w3", l, gi))
                for fi in range(nf):
                    f = f0 + fi
                    for ci, cx in enumerate(cxs):
                        NT = cx.NT
                        aT, ab, s1, s1b = aTs[ci], abs_[ci], s1s[ci], s1bs[ci]
                        b1, bb1, _ = pb()
                        b3, bb3, _ = pb()
                        for kc in range(KC):
                            mm(b1[:, :NT], w1u[:, kc, fi * 128:(fi + 1) * 128], cx.hT[:, kc, :NT], kc == 0, kc == KC - 1,
                               [w1b_, cx.hb[kc]], [bb1])
                        for kc in range(KC):
                            mm(b3[:, :NT], w3u[:, kc, fi * 128:(fi + 1) * 128], cx.hT[:, kc, :NT], kc == 0, kc == KC - 1,
                               [w3b_, cx.hb[kc]], [bb3])
                        act(s1[:, f % 2, :NT], b1[:, :NT], AF.Silu, [bb1], [s1b[f % 2]])
                        tt(aT[:, f, :NT], s1[:, f % 2, :NT], b3[:, :NT], ALU.mult, [s1b[f % 2], bb3], [ab[f]])
            for half in range(2):
                bss = [[pb(hold=True) for _ in range(4)] for _ in cxs]
                for ui, (f0, nf) in enumerate(UG):
                    w2u, w2b_ = wget(("w2", l, half, ui))
                    for dm in range(4):
                        for fi in range(nf):
                            f = f0 + fi
                            for ci, cx in enumerate(cxs):
                                mm(bss[ci][dm][0][:, :cx.NT], w2u[:, fi, dm * 128:(dm + 1) * 128], aTs[ci][:, f, :cx.NT],
                                   f == 0, f == NF - 1, [w2b_, abs_[ci][f]], [bss[ci][dm][1]])
                for ci, cx in enumerate(cxs):
                    NT = cx.NT
                    for dm in range(4):
                        kc = half * 4 + dm
                        tt(cx.xT[:, kc, :NT], bss[ci][dm][0][:, :NT], cx.xT[:, kc, :NT], ALU.add, [bss[ci][dm][1], cx.xb[kc]], [cx.xb[kc]])
                        release(bss[ci][dm][2])

        def gn_gate(ob, obb, g_ap, gbuf, u_ap, ubuf, on_ap, onbuf, si):
            st = stat[:, si, :]
            sbf = statb[si]
            S.op("dve", lambda e: e.bn_stats(out=st[:, 0:6], in_=ob), [obb], [sbf])
            S.op("dve", lambda e: e.bn_aggr(out=st[:, 6:8], in_=st[:, 0:6]), [sbf], [sbf])
            act(st[:, 8:9], st[:, 7:8], AF.Sqrt, [sbf], [sbf], bias=EPS, scale=1.0)
            recip(st[:, 9:10], st[:, 8:9], [sbf], [sbf])
            stt(st[:, 10:11], st[:, 6:7], -1.0, st[:, 9:10], ALU.mult, ALU.mult, [sbf], [sbf])
            act(on_ap, ob, AF.Identity, [obb, sbf], [onbuf], bias=st[:, 10:11], scale=st[:, 9:10])
            tt(u_ap, on_ap, g_ap, ALU.mult, [onbuf, gbuf], [ubuf])

        def layer0(cx, tile_i, sample, rope_i, rider=None, pre=None):
            NT = cx.NT
            nch = NT // 128
            A.reset()
            if pre is None:
                qT = [A.alloc(f"qT{i}", [128, 2, NT], BF16) for i in range(2)]
                qdT = [A.alloc(f"qdT{i}", [128, 2, NT], BF16) for i in range(2)]
                kT = [A.alloc(f"kT{i}", [128, 2, NT], BF16) for i in range(2)]
                qTb = [[A.buf("qT") for _ in range(2)] for _ in range(2)]
                qdTb = [[A.buf("qdT") for _ in range(2)] for _ in range(2)]
                kTb = [[A.buf("kT") for _ in range(2)] for _ in range(2)]
                kd = [A.alloc(f"kd{i}", [128, nch, 256], BF16) for i in range(2)]
                kdb = [[A.buf("kd") for _ in range(nch)] for _ in range(2)]
                vv = [A.alloc(f"v{i}", [128, nch, 512], BF16) for i in range(2)]
                vb = [[A.buf("v") for _ in range(nch)] for _ in range(2)]
                gg = [A.alloc(f"g{i}", [128, nch, 512], BF16) for i in range(2)]
                gb = [[A.buf("g") for _ in range(nch)] for _ in range(2)]
            else:
                qT, qdT, kT, kd, vv, gg = pre["qT"], pre["qdT"], pre["kT"], pre["kd"], pre["v"], pre["g"]
                qTb, qdTb, kTb, kdb, vb, gb = pre["qTb"], pre["qdTb"], pre["kTb"], pre["kdb"], pre["vb"], pre["gb"]
            npar = len(qT)
            R = None
            if rider is not None:
                rcx, r_rope, store, store_bufs = rider
                pend = {}
                for b_ in store_bufs:
                    if b_.w is not None and pend.get(b_.w[0], -1) < b_.w[1]:
                        pend[b_.w[0]] = b_.w[1]
                    for k_, v_ in b_.rs.items():
                        if pend.get(k_, -1) < v_:
                            pend[k_] = v_
                flat = store.rearrange("p a b -> p (a b)").bitcast(BF16)
                R = {k_: [] for k_ in ("qT", "qdT", "kT", "kd", "v", "g", "qTb", "qdTb", "kTb", "kdb", "vb", "gb")}
                for h_ in range(RH):
                    o_ = h_ * 2048
                    R["qT"].append(flat[:, o_:o_ + 256].rearrange("p (a t) -> p a t", a=2))
                    R["qdT"].append(flat[:, o_ + 256:o_ + 512].rearrange("p (a t) -> p a t", a=2))
                    R["kT"].append(flat[:, o_ + 512:o_ + 768].rearrange("p (a t) -> p a t", a=2))
                    R["kd"].append(flat[:, o_ + 768:o_ + 1024].rearrange("p (c d) -> p c d", c=1))
                    R["v"].append(flat[:, o_ + 1024:o_ + 1536].rearrange("p (c d) -> p c d", c=1))
                    R["g"].append(flat[:, o_ + 1536:o_ + 2048].rearrange("p (c d) -> p c d", c=1))
                    R["qTb"].append([Buf("rqT", pending=pend) for _ in range(2)])
                    R["qdTb"].append([Buf("rqdT", pending=pend) for _ in range(2)])
                    R["kTb"].append([Buf("rkT", pending=pend) for _ in range(2)])
                    R["kdb"].append([Buf("rkd", pending=pend)])
                    R["vb"].append([Buf("rv", pending=pend)])
                    R["gb"].append([Buf("rg", pending=pend)])
                r_cos = rope[r_rope][:, 0, :NS]
                r_sin = rope[r_rope][:, 1, :NS]
                r_rpb = ropeb[r_rope]
            if pre is None:
                tmp = A.alloc("rt", [128, 4, NT], F32)
                tmpb = [A.buf(f"rt{i}") for i in range(4)]
                tq = A.alloc("tq", [128, 2, NT], F32)
                tqb = [A.buf("tq0"), A.buf("tq1")]
            uT = A.alloc("uT", [128, 16, NT], BF16)
            uTb = [[A.buf("uT") for _ in range(nch)] for _ in range(RH)]
            on = A.alloc("on", [128, 2, 512], F32)
            onb = [A.buf("on0"), A.buf("on1")]
            uu = A.alloc("u", [128, 2, 512], BF16)
            ub = [A.buf("u0"), A.buf("u1")]
            sTm = A.alloc("sTm", [128, 2, 128], BF16)
            sTmb = [A.buf("sTm0"), A.buf("sTm1")]
            if sample:
                NSR = 4
                S0 = [A.alloc(f"S0{i}", [128, 2, 512], F32) for i in range(NSR)]
                S0b = [A.buf("S0") for i in range(NSR)]
                S0bf = [A.alloc(f"S0bf{i}", [128, 2, 512], BF16) for i in range(3)]
                S0bfb = [A.buf("S0bf") for i in range(3)]
                Sn = [A.alloc(f"Sn{i}", [128, 2, 512], F32) for i in range(NSR)]
                Snb = [A.buf("Sn") for i in range(NSR)]
                Qz = A.alloc("Qz", [128, 2, NB, 128], BF16)
                Qzb = A.buf("Qz")
                KZ = A.alloc("KZ", [128, 2, NB, 128], BF16)
                KZb = A.buf("KZ")
            rp = rope[rope_i]
            rpb = ropeb[rope_i]
            cosv = rp[:, 0, :NT]
            sinv = rp[:, 1, :NT]
            mt_off = CF_MTS if sample else CF_MT
            qd_off = CF_QDS if sample else CF_QD
            kd_off = CF_KDS if sample else CF_KD
            gam = _gammas()
            sidx = [0]
            s0_issued = [0]

            def issue_s0(u):
                if u >= RH * NB or u < s0_issued[0]:
                    return
                assert u == s0_issued[0]
                s0_issued[0] += 1
                dma("sp", S0[u % 4][:, :, :], st_in[u % NB, u // NB].rearrange("(a p) e -> p a e", p=128), (), [S0b[u % 4]], f"s0{u % 4}")

            if sample:
                for u_ in range(4):
                    issue_s0(u_)
            wun = {}

            def p_qk(hd, which):
                def f():
                    p = hd % npar
                    if which == 0:
                        wun[hd] = wget(("qk", hd))
                    wu, wub = wun[hd]
                    base = which * 256
                    b1, bb1, _ = pb()
                    b2, bb2, _ = pb()
                    for kc in range(KC):
                        mm(b1[:, :NT], wu[:, kc, base:base + 128], cx.hT[:, kc, :NT], kc == 0, kc == KC - 1, [wub, cx.hb[kc]], [bb1])
                    for kc in range(KC):
                        mm(b2[:, :NT], wu[:, kc, base + 128:base + 256], cx.hT[:, kc, :NT], kc == 0, kc == KC - 1, [wub, cx.hb[kc]], [bb2])
                    tt(tmp[:, 0, :], b1[:, :NT], cosv, ALU.mult, [bb1, rpb], [tmpb[0]])
                    tt(tmp[:, 1, :], b2[:, :NT], sinv, ALU.mult, [bb2, rpb], [tmpb[1]])
                    tt(tmp[:, 2, :], b1[:, :NT], sinv, ALU.mult, [bb1, rpb], [tmpb[2]])
                    tt(tmp[:, 3, :], b2[:, :NT], cosv, ALU.mult, [bb2, rpb], [tmpb[3]])
                    if which == 0:
                        tt(tq[:, 0, :], tmp[:, 0, :], tmp[:, 1, :], ALU.subtract, [tmpb[0], tmpb[1]], [tqb[0]])
                        tt(tq[:, 1, :], tmp[:, 2, :], tmp[:, 3, :], ALU.add, [tmpb[2], tmpb[3]], [tqb[1]])
                        for a in range(2):
                            act(qT[p][:, a, :], tq[:, a, :], AF.Copy, [tqb[a]], [qTb[p][a]])
                            qdv = cf[:, qd_off + hd * 128:qd_off + (hd + 1) * 128].unsqueeze(1).broadcast_to([128, nch, 128])
                            tt(qdT[p][:, a, :].rearrange("p (c i) -> p c i", c=nch),
                               tq[:, a, :].rearrange("p (c i) -> p c i", c=nch), qdv, ALU.mult,
                               [tqb[a], cfb], [qdTb[p][a]])
                    else:
                        tt(kT[p][:, 0, :], tmp[:, 0, :], tmp[:, 1, :], ALU.subtract, [tmpb[0], tmpb[1]], [kTb[p][0]])
                        tt(kT[p][:, 1, :], tmp[:, 2, :], tmp[:, 3, :], ALU.add, [tmpb[2], tmpb[3]], [kTb[p][1]])
                    if R is not None:
                        b1, bb1, _ = pb()
                        b2, bb2, _ = pb()
                        for kc in range(KC):
                            mm(b1[:, :NS], wu[:, kc, base:base + 128], rcx.hT[:, kc, :NS], kc == 0, kc == KC - 1, [wub, rcx.hb[kc]], [bb1])
                        for kc in range(KC):
                            mm(b2[:, :NS], wu[:, kc, base + 128:base + 256], rcx.hT[:, kc, :NS], kc == 0, kc == KC - 1, [wub, rcx.hb[kc]], [bb2])
                        tt(tmp[:, 0, :NS], b1[:, :NS], r_cos, ALU.mult, [bb1, r_rpb], [tmpb[0]])
                        tt(tmp[:, 1, :NS], b2[:, :NS], r_sin, ALU.mult, [bb2, r_rpb], [tmpb[1]])
                        tt(tmp[:, 2, :NS], b1[:, :NS], r_sin, ALU.mult, [bb1, r_rpb], [tmpb[2]])
                        tt(tmp[:, 3, :NS], b2[:, :NS], r_cos, ALU.mult, [bb2, r_rpb], [tmpb[3]])
                        if which == 0:
                            tt(tq[:, 0, :NS], tmp[:, 0, :NS], tmp[:, 1, :NS], ALU.subtract, [tmpb[0], tmpb[1]], [tqb[0]])
                            tt(tq[:, 1, :NS], tmp[:, 2, :NS], tmp[:, 3, :NS], ALU.add, [tmpb[2], tmpb[3]], [tqb[1]])
                            for a in range(2):
                                act(R["qT"][hd][:, a, :], tq[:, a, :NS], AF.Copy, [tqb[a]], [R["qTb"][hd][a]])
                                tt(R["qdT"][hd][:, a, :], tq[:, a, :NS], cf[:, CF_QDS + hd * 128:CF_QDS + (hd + 1) * 128], ALU.mult,
                                   [tqb[a], cfb], [R["qdTb"][hd][a]])
                        else:
                            tt(R["kT"][hd][:, 0, :], tmp[:, 0, :NS], tmp[:, 1, :NS], ALU.subtract, [tmpb[0], tmpb[1]], [R["kTb"][hd][0]])
                            tt(R["kT"][hd][:, 1, :], tmp[:, 2, :NS], tmp[:, 3, :NS], ALU.add, [tmpb[2], tmpb[3]], [R["kTb"][hd][1]])
                return f

            def p_vg(hd, kind):
                def f():
                    p = hd % npar
                    wu_, wb_ = wget((kind, hd))
                    for c in range(nch):
                        bk, bb, _ = pb()
                        for kc in range(KC):
                            mm(bk[:, :], cx.hT[:, kc, c * 128:(c + 1) * 128], wu_[:, kc, :], kc == 0, kc == KC - 1, [wb_, cx.hb[kc]], [bb])
                        if kind == "v":
                            act(vv[p][:, c, :], bk[:, :], AF.Copy, [bb], [vb[p][c]])
                        else:
                            act(gg[p][:, c, :], bk[:, :], AF.Silu, [bb], [gb[p][c]])
                    if R is not None:
                        bk, bb, _ = pb()
                        for kc in range(KC):
                            mm(bk[:, :], rcx.hT[:, kc, 0:NS], wu_[:, kc, :], kc == 0, kc == KC - 1, [wb_, rcx.hb[kc]], [bb])
                        if kind == "v":
                            act(R["v"][hd][:, 0, :], bk[:, :], AF.Copy, [bb], [R["vb"][hd][0]])
                        else:
                            act(R["g"][hd][:, 0, :], bk[:, :], AF.Silu, [bb], [R["gb"][hd][0]])
                return f

            def p_kd(hd):
                def f():
                    p = hd % npar
                    for c in range(nch):
                        tb, tbb, _ = pb()
                        tbv = tb[:, :].bitcast(BF16)
                        for a in range(2):
                            tr(tbv[:, a * 128:(a + 1) * 128], kT[p][:, a, c * 128:(c + 1) * 128], ident_bf, [kTb[p][a], cbb], [tbb])
                        act(kd[p][:, c, :], tbv[:, 0:256], AF.Copy, [tbb, cfb], [kdb[p][c]],
                            scale=cf[:, kd_off + hd:kd_off + hd + 1])
                    if R is not None:
                        tb, tbb, _ = pb()
                        tbv = tb[:, :].bitcast(BF16)
                        for a in range(2):
                            tr(tbv[:, a * 128:(a + 1) * 128], R["kT"][hd][:, a, :], ident_bf, [R["kTb"][hd][a], cbb], [tbb])
                        act(R["kd"][hd][:, 0, :], tbv[:, 0:256], AF.Copy, [tbb, cfb], [R["kdb"][hd][0]],
                            scale=cf[:, CF_KDS + hd:CF_KDS + hd + 1])
                return f

            def proj_pieces(hd):
                if pre is not None:
                    return []
                return [p_qk(hd, 0), p_qk(hd, 1), p_vg(hd, "v"), p_vg(hd, "g"), p_kd(hd)]

            cst = {}

            def c_main(hd, c):
                p = hd % npar
                cd = float(_consts()[3][1 if sample else 0][hd])
                cs = slice(c * 128, (c + 1) * 128)
                if sample:
                    for a in range(2):
                        tt(Qz[:, a, :, :], qdT[p][:, a, :].unsqueeze(1).broadcast_to([128, NB, 128]),
                           cb[:, CB_BLK:CB_BLK + NB * 128].rearrange("p (b i) -> p b i", b=NB), ALU.mult,
                           [qdTb[p][a], cbb], [Qzb])
                        tt(KZ[:, a, :, :], kd[p][:, 0, a * 128:(a + 1) * 128].unsqueeze(1).broadcast_to([128, NB, 128]),
                           cf[:, CF_KZM:CF_KZM + NB].unsqueeze(2).broadcast_to([128, NB, 128]), ALU.mult,
                           [kdb[p][0], cfb], [KZb])
                sbk, sbb, _ = pb()
                for a in range(2):
                    mm(sbk[:, :128], kT[p][:, a, cs], qT[p][:, a, cs], a == 0, a == 1, [kTb[p][a], qTb[p][a]], [sbb])
                si = sidx[0] % 2
                sidx[0] += 1
                tt(sTm[:, si, :], sbk[:, :128], cf[:, mt_off + hd * 128:mt_off + (hd + 1) * 128], ALU.mult,
                   [sbb, cfb], [sTmb[si]])
                if not sample:
                    pbs = []
                    for a in range(2):
                        pk, pkb, _ = pb()
                        mm(pk[:, :], kd[p][:, c, a * 128:(a + 1) * 128], vv[p][:, c, :], True, True, [kdb[p][c], vb[p][c]], [pkb])
                        pbs.append((pk, pkb))
                    ob, obb, _ = pb()
                    mm(ob[:, :], sTm[:, si, :], vv[p][:, c, :], True, False, [sTmb[si], vb[p][c]], [obb])
                    for a in range(2):
                        mm(ob[:, :], qdT[p][:, a, cs], Sbf[:, hd, a, :], False, a == 1, [qdTb[p][a], Sbfb[hd][a]], [obb])
                    for a in range(2):
                        stt(Sst[:, hd, a, :], Sst[:, hd, a, :], cd, pbs[a][0][:, :], ALU.mult, ALU.add,
                            [Sb[hd][a], pbs[a][1]], [Sb[hd][a]])
                        act(Sbf[:, hd, a, :], Sst[:, hd, a, :], AF.Copy, [Sb[hd][a]], [Sbfb[hd][a]])
                else:
                    ob, obb, obi = pb(hold=True)
                    mm(ob[:, :], sTm[:, si, :], vv[p][:, c, :], True, False, [sTmb[si], vb[p][c]], [obb])
                    for b in range(NB):
                        u = hd * NB + b
                        ui = u % 4
                        u2 = u % 3
                        issue_s0(u)
                        issue_s0(u + 1)
                        issue_s0(u + 2)
                        issue_s0(u + 3)
                        for a in range(2):
                            act(S0bf[u2][:, a, :], S0[ui][:, a, :], AF.Copy, [S0b[ui]], [S0bfb[u2]])
                        for a in range(2):
                            mm(ob[:, :], Qz[:, a, b, :], S0bf[u2][:, a, :], False, (b == NB - 1 and a == 1), [Qzb, S0bfb[u2]], [obb])
                        for a in range(2):
                            pk, pkb, _ = pb()
                            mm(pk[:, :], KZ[:, a, b, :], vv[p][:, c, :], True, True, [KZb, vb[p][c]], [pkb])
                            stt(Sn[ui][:, a, :], S0[ui][:, a, :], cd, pk[:, :], ALU.mult, ALU.add, [S0b[ui], pkb], [Snb[ui]])
                        dma("pool", srs[b, hd].rearrange("(a p) e -> p a e", p=128), Sn[ui][:, :, :], [Snb[ui]], [], f"sn{ui}")
                    release(obi)
                gs = gidx[0] % 2
                gidx[0] += 1
                gn_gate(ob[:, :], obb, gg[p][:, c, :], gb[p][c], uu[:, gs, :], ub[gs], on[:, gs, :], onb[gs], (hd * nch + c) % 4)
                cst[(hd, c)] = gs

            def c_tail(hd, c):
                gs = cst[(hd, c)]
                cs = slice(c * 128, (c + 1) * 128)
                tb, tbb, _ = pb()
                tbv = tb[:, :].bitcast(BF16)
                for ec in range(4):
                    tr(tbv[:, ec * 128:(ec + 1) * 128], uu[:, gs, ec * 128:(ec + 1) * 128], ident_bf, [ub[gs], cbb], [tbb])
                act(uT[:, hd * 4:(hd + 1) * 4, cs], tbv[:, 0:512].rearrange("p (a b) -> p a b", a=4), AF.Copy, [tbb], [uTb[hd][c]])

            gidx = [0]
            for pc in proj_pieces(0):
                pc()
            for hd in range(RH):
                q_ = proj_pieces(hd + 1) if hd + 1 < RH else []
                for c in range(nch):
                    c_main(hd, c)
                    if q_:
                        q_.pop(0)()
                    if c > 0:
                        c_tail(hd, c - 1)
                while len(q_) > 1:
                    q_.pop(0)()
                c_tail(hd, nch - 1)
                while q_:
                    q_.pop(0)()
            for half in range(2):
                bs = [pb(hold=True) for _ in range(4)]
                for ecg in range(2):
                    wou, wob = wget(("wo", half, ecg))
                    for dm in range(4):
                        for ec in range(8):
                            e_ = ecg * 8 + ec
                            mm(bs[dm][0][:, :NT], wou[:, ec, dm * 128:(dm + 1) * 128], uT[:, e_, :NT], e_ == 0, e_ == 15,
                               [wob] + uTb[e_ // 4], [bs[dm][1]])
                for dm in range(4):
                    kc = half * 4 + dm
                    tt(cx.xT[:, kc, :NT], bs[dm][0][:, :NT], cx.xT[:, kc, :NT], ALU.add, [bs[dm][1], cx.xb[kc]], [cx.xb[kc]])
                    release(bs[dm][2])
            if (not sample) and tile_i == NPT - 1:
                dma("sp", srp.rearrange("h (a p) e -> p h a e", p=128), Sst[:, :, :, :],
                    [Sb[h][a] for h in range(RH) for a in range(2)], [], "srp")
            return R

        def layer1(cx, tile_i, sample):
            NT = cx.NT
            nblk = NT // 128
            A.reset()
            qT1 = A.alloc("qT1", [128, 8, NT], BF16)
            qT1b = [A.buf("qT1") for _ in range(8)]
            kvT = A.alloc("kvT", [128, 2, NT], F32)
            kvTb = [A.buf("kT1"), A.buf("vT1")]
            oT = A.alloc("oT", [128, 8, NT], BF16)
            oTb2 = [[A.buf("oT") for _ in range(nblk)] for _ in range(8)]
            ee = A.alloc("ee", [128, 8, 128], BF16)
            eeb = [A.buf("ee") for _ in range(8)]
            if sample:
                pT5 = A.alloc("pT", [128, 4, NB, 32], BF16)
                qS = A.alloc("qS", [128, 2, NB, 32], BF16)
                qSb = A.buf("qS")
            else:
                pT = A.alloc("pT", [128, 12, 128], BF16)
            pTb = [A.buf("pT") for _ in range(16 if sample else 12)]
            rec = A.alloc("rec", [128, 2, 512], F32)
            recb = [A.buf("rec0"), A.buf("rec1")]
            tok = A.alloc("tok", [128, 2, 128], F32)
            tokb = [A.buf("tok0"), A.buf("tok1")]
            if sample:
                Kc = [A.alloc(f"Kc{i}", [128, 2, 128], F32) for i in range(2)]
                Kcb = [A.buf("Kc0"), A.buf("Kc1")]
                Kcz = [A.alloc(f"Kcz{i}", [128, 4, 128], BF16) for i in range(2)]
                Kczb = [A.buf("Kcz0"), A.buf("Kcz1")]
                Vcz = [A.alloc(f"Vcz{i}", [128, 4, 128], BF16) for i in range(2)]
                Vczb = [A.buf("Vcz0"), A.buf("Vcz1")]
                kzc = [A.alloc(f"kzc{i}", [128, 4, 128], BF16) for i in range(2)]
                kzcb = [A.buf("kzc0"), A.buf("kzc1")]
                e32 = A.alloc("e32", [128, 4, 32], BF16)
                e32b = [A.buf("e32") for _ in range(4)]
                pTc = [A.alloc(f"pTc{i}", [128, 4, 32], BF16) for i in range(2)]
                pTcb = [A.buf("pTc0"), A.buf("pTc1")]

                def issue_kc(b):
                    if b < NB:
                        dma("sp", Kc[b % 2][:, 0, :], ck[b], [], [Kcb[b % 2]], f"kc{b % 2}")
                        dma("sp", Kc[b % 2][:, 1, :], cv[b], [], [Kcb[b % 2]], f"kc{b % 2}")
                issue_kc(0)
                issue_kc(1)
            for qu in range(2):
                wu, wub = wget(("q1", qu))
                for j in range(4):
                    hp = qu * 4 + j
                    bk, bb, _ = pb()
                    for kc in range(KC):
                        mm(bk[:, :NT], wu[:, kc, j * 128:(j + 1) * 128], cx.hT[:, kc, :NT], kc == 0, kc == KC - 1, [wub, cx.hb[kc]], [bb])
                    act(qT1[:, hp, :], bk[:, :NT], AF.Identity, [bb, spb], [qT1b[hp]], bias=sp[:, SP_BQ + hp:SP_BQ + hp + 1])
            wu, wub = wget(("kz",))
            for var in range(4):
                bk, bb, _ = pb()
                for kc in range(KC):
                    mm(bk[:, :NT], wu[:, kc, var * 128:(var + 1) * 128], cx.hT[:, kc, :NT], kc == 0, kc == KC - 1, [wub, cx.hb[kc]], [bb])
                act(kz[:, var, 128:128 + NT], bk[:, :NT], AF.Identity, [bb, spb], [kzb[1 + i] for i in range(nblk)],
                    bias=sp[:, SP_BKZ + var:SP_BKZ + var + 1])
            wu, wub = wget(("kv",))
            for j in range(2):
                bk, bb, _ = pb()
                for kc in range(KC):
                    mm(bk[:, :NT], wu[:, kc, j * 128:(j + 1) * 128], cx.hT[:, kc, :NT], kc == 0, kc == KC - 1, [wub, cx.hb[kc]], [bb])
                act(kvT[:, j, :], bk[:, :NT], AF.Identity, [bb, spb], [kvTb[j]], bias=sp[:, SP_BK + j:SP_BK + j + 1])
            for blk in range(nblk):
                cs = slice(blk * 128, (blk + 1) * 128)
                tb, tbb, _ = pb()
                tr(tb[:, 0:128], kvT[:, 1, cs], ident_f, [kvTb[1], cfb], [tbb])
                for var in range(4):
                    kvh, par = var // 2, var % 2
                    act(vz[:, 1 + blk, var, par * 64:(par + 1) * 64], tb[:, kvh * 64:(kvh + 1) * 64], AF.Copy, [tbb], [vzb[1 + blk]])
                last = sample or (tile_i == NPT - 1 and blk == nblk - 1)
                if last:
                    tt_i = 0
                    act(tok[:, 1, :], tb[:, 0:128], AF.Copy, [tbb], [tokb[1]])
                    tb2, tbb2, _ = pb()
                    tr(tb2[:, 0:128], kvT[:, 0, cs], ident_f, [kvTb[0], cfb], [tbb2])
                    act(tok[:, 0, :], tb2[:, 0:128], AF.Copy, [tbb2], [tokb[0]])
                    if sample:
                        for b in range(NB):
                            dma("sp", ks_o[b, 128 - DS:128, :], tok[b * DS:(b + 1) * DS, 0, :], [tokb[0]], [], "ko")
                            dma("sp", vs_o[b, 128 - DS:128, :], tok[b * DS:(b + 1) * DS, 1, :], [tokb[1]], [], "ko")
                        dma("sp", ks_o[:, 0:128 - DS, :], ck[:, DS:128, :], [], [], "ko")
                        dma("sp", vs_o[:, 0:128 - DS, :], cv[:, DS:128, :], [], [], "ko")
                    else:
                        dma("sp", kp_o[:, :], tok[:, 0, :], [tokb[0]], [], "ko")
                        dma("sp", vp_o[:, :], tok[:, 1, :], [tokb[1]], [], "ko")
            m_cur = cb[:, CB_MCUR:CB_MCUR + 128]
            m_prev = cb[:, CB_MPREV:CB_MPREV + 128]
            m_new = cb[:, CB_MNEW:CB_MNEW + 128]
            ei = [0]

            if not sample:
                its = [(blk, hp) for blk in range(nblk) for hp in range(8)]
                sres = {}

                def a_scores(i):
                    blk, hp = its[i]
                    gblk = tile_i * (TT // 128) + blk
                    qs = slice(blk * 128, (blk + 1) * 128)
                    kbs = [blk + 1] + ([blk] if gblk > 0 else [])
                    nk = len(kbs)
                    kvh = hp // 4
                    items = []
                    sbk, sbb, _ = pb()
                    col = 0
                    for par in range(2):
                        var = kvh * 2 + par
                        for kbi, kblk in enumerate(kbs):
                            pi = (i % 3) * 4 + par * 2 + kbi
                            mm(sbk[:, col * 128:(col + 1) * 128], kz[:, var, kblk * 128:(kblk + 1) * 128], qT1[:, hp, qs], True, True,
                               [kzb[kblk], qT1b[hp]], [sbb])
                            col += 1
                            items.append((par, var, kblk, pi))
                    w_ = 2 * nk * 128
                    e_i = i % 2
                    act(ee[:, e_i * 4:e_i * 4 + 2 * nk, :], sbk[:, :w_].rearrange("p (a c) -> p a c", c=128), AF.Exp, [sbb], [eeb[e_i]], scale=0.125)
                    base = (i % 3) * 4
                    dst = pT[:, base:base + 4, :].rearrange("p (a b) c -> p a b c", a=2)[:, :, 0:nk, :]
                    src = ee[:, e_i * 4:e_i * 4 + 2 * nk, :].rearrange("p (a b) c -> p a b c", a=2)
                    msk = cb[:, CB_MCUR:CB_MCUR + nk * 128].rearrange("p (b c) -> p b c", c=128).unsqueeze(1).broadcast_to([128, 2, nk, 128])
                    tt(dst, src, msk, ALU.mult, [eeb[e_i], cbb], [pTb[i % 3]])
                    sres[i] = items

                def a_nd(i):
                    blk, hp = its[i]
                    qs = slice(blk * 128, (blk + 1) * 128)
                    items = sres.pop(i)
                    nb_, nbb, _ = pb()
                    for n, (par, var, kblk, pi) in enumerate(items):
                        mm(nb_[:, :128], vz[:, kblk, var, :], pT[:, pi, :], n == 0, n == len(items) - 1, [vzb[kblk], pTb[i % 3]], [nbb])
                    db_, dbb, _ = pb()
                    for n, (par, var, kblk, pi) in enumerate(items):
                        mm(db_[:, :128], cb[:, CB_ONESZ + par * 128:CB_ONESZ + (par + 1) * 128], pT[:, pi, :], n == 0, n == len(items) - 1,
                           [cbb, pTb[i % 3]], [dbb])
                    ri = i % 2
                    act(rec[:, ri, :128], db_[:, :128], AF.Identity, [dbb, esinkb], [recb[ri]], bias=esink[:, hp:hp + 1])
                    recip(rec[:, ri, 128:256], rec[:, ri, :128], [recb[ri]], [recb[ri]])
                    tt(oT[:, hp, qs], nb_[:, :128], rec[:, ri, 128:256], ALU.mult, [nbb, recb[ri]], [oTb2[hp][blk]])

                a_scores(0)
                a_scores(1)
                for i in range(len(its)):
                    if i + 2 < len(its):
                        a_scores(i + 2)
                    a_nd(i)
                S.op("act", lambda e: e.activation(out=kz[:, :, 0:128], in_=kz[:, :, NT:NT + 128], func=AF.Copy), [kzb[nblk]], [kzb[0]])
                S.op("dve", lambda e: e.tensor_copy(out=vz[:, 0, :, :], in_=vz[:, nblk, :, :]), [vzb[nblk]], [vzb[0]])
            else:
                qs = slice(0, 128)
                for gi_ in range(4):
                    par, kvh = gi_ // 2, gi_ % 2
                    var = kvh * 2 + par
                    sbk, sbb, _ = pb()
                    for m in range(4):
                        hp = kvh * 4 + m
                        mm(sbk[:, m * 128:(m + 1) * 128], kz[:, var, 128:256], qT1[:, hp, qs], True, True, [kzb[1], qT1b[hp]], [sbb])
                    e_i = gi_ % 2
                    act(ee[:, e_i * 4:e_i * 4 + 4, :], sbk[:, :].rearrange("p (a c) -> p a c", c=128), AF.Exp, [sbb], [eeb[e_i]], scale=0.125)
                    tt(pT5[:, par * 2 + kvh, :, :].rearrange("p b (m i) -> p b m i", m=4),
                       ee[:, e_i * 4:e_i * 4 + 4, :].rearrange("p m (b i) -> p b m i", b=NB),
                       m_new.rearrange("p (b i) -> p b i", b=NB).unsqueeze(2).broadcast_to([128, NB, 4, DS]), ALU.mult,
                       [eeb[e_i], cbb], [pTb[gi_]])
                for kvh in range(2):
                    S.op("dve", (lambda o_, i_: (lambda e: e.tensor_copy(out=o_, in_=i_)))(
                        qS[:, kvh, :, :].rearrange("p b (m i) -> p b m i", m=4),
                        qT1[:, kvh * 4:(kvh + 1) * 4, :].rearrange("p m (b i) -> p b m i", b=NB)),
                        qT1b[kvh * 4:(kvh + 1) * 4], [qSb])
                nbk = [pb(hold=True) for _ in range(2)]
                dbk = [pb(hold=True) for _ in range(2)]
                m_cache = cb[:, CB_MCACHE:CB_MCACHE + 32].rearrange("p (m i) -> p m i", m=4)
                memset("dve", Kcz[0][:, :, :], 0.0, [Kczb[0]])
                memset("dve", Kcz[1][:, :, :], 0.0, [Kczb[1]])
                memset("dve", Vcz[0][:, :, :], 0.0, [Vczb[0]])
                memset("dve", Vcz[1][:, :, :], 0.0, [Vczb[1]])

                def s1(b):
                    ui = b % 2
                    for par in range(2):
                        kdst = Kcz[ui][:, par:4:2, par * 64:(par + 1) * 64] if False else \
                            Kcz[ui][:, :, :].rearrange("p (k r) c -> p k r c", k=2)[:, :, par, par * 64:(par + 1) * 64]
                        vdst = Vcz[ui][:, :, :].rearrange("p (k r) c -> p k r c", k=2)[:, :, par, par * 64:(par + 1) * 64]
                        ksrc = Kc[ui][:, 0, :].rearrange("p (k d) -> p k d", k=2)
                        vsrc = Kc[ui][:, 1, :].rearrange("p (k d) -> p k d", k=2)
                        S.op("dve", (lambda o_, i_: (lambda e: e.tensor_copy(out=o_, in_=i_)))(kdst, ksrc), [Kcb[ui]], [Kczb[ui]])
                        act(vdst, vsrc, AF.Copy, [Kcb[ui]], [Vczb[ui]])
                    tb, tbb, _ = pb()
                    tbv = tb[:, :].bitcast(BF16)
                    for var in range(4):
                        tr(tbv[:, var * 128:(var + 1) * 128], Kcz[ui][:, var, :], ident_bf, [Kczb[ui], cbb], [tbb])
                    act(kzc[ui][:, :, :], tbv[:, 0:512].rearrange("p (a b) -> p a b", a=4), AF.Copy, [tbb], [kzcb[ui]])

                def s2(b):
                    ui = b % 2
                    sbk, sbb, _ = pb()
                    for var in range(4):
                        kvh, par = var // 2, var % 2
                        mm(sbk[:, var * 32:(var + 1) * 32], kzc[ui][:, var, :], qS[:, kvh, b, :], True, True, [kzcb[ui], qSb], [sbb])
                    act(e32[:, :, :], sbk[:, 0:128].rearrange("p (v c) -> p v c", v=4), AF.Exp, [sbb], [e32b[0]], scale=0.125)
                    tt(pTc[ui][:, :, :].rearrange("p v (m i) -> p v m i", m=4), e32[:, :, :].rearrange("p v (m i) -> p v m i", m=4),
                       m_cache.unsqueeze(1).broadcast_to([128, 4, 4, DS]), ALU.mult, [e32b[0], cbb], [pTcb[ui]])

                def s3(b):
                    ui = b % 2
                    for kvh in range(2):
                        for (bk3, lhs_kind) in ((nbk[kvh], "v"), (dbk[kvh], "o")):
                            outv = bk3[0][:, b * 32:(b + 1) * 32]
                            n = 0
                            for par in range(2):
                                var = kvh * 2 + par
                                lhs = Vcz[ui][:, var, :] if lhs_kind == "v" else cb[:, CB_ONESZ + par * 128:CB_ONESZ + (par + 1) * 128]
                                rds = [Vczb[ui] if lhs_kind == "v" else cbb, pTcb[ui]]
                                mm(outv, lhs, pTc[ui][:, var, :], n == 0, False, rds, [bk3[1]])
                                n += 1
                            for par in range(2):
                                var = kvh * 2 + par
                                lhs = vz[:, 1, var, :] if lhs_kind == "v" else cb[:, CB_ONESZ + par * 128:CB_ONESZ + (par + 1) * 128]
                                rhs = pT5[:, par * 2 + kvh, b, :]
                                rds = [vzb[1] if lhs_kind == "v" else cbb, pTb[par * 2 + kvh]]
                                mm(outv, lhs, rhs, False, par == 1, rds, [bk3[1]])

                s1(0)
                for b in range(NB):
                    s2(b)
                    if b >= 1:
                        s3(b - 1)
                    issue_kc(b + 2)
                    if b + 1 < NB:
                        s1(b + 1)
                s3(NB - 1)
                for kvh in range(2):
                    dv = dbk[kvh][0][:, :].rearrange("p (b m i) -> p b m i", b=NB, m=4)
                    nv = nbk[kvh][0][:, :].rearrange("p (b m i) -> p b m i", b=NB, m=4)
                    r0 = rec[:, 0, :].rearrange("p (b m i) -> p b m i", b=NB, m=4)
                    r1 = rec[:, 1, :].rearrange("p (b m i) -> p b m i", b=NB, m=4)
                    for m in range(4):
                        hp = kvh * 4 + m
                        ts(r0[:, :, m, :], dv[:, :, m, :], esink[:, hp:hp + 1], None, ALU.add, ALU.bypass,
                           [dbk[kvh][1], esinkb], [recb[0]])
                    recip(rec[:, 1, :], rec[:, 0, :], [recb[0]], [recb[1]])
                    for m in range(4):
                        hp = kvh * 4 + m
                        tt(oT[:, hp, :].rearrange("p (b i) -> p b i", b=NB), nv[:, :, m, :], r1[:, :, m, :], ALU.mult,
                           [nbk[kvh][1], recb[1]], oTb2[hp])
                    release(nbk[kvh][2])
                    release(dbk[kvh][2])
            for half in range(2):
                wu, wub = wget(("wo1", half))
                for dm in range(4):
                    kc = half * 4 + dm
                    bk, bb, _ = pb()
                    for hp in range(8):
                        mm(bk[:, :NT], wu[:, hp, dm * 128:(dm + 1) * 128], oT[:, hp, :], hp == 0, hp == 7, [wub] + oTb2[hp], [bb])
                    stt(cx.xT[:, kc, :NT], bk[:, :NT], sp[:, SP_BO + kc:SP_BO + kc + 1], cx.xT[:, kc, :NT], ALU.add, ALU.add,
                        [bb, spb, cx.xb[kc]], [cx.xb[kc]])

        xin_p = xTp.rearrange("(kc p) t -> p kc t", p=128)
        yout_p = yTp.rearrange("(kc p) t -> p kc t", p=128)
        xin_s = xTs.rearrange("(kc p) t -> p kc t", p=128)
        yout_s = yTs.rearrange("(kc p) t -> p kc t", p=128)

        def load_rope(i, c0, n):
            dma("sp", rope[i][:, :, :n], rope_d[:, :, c0:c0 + n].rearrange("c p t -> p c t"), [], [ropeb[i]], f"rope{i}")

        dma("sp", pcx[0].xT[:, :, :], xin_p[:, :, 0:TT], [], pcx[0].xb, "xin0")
        load_rope(0, 0, TT)
        dma("sp", scx.xT[:, :, :], xin_s[:, :, :], [], scx.xb, "xins")
        for t in range(NPT):
            cx = pcx[t % 2]
            last = t == NPT - 1
            if not last:
                load_rope((t + 1) % 2, (t + 1) * TT, TT)
                dma("sp", pcx[(t + 1) % 2].xT[:, :, :], xin_p[:, :, (t + 1) * TT:(t + 2) * TT], [], pcx[(t + 1) % 2].xb, f"xin{(t + 1) % 2}")
            cxs = [cx, scx] if last else [cx]

            def dbg(i):
                if DEBUG and t == DEBUG_TILE:
                    dma("sp", dbg_o[i].rearrange("(kc p) t -> p kc t", p=128)[:, :, :TT], cx.xT[:, :, :TT], cx.xb, [], "dbg")
            rmsnorm(cx, 0)
            if last:
                load_rope((t + 1) % 2, SEQ, NS)
                rmsnorm(scx, 0)
                idle = pcx[(t + 1) % 2]
                R_ = layer0(cx, t, False, t % 2, rider=(scx, (t + 1) % 2, idle.xT, idle.xb))
                layer0(scx, NPT, True, (t + 1) % 2, pre=R_)
            else:
                layer0(cx, t, False, t % 2)
            dbg(0)
            for c_ in cxs:
                rmsnorm(c_, 1)
            ffn(cxs, 0)
            dbg(1)
            rmsnorm(cx, 2)
            layer1(cx, t, False)
            if last:
                rmsnorm(scx, 2)
                layer1(scx, NPT, True)
            dbg(2)
            for c_ in cxs:
                rmsnorm(c_, 3)
            ffn(cxs, 1)
            dbg(3)
            rmsnorm(cx, 4, to_x=True)
            dma("sp", yout_p[:, :, t * TT:(t + 1) * TT], cx.xT[:, :, :], cx.xb, [], "yout")
            if last:
                rmsnorm(scx, 4, to_x=True)
                dma("sp", yout_s[:, :, :], scx.xT[:, :, :], scx.xb, [], "yout")
        outs = Buf("outs")
        for name in list(S.dcount):
            if name in ("yout", "srp", "ko", "dbg") or name.startswith("sn"):
                outs.rs[("d", name)] = S.dcount[name]
        S.op("sp", lambda e: e.nop(), (), [outs])
        assert wstate["k"] == len(plan)
        S.emit(nc, es)
    return nc


_CACHE = {}


def _host_layout(inp):
    f = np.float32
    g = lambda k: np.ascontiguousarray(np.asarray(inp[k], dtype=f))
    x_prompt, x_sample = g("x_prompt"), g("x_sample")
    state_ret, ck, cv = g("state_ret")[0], g("cache_swa_k")[0], g("cache_swa_v")[0]
    wq, wk = g("ret_w_q")[0], g("ret_w_k")[0]
    wqk = np.concatenate([np.concatenate([wq[:, h * 256:(h + 1) * 256], wk[:, h * 256:(h + 1) * 256]], axis=1) for h in range(RH)], axis=1)
    wqkv = g("swa_w_qkv")[0]
    bqkv = g("swa_b_qkv")[0]
    wq1 = wqkv[:, :1024]
    wk1 = wqkv[:, 1024:1152]
    wv1 = wqkv[:, 1152:1280]
    wkz = np.zeros((D, 4, 128), f)
    bkz = np.zeros((4, 128), f)
    for kvh in range(2):
        for par in range(2):
            wkz[:, kvh * 2 + par, par * 64:(par + 1) * 64] = wk1[:, kvh * 64:(kvh + 1) * 64]
            bkz[kvh * 2 + par, par * 64:(par + 1) * 64] = bqkv[1024 + kvh * 64:1024 + (kvh + 1) * 64]
    wkz = wkz.reshape(D, 512)
    wkv = np.concatenate([wk1, wv1], axis=1)
    spar = np.zeros((128, SP_N), f)
    gains = [g("norm_mix")[0], g("norm_ffn")[0], g("norm_mix")[1], g("norm_ffn")[1], g("norm_final")]
    for i, v in enumerate(gains):
        spar[:, SP_G + i * 8:SP_G + (i + 1) * 8] = v.reshape(8, 128).T
    spar[:, SP_BQ:SP_BQ + 8] = bqkv[:1024].reshape(8, 128).T
    spar[:, SP_BKZ:SP_BKZ + 4] = bkz.T
    spar[:, SP_BK] = bqkv[1024:1152]
    spar[:, SP_BV] = bqkv[1152:1280]
    spar[:, SP_BO:SP_BO + 8] = g("swa_b_o")[0].reshape(8, 128).T
    sinks = g("swa_sinks")[0]
    spar[:, SP_SINK:SP_SINK + 8] = np.repeat(sinks.reshape(8, 2), 64, axis=1).T
    cf32, cb, rope, _ = _consts()
    shared = {
        "wqk": np.ascontiguousarray(wqk), "wv": g("ret_w_v")[0], "wg": g("ret_w_g")[0], "wo": g("ret_w_o")[0],
        "wq1": np.ascontiguousarray(wq1), "wkz": wkz, "wkv": np.ascontiguousarray(wkv), "wo1": g("swa_w_o")[0],
        "w1": g("ffn_w1"), "w3": g("ffn_w3"), "w2": g("ffn_w2"), "spar": spar, "cf32": cf32, "cb": cb, "rope": rope,
    }
    in_maps = []
    for c in range(NCORES):
        m = dict(shared)
        m["xTp"] = np.ascontiguousarray(x_prompt[c].T)
        m["xTs"] = np.ascontiguousarray(x_sample[c * NB:(c + 1) * NB].reshape(NS, D).T)
        m["st_in"] = np.ascontiguousarray(state_ret[c * NB:(c + 1) * NB])
        m["ck"] = np.ascontiguousarray(ck[c * NB:(c + 1) * NB].reshape(NB, 128, 128))
        m["cv"] = np.ascontiguousarray(cv[c * NB:(c + 1) * NB].reshape(NB, 128, 128))
        in_maps.append(m)
    return in_maps


def kernel(**inputs):
    if "nc" not in _CACHE:
        _CACHE["nc"] = build_program()
    nc = _CACHE["nc"]
    in_maps = _host_layout(inputs)
    res = run_bass_kernel_spmd(nc, in_maps, core_ids=list(range(NCORES)))
    R = res.results
    f = np.float32
    y_prompt = np.stack([R[c]["yTp"].T for c in range(NCORES)]).astype(f)
    y_sample = np.concatenate([R[c]["yTs"].T.reshape(NB, DS, D) for c in range(NCORES)]).astype(f)
    srp = np.stack([R[c]["srp"] for c in range(NCORES)])[None].astype(f)
    srs = np.concatenate([R[c]["srs"] for c in range(NCORES)])[None].astype(f)
    kp = np.stack([R[c]["kp"].reshape(128, 2, 64) for c in range(NCORES)])[None].astype(f)
    vp = np.stack([R[c]["vp"].reshape(128, 2, 64) for c in range(NCORES)])[None].astype(f)
    ks = np.concatenate([R[c]["ks"].reshape(NB, 128, 2, 64) for c in range(NCORES)])[None].astype(f)
    vs = np.concatenate([R[c]["vs"].reshape(NB, 128, 2, 64) for c in range(NCORES)])[None].astype(f)
    return (y_prompt, y_sample, srp, srs, kp, vp, ks, vs)
```

```python
import contextlib
import numpy as np
import concourse.bass as bass
import concourse.mybir as mybir
from concourse.bass_utils import run_bass_kernel_spmd

F32 = mybir.dt.float32
BF16 = mybir.dt.bfloat16
AF = mybir.ActivationFunctionType
ALU = mybir.AluOpType

NCORES = 8
D = 1024
KC = 8
SEQ = 2048
TT = 512
NPT = SEQ // TT
NS = 128
NB = 16
DS = 8
PAST = 16384
DFF = 2816
NF = DFF // 128
RH = 4
EPS = 1e-6
WSLOT = 4096
NW = 4
DEBUG = False
DEBUG_TILE = 0


class Buf:
    __slots__ = ("name", "excl", "const", "w", "rs")

    def __init__(self, name, excl=False, const=False, pending=None):
        self.name = name
        self.excl = excl
        self.const = const
        self.w = None
        self.rs = dict(pending) if pending else {}


class Sched:
    ENG = ("pe", "act", "dve", "pool", "sp")

    def __init__(self):
        self.ops = []
        self.cnt = {e: 0 for e in self.ENG}
        self.dcount = {}
        self.sig = set()

    def op(self, eng, fn, r=(), w=(), dsem=None):
        self.cnt[eng] += 1
        idx = self.cnt[eng]
        deps = {}

        def need(key, val):
            if deps.get(key, -1) < val:
                deps[key] = val

        for b in r:
            if b.w is not None:
                need(*b.w)
            if b.excl:
                for k, v in b.rs.items():
                    if k != ("e", eng):
                        need(k, v)
        for b in w:
            if b.w is not None:
                need(*b.w)
            for k, v in b.rs.items():
                need(k, v)
        if dsem is not None:
            self.dcount[dsem] = self.dcount.get(dsem, 0) + 16
            ev = (("d", dsem), self.dcount[dsem])
        else:
            ev = (("e", eng), idx)
        waits = []
        for key, val in deps.items():
            if key[0] == "e":
                if key[1] == "pe" and eng == "pe":
                    continue
                if key == ev[0] and val >= idx:
                    continue
                waits.append((key, val))
                self.sig.add((key[1], val))
            else:
                v = self.dcount[key[1]]
                if key == ev[0]:
                    v -= 16
                if v > 0:
                    waits.append((key, v))
        self.ops.append((eng, idx, fn, waits, dsem))
        for b in w:
            b.w = ev
            b.rs = {}
        for b in r:
            if b.const or b in w:
                continue
            b.rs[ev[0]] = ev[1]

    def emit(self, nc, es):
        handles = {"pe": nc.tensor, "act": nc.scalar, "dve": nc.vector, "pool": nc.gpsimd, "sp": nc.sync}
        sems = {e: es.enter_context(nc.semaphore("s_" + e)) for e in self.ENG}
        dsems = {n: es.enter_context(nc.semaphore("d_" + n)) for n in self.dcount}
        rank = {}
        for e in self.ENG:
            ids = sorted(i for (en, i) in self.sig if en == e)
            for k, i in enumerate(ids):
                rank[(e, i)] = k + 1
        seen = {e: {} for e in self.ENG}
        for (eng, idx, fn, waits, dsem) in self.ops:
            h = handles[eng]
            for key, val in waits:
                if key[0] == "e":
                    v = rank[(key[1], val)]
                    sem = sems[key[1]]
                else:
                    v = val
                    sem = dsems[key[1]]
                if seen[eng].get(key, 0) >= v:
                    continue
                seen[eng][key] = v
                h.wait_ge(sem, v)
            ins = fn(h)
            if dsem is not None:
                ins.then_inc(dsems[dsem], 16)
            elif (eng, idx) in self.sig:
                ins.then_inc(sems[eng], 1)


def _gammas():
    h = np.arange(RH, dtype=np.float64)
    return 1.0 - np.exp2(-(5.0 + h))


def _ref_decay(chunk):
    try:
        import jax
        import jax.numpy as jnp
        with jax.default_device(jax.devices("cpu")[0]):
            h = RH
            log_gamma = jnp.log1p(-jnp.exp2(-(5.0 + jnp.arange(h, dtype=jnp.float32))))
            idx = jnp.arange(chunk, dtype=jnp.float32)
            diff = idx[:, None] - idx[None, :]
            intra = jnp.where(diff >= 0, jnp.exp(log_gamma[:, None, None] * jnp.maximum(diff, 0.0)), 0.0)
            q_decay = jnp.exp(log_gamma[None, :] * (idx[:, None] + 1.0))
            k_decay = jnp.exp(log_gamma[None, :] * (chunk - 1.0 - idx[:, None]))
            chunk_decay = jnp.exp(log_gamma * chunk)
            return (np.asarray(intra, np.float32), np.asarray(q_decay, np.float32),
                    np.asarray(k_decay, np.float32), np.asarray(chunk_decay, np.float32))
    except Exception:
        f = np.float32
        log_gamma = np.log1p(-np.exp2(-(f(5.0) + np.arange(RH, dtype=f)))).astype(f)
        idx = np.arange(chunk, dtype=f)
        diff = idx[:, None] - idx[None, :]
        intra = np.where(diff >= 0, np.exp(log_gamma[:, None, None] * np.maximum(diff, f(0.0))), f(0.0)).astype(f)
        q_decay = np.exp(log_gamma[None, :] * (idx[:, None] + f(1.0))).astype(f)
        k_decay = np.exp(log_gamma[None, :] * (f(chunk) - f(1.0) - idx[:, None])).astype(f)
        chunk_decay = np.exp(log_gamma * f(chunk)).astype(f)
        return intra, q_decay, k_decay, chunk_decay


def _ref_rope():
    try:
        import jax
        import jax.numpy as jnp
        with jax.default_device(jax.devices("cpu")[0]):
            half = 128
            inv = 1.0 / (10000.0 ** (jnp.arange(half, dtype=jnp.float32) / half))
            pos = jnp.concatenate([jnp.arange(SEQ), PAST + (jnp.arange(NS) % DS)])
            ang = pos.astype(jnp.float32)[:, None] * inv[None, :]
            return np.asarray(jnp.cos(ang), np.float32).T, np.asarray(jnp.sin(ang), np.float32).T
    except Exception:
        f = np.float32
        half = 128
        inv = (f(1.0) / (f(10000.0) ** (np.arange(half, dtype=f) / f(half)))).astype(f)
        pos = np.concatenate([np.arange(SEQ), PAST + (np.arange(NS) % DS)]).astype(f)
        ang = (pos[:, None] * inv[None, :]).astype(f).astype(np.float64)
        return np.cos(ang).astype(f).T, np.sin(ang).astype(f).T


_CONST_CACHE = {}


def _consts():
    if "c" in _CONST_CACHE:
        return _CONST_CACHE["c"]
    i = np.arange(128)
    intra_p, qdec_p, kdec_p, cd_p = _ref_decay(128)
    intra_s, qdec_s, kdec_s, cd_s = _ref_decay(DS)
    sixteenth = np.float32(1.0 / 16.0)
    mt = np.zeros((128, RH, 128), np.float32)
    mts = np.zeros((128, RH, 128), np.float32)
    same = (i[:, None] // DS) == (i[None, :] // DS)
    difs = (i[None, :] % DS) - (i[:, None] % DS)
    for h in range(RH):
        mt[:, h, :] = intra_p[h].T * sixteenth
        blk = intra_s[h][(i[None, :] % DS), (i[:, None] % DS)]
        mts[:, h, :] = np.where(same, blk, 0.0) * sixteenth
    qd = np.zeros((128, RH, 128), np.float32)
    qds = np.zeros((128, RH, 128), np.float32)
    kd = np.zeros((128, RH), np.float32)
    kds = np.zeros((128, RH), np.float32)
    for h in range(RH):
        qd[:, h, :] = qdec_p[:, h][None, :]
        qds[:, h, :] = qdec_s[i % DS, h][None, :]
        kd[:, h] = kdec_p[:, h] * sixteenth
        kds[:, h] = kdec_s[i % DS, h] * sixteenth
    kzm = ((i[:, None] // DS) == np.arange(NB)[None, :]).astype(np.float32)
    ident = np.eye(128, dtype=np.float32)
    cf32 = np.concatenate([mt.reshape(128, -1), mts.reshape(128, -1), qd.reshape(128, -1), qds.reshape(128, -1),
                           kd, kds, kzm, ident], axis=1).astype(np.float32)
    ones = np.ones((128, 128))
    onesz = np.zeros((128, 2, 128))
    onesz[:, 0, :64] = 1.0
    onesz[:, 1, 64:] = 1.0
    m_cur = (i[:, None] <= i[None, :]).astype(np.float64)
    m_prev = (i[:, None] > i[None, :]).astype(np.float64)
    m_new = (same & (difs >= 0)).astype(np.float64)
    m_cache = np.zeros((128, 4, DS))
    for t in range(DS):
        m_cache[:, :, t] = (i > t)[:, None]
    blockmask = np.zeros((128, NB, 128))
    for b in range(NB):
        blockmask[:, b, b * DS:(b + 1) * DS] = 1.0
    cb = np.concatenate([ident, ones, onesz.reshape(128, -1), m_cur, m_prev, m_new, m_cache.reshape(128, -1),
                         blockmask.reshape(128, -1)], axis=1).astype(np.float32)
    cosT, sinT = _ref_rope()
    rope = np.stack([cosT, sinT]).astype(np.float32)
    _CONST_CACHE["c"] = (cf32, cb, rope, (cd_p, cd_s))
    return _CONST_CACHE["c"]


CF_MT = 0
CF_MTS = 512
CF_QD = 1024
CF_QDS = 1536
CF_KD = 2048
CF_KDS = 2052
CF_KZM = 2056
CF_ID = 2072
CF_N = 2200
CB_ID = 0
CB_ONES = 128
CB_ONESZ = 256
CB_MCUR = 512
CB_MPREV = 640
CB_MNEW = 768
CB_MCACHE = 896
CB_BLK = 928
CB_N = 928 + 2048
SP_G = 0
SP_BQ = 40
SP_BKZ = 48
SP_BK = 52
SP_BV = 53
SP_BO = 54
SP_SINK = 62
SP_N = 70


def build_program():
    nc = bass.Bass("TRN2", target_bir_lowering=False)
    S = Sched()

    def din(name, shape):
        return nc.dram_tensor(name, list(shape), F32, kind="ExternalInput").ap()

    def dout(name, shape):
        return nc.dram_tensor(name, list(shape), F32, kind="ExternalOutput").ap()

    xTp = din("xTp", [D, SEQ])
    xTs = din("xTs", [D, NS])
    st_in = din("st_in", [NB, RH, 256, 512])
    ck = din("ck", [NB, 128, 128])
    cv = din("cv", [NB, 128, 128])
    wqk = din("wqk", [D, RH * 512])
    wv = din("wv", [D, 2048])
    wg = din("wg", [D, 2048])
    wo = din("wo", [2048, D])
    wq1 = din("wq1", [D, 1024])
    wkz = din("wkz", [D, 512])
    wkv = din("wkv", [D, 256])
    wo1 = din("wo1", [D, D])
    w1 = din("w1", [2, D, DFF])
    w3 = din("w3", [2, D, DFF])
    w2 = din("w2", [2, DFF, D])
    spar = din("spar", [128, SP_N])
    cf_d = din("cf32", [128, CF_N])
    cb_d = din("cb", [128, CB_N])
    rope_d = din("rope", [2, 128, SEQ + NS])

    yTp = dout("yTp", [D, SEQ])
    yTs = dout("yTs", [D, NS])
    srp = dout("srp", [RH, 256, 512])
    srs = dout("srs", [NB, RH, 256, 512])
    kp_o = dout("kp", [128, 128])
    vp_o = dout("vp", [128, 128])
    ks_o = dout("ks", [NB, 128, 128])
    vs_o = dout("vs", [NB, 128, 128])
    if DEBUG:
        dbg_o = dout("dbg", [4, D, TT])

    es = contextlib.ExitStack()
    with es:
        def sb(name, shape, dt):
            return es.enter_context(nc.sbuf_tensor(name, list(shape), dt))

        class Ctx:
            pass

        hT_p = sb("hT", [128, KC, TT], BF16)
        hb_p = [Buf(f"h{k}") for k in range(KC)]
        pcx = []
        for i in range(2):
            c_ = Ctx()
            c_.xT = sb(f"xT{i}", [128, KC, TT], F32)
            c_.xb = [Buf(f"x{i}_{k}") for k in range(KC)]
            c_.hT = hT_p
            c_.hb = hb_p
            c_.NT = TT
            pcx.append(c_)
        scx = Ctx()
        scx.xT = sb("xTs_sb", [128, KC, NS], F32)
        scx.xb = [Buf(f"xs{k}") for k in range(KC)]
        scx.hT = sb("hTs_sb", [128, KC, NS], BF16)
        scx.hb = [Buf(f"hs{k}") for k in range(KC)]
        scx.NT = NS
        sq = sb("sq", [128, 2, TT], BF16)
        sqb = [Buf("sq0"), Buf("sq1")]
        rtmp = sb("rtmp", [128, TT], F32)
        rtmpb = Buf("rtmp")
        rstd = sb("rstd", [128, TT], F32)
        rstdb = Buf("rstd")
        wring = [sb(f"wr{i}", [128, WSLOT], BF16) for i in range(NW)]
        wrb = [Buf(f"wr{i}") for i in range(NW)]
        rope = [sb(f"rope{i}", [128, 2, TT], F32) for i in range(2)]
        ropeb = [Buf("rope0"), Buf("rope1")]
        cf = sb("cf", [128, CF_N], F32)
        cfb = Buf("cf", const=True)
        cb = sb("cbt", [128, CB_N], BF16)
        cbb = Buf("cb", const=True)
        sp = sb("sp", [128, SP_N], F32)
        spb = Buf("sp", const=True)
        esink = sb("esink", [128, 8], F32)
        esinkb = Buf("esink", const=True)
        Sst = sb("Sst", [128, RH, 2, 512], F32)
        Sbf = sb("Sbf", [128, RH, 2, 512], BF16)
        Sb = [[Buf(f"S{h}{a}") for a in range(2)] for h in range(RH)]
        Sbfb = [[Buf(f"Sbf{h}{a}") for a in range(2)] for h in range(RH)]
        ARENA_COLS = 34112
        arena = sb("arena", [128, ARENA_COLS], BF16)
        stat = sb("stat", [128, 4, 16], F32)
        statb = [Buf(f"stat{i}") for i in range(4)]
        kz = sb("kz", [128, 4, 128 + TT], BF16)
        kzb = [Buf(f"kz{i}") for i in range(5)]
        vz = sb("vz", [128, 5, 4, 128], BF16)
        vzb = [Buf(f"vz{i}") for i in range(5)]

        banks = [es.enter_context(nc.psum_tensor(f"pb{i}", [128, 512], F32)) for i in range(8)]
        bankb = [Buf(f"bank{i}", excl=True) for i in range(8)]
        held = [False] * 8
        rr = [0]

        def pb(hold=False):
            for _ in range(16):
                i = rr[0] % 8
                rr[0] += 1
                if not held[i]:
                    if hold:
                        held[i] = True
                    return banks[i], bankb[i], i
            raise RuntimeError("no psum bank")

        def release(i):
            held[i] = False

        class Arena:
            def __init__(self):
                self.off = 0
                self.cur = []
                self.pending = {}

            def reset(self):
                for b in self.cur:
                    if b.w is not None:
                        k, v = b.w
                        if self.pending.get(k, -1) < v:
                            self.pending[k] = v
                    for k, v in b.rs.items():
                        if self.pending.get(k, -1) < v:
                            self.pending[k] = v
                self.cur = []
                self.off = 0

            def buf(self, name):
                b = Buf(name, pending=self.pending)
                self.cur.append(b)
                return b

            def alloc(self, name, shape, dt):
                n = 1
                for s in shape[1:]:
                    n *= s
                nb = n * (2 if dt == F32 else 1)
                nb = (nb + 15) // 16 * 16
                assert self.off + nb <= ARENA_COLS, (name, self.off, nb)
                v = arena[:, self.off:self.off + nb]
                self.off += nb
                if dt == F32:
                    v = v.bitcast(F32)[:, :n]
                else:
                    v = v[:, :n]
                if len(shape) == 3:
                    v = v.rearrange("p (a b) -> p a b", a=shape[1])
                elif len(shape) == 4:
                    v = v.rearrange("p (a b c) -> p a b c", a=shape[1], b=shape[2])
                return v

        A = Arena()

        def mm(out, lhsT, rhs, start, stop, r, w):
            S.op("pe", lambda e: e.matmul(out, lhsT, rhs, start=start, stop=stop), r, w)

        def tr(out, in_, ident, r, w):
            S.op("pe", lambda e: e.transpose(out, in_, ident), r, w)

        def act(out, in_, func, r, w, bias=0.0, scale=1.0):
            S.op("act", lambda e: e.activation(out=out, in_=in_, func=func, bias=bias, scale=scale), r, w)

        def tt(out, in0, in1, op, r, w, eng="dve"):
            S.op(eng, lambda e: e.tensor_tensor(out=out, in0=in0, in1=in1, op=op), r, w)

        def stt(out, in0, scalar, in1, op0, op1, r, w):
            S.op("dve", lambda e: e.scalar_tensor_tensor(out=out, in0=in0, scalar=scalar, in1=in1, op0=op0, op1=op1), r, w)

        def ts(out, in0, s1, s2, op0, op1, r, w):
            S.op("dve", lambda e: e.tensor_scalar(out=out, in0=in0, scalar1=s1, scalar2=s2, op0=op0, op1=op1), r, w)

        def recip(out, in_, r, w):
            S.op("dve", lambda e: e.reciprocal(out=out, in_=in_), r, w)

        def dma(eng, out, in_, r, w, sem):
            S.op(eng, lambda e: e.dma_start(out=out, in_=in_), r, w, dsem=sem)

        def memset(eng, ap, val, w):
            S.op(eng, lambda e: e.memset(ap, val), (), w)

        def wview(ap2d, ncols):
            return ap2d.rearrange("(kc p) n -> p kc n", p=128), ncols

        units = {}
        wqk_v = wqk.rearrange("(kc p) n -> p kc n", p=128)
        wv_v = wv.rearrange("(kc p) n -> p kc n", p=128)
        wg_v = wg.rearrange("(kc p) n -> p kc n", p=128)
        wo_v = wo.rearrange("(kc p) n -> p kc n", p=128)
        wq1_v = wq1.rearrange("(kc p) n -> p kc n", p=128)
        wkz_v = wkz.rearrange("(kc p) n -> p kc n", p=128)
        wkv_v = wkv.rearrange("(kc p) n -> p kc n", p=128)
        wo1_v = wo1.rearrange("(kc p) n -> p kc n", p=128)
        for h in range(RH):
            units[("qk", h)] = (wqk_v[:, :, h * 512:(h + 1) * 512], (8, 512))
            units[("v", h)] = (wv_v[:, :, h * 512:(h + 1) * 512], (8, 512))
            units[("g", h)] = (wg_v[:, :, h * 512:(h + 1) * 512], (8, 512))
        for half in range(2):
            for ecg in range(2):
                units[("wo", half, ecg)] = (wo_v[:, ecg * 8:(ecg + 1) * 8, half * 512:(half + 1) * 512], (8, 512))
            units[("q1", half)] = (wq1_v[:, :, half * 512:(half + 1) * 512], (8, 512))
            units[("wo1", half)] = (wo1_v[:, :, half * 512:(half + 1) * 512], (8, 512))
        units[("kz",)] = (wkz_v, (8, 512))
        units[("kv",)] = (wkv_v, (8, 256))
        FG = [(0, 4), (4, 4), (8, 4), (12, 4), (16, 4), (20, 2)]
        UG = [(0, 8), (8, 8), (16, 6)]
        for l in range(2):
            w1_v = w1[l].rearrange("(kc p) n -> p kc n", p=128)
            w3_v = w3[l].rearrange("(kc p) n -> p kc n", p=128)
            w2_v = w2[l].rearrange("(f p) n -> p f n", p=128)
            for gi, (f0, nf) in enumerate(FG):
                units[("w1", l, gi)] = (w1_v[:, :, f0 * 128:(f0 + nf) * 128], (8, nf * 128))
                units[("w3", l, gi)] = (w3_v[:, :, f0 * 128:(f0 + nf) * 128], (8, nf * 128))
            for half in range(2):
                for ui, (f0, nf) in enumerate(UG):
                    units[("w2", l, half, ui)] = (w2_v[:, f0:f0 + nf, half * 512:(half + 1) * 512], (nf, 512))

        plan = []
        for t in range(NPT):
            nrep = 2 if t == NPT - 1 else 1
            for rep in range(nrep):
                if rep == 0:
                    for h in range(RH):
                        plan += [("qk", h), ("v", h), ("g", h)]
                plan += [("wo", 0, 0), ("wo", 0, 1), ("wo", 1, 0), ("wo", 1, 1)]
            for l in range(2):
                if l == 1:
                    for _ in range(nrep):
                        plan += [("q1", 0), ("q1", 1), ("kz",), ("kv",), ("wo1", 0), ("wo1", 1)]
                for gi in range(len(FG)):
                    plan += [("w1", l, gi), ("w3", l, gi)]
                for half in range(2):
                    for ui in range(len(UG)):
                        plan.append(("w2", l, half, ui))
        wstate = {"k": 0, "loaded": 0}

        def slot_view(slot, shp):
            a, b = shp
            return wring[slot][:, :a * b].rearrange("p (a b) -> p a b", a=a)

        def wget(key):
            u = wstate["k"]
            assert plan[u] == key, (u, plan[u], key)
            wstate["k"] += 1
            while wstate["loaded"] < min(len(plan), u + NW - 1):
                j = wstate["loaded"]
                src, shp = units[plan[j]]
                sl = j % NW
                dma("pool", slot_view(sl, shp), src, (), [wrb[sl]], f"w{sl}")
                wstate["loaded"] += 1
            src, shp = units[key]
            return slot_view(u % NW, shp), wrb[u % NW]

        dma("sp", cf[:, :], cf_d[:, :], (), [cfb], "c0")
        dma("sp", sp[:, :], spar[:, :], (), [spb], "c0")
        dma("pool", cb[:, :], cb_d[:, :], (), [cbb], "c1")
        act(esink[:, :], sp[:, SP_SINK:SP_SINK + 8], AF.Exp, [spb], [esinkb])
        for h in range(RH):
            for a in range(2):
                memset("dve", Sst[:, h, a, :], 0.0, [Sb[h][a]])
                memset("dve", Sbf[:, h, a, :], 0.0, [Sbfb[h][a]])
        for i in range(5):
            memset("dve", vz[:, i, :, :], 0.0, [vzb[i]])

        ident_bf = cb[:, CB_ID:CB_ID + 128]
        ones_bf = cb[:, CB_ONES:CB_ONES + 128]
        ident_f = cf[:, CF_ID:CF_ID + 128]

        def gain(gi, kc):
            return sp[:, SP_G + gi * 8 + kc:SP_G + gi * 8 + kc + 1]

        def rmsnorm(cx, gi, to_x=False):
            NT = cx.NT
            bk, bb, _ = pb()
            for kc in range(KC):
                s = kc % 2
                act(sq[:, s, :NT], cx.xT[:, kc, :NT], AF.Square, [cx.xb[kc]], [sqb[s]])
                mm(bk[:, :NT], ones_bf, sq[:, s, :NT], kc == 0, kc == KC - 1, [sqb[s], cbb], [bb])
            act(rtmp[:, :NT], bk[:, :NT], AF.Sqrt, [bb], [rtmpb], bias=EPS, scale=1.0 / D)
            recip(rstd[:, :NT], rtmp[:, :NT], [rtmpb], [rstdb])
            for kc in range(KC):
                if to_x:
                    stt(cx.xT[:, kc, :NT], cx.xT[:, kc, :NT], gain(gi, kc), rstd[:, :NT], ALU.mult, ALU.mult,
                        [cx.xb[kc], rstdb, spb], [cx.xb[kc]])
                else:
                    stt(cx.hT[:, kc, :NT], cx.xT[:, kc, :NT], gain(gi, kc), rstd[:, :NT], ALU.mult, ALU.mult,
                        [cx.xb[kc], rstdb, spb], [cx.hb[kc]])

        def ffn(cxs, l):
            A.reset()
            aTs, abs_, s1s, s1bs = [], [], [], []
            for ci, cx in enumerate(cxs):
                aTs.append(A.alloc(f"aT{ci}", [128, NF, cx.NT], BF16))
                abs_.append([A.buf(f"a{f}") for f in range(NF)])
                s1s.append(A.alloc(f"s1{ci}", [128, 2, cx.NT], F32))
                s1bs.append([A.buf("s1a"), A.buf("s1b")])
            for gi, (f0, nf) in enumerate(FG):
                w1u, w1b_ = wget(("w1", l, gi))
                w3u, w3b_ = wget(("w3", l, gi))
                for fi in range(nf):
                    f = f0 + fi
                    for ci, cx in enumerate(cxs):
                        NT = cx.NT
                        aT, ab, s1, s1b = aTs[ci], abs_[ci], s1s[ci], s1bs[ci]
                        b1, bb1, _ = pb()
                        b3, bb3, _ = pb()
                        for kc in range(KC):
                            mm(b1[:, :NT], w1u[:, kc, fi * 128:(fi + 1) * 128], cx.hT[:, kc, :NT], kc == 0, kc == KC - 1,
                               [w1b_, cx.hb[kc]], [bb1])
                        for kc in range(KC):
                            mm(b3[:, :NT], w3u[:, kc, fi * 128:(fi + 1) * 128], cx.hT[:, kc, :NT], kc == 0, kc == KC - 1,
                               [w3b_, cx.hb[kc]], [bb3])
                        act(s1[:, f % 2, :NT], b1[:, :NT], AF.Silu, [bb1], [s1b[f % 2]])
                        tt(aT[:, f, :NT], s1[:, f % 2, :NT], b3[:, :NT], ALU.mult, [s1b[f % 2], bb3], [ab[f]])
            for half in range(2):
                bss = [[pb(hold=True) for _ in range(4)] for _ in cxs]
                for ui, (f0, nf) in enumerate(UG):
                    w2u, w2b_ = wget(("w2", l, half, ui))
                    for dm in range(4):
                        for fi in range(nf):
                            f = f0 + fi
                            for ci, cx in enumerate(cxs):
                                mm(bss[ci][dm][0][:, :cx.NT], w2u[:, fi, dm * 128:(dm + 1) * 128], aTs[ci][:, f, :cx.NT],
                                   f == 0, f == NF - 1, [w2b_, abs_[ci][f]], [bss[ci][dm][1]])
                for ci, cx in enumerate(cxs):
                    NT = cx.NT
                    for dm in range(4):
                        kc = half * 4 + dm
                        tt(cx.xT[:, kc, :NT], bss[ci][dm][0][:, :NT], cx.xT[:, kc, :NT], ALU.add, [bss[ci][dm][1], cx.xb[kc]], [cx.xb[kc]])
                        release(bss[ci][dm][2])

        def gn_gate(ob, obb, g_ap, gbuf, u_ap, ubuf, on_ap, onbuf, si):
            st = stat[:, si, :]
            sbf = statb[si]
            S.op("dve", lambda e: e.bn_stats(out=st[:, 0:6], in_=ob), [obb], [sbf])
            S.op("dve", lambda e: e.bn_aggr(out=st[:, 6:8], in_=st[:, 0:6]), [sbf], [sbf])
            act(st[:, 8:9], st[:, 7:8], AF.Sqrt, [sbf], [sbf], bias=EPS, scale=1.0)
            recip(st[:, 9:10], st[:, 8:9], [sbf], [sbf])
            stt(st[:, 10:11], st[:, 6:7], -1.0, st[:, 9:10], ALU.mult, ALU.mult, [sbf], [sbf])
            act(on_ap, ob, AF.Identity, [obb, sbf], [onbuf], bias=st[:, 10:11], scale=st[:, 9:10])
            tt(u_ap, on_ap, g_ap, ALU.mult, [onbuf, gbuf], [ubuf])

        def layer0(cx, tile_i, sample, rope_i, rider=None, pre=None):
            NT = cx.NT
            nch = NT // 128
            A.reset()
            if pre is None:
                qT = [A.alloc(f"qT{i}", [128, 2, NT], BF16) for i in range(2)]
                qdT = [A.alloc(f"qdT{i}", [128, 2, NT], BF16) for i in range(2)]
                kT = [A.alloc(f"kT{i}", [128, 2, NT], BF16) for i in range(2)]
                qTb = [[A.buf("qT") for _ in range(2)] for _ in range(2)]
                qdTb = [[A.buf("qdT") for _ in range(2)] for _ in range(2)]
                kTb = [[A.buf("kT") for _ in range(2)] for _ in range(2)]
                kd = [A.alloc(f"kd{i}", [128, nch, 256], BF16) for i in range(2)]
                kdb = [[A.buf("kd") for _ in range(nch)] for _ in range(2)]
                vv = [A.alloc(f"v{i}", [128, nch, 512], BF16) for i in range(2)]
                vb = [[A.buf("v") for _ in range(nch)] for _ in range(2)]
                gg = [A.alloc(f"g{i}", [128, nch, 512], BF16) for i in range(2)]
                gb = [[A.buf("g") for _ in range(nch)] for _ in range(2)]
            else:
                qT, qdT, kT, kd, vv, gg = pre["qT"], pre["qdT"], pre["kT"], pre["kd"], pre["v"], pre["g"]
                qTb, qdTb, kTb, kdb, vb, gb = pre["qTb"], pre["qdTb"], pre["kTb"], pre["kdb"], pre["vb"], pre["gb"]
            npar = len(qT)
            R = None
            if rider is not None:
                rcx, r_rope, store, store_bufs = rider
                pend = {}
                for b_ in store_bufs:
                    if b_.w is not None and pend.get(b_.w[0], -1) < b_.w[1]:
                        pend[b_.w[0]] = b_.w[1]
                    for k_, v_ in b_.rs.items():
                        if pend.get(k_, -1) < v_:
                            pend[k_] = v_
                flat = store.rearrange("p a b -> p (a b)").bitcast(BF16)
                R = {k_: [] for k_ in ("qT", "qdT", "kT", "kd", "v", "g", "qTb", "qdTb", "kTb", "kdb", "vb", "gb")}
                for h_ in range(RH):
                    o_ = h_ * 2048
                    R["qT"].append(flat[:, o_:o_ + 256].rearrange("p (a t) -> p a t", a=2))
                    R["qdT"].append(flat[:, o_ + 256:o_ + 512].rearrange("p (a t) -> p a t", a=2))
                    R["kT"].append(flat[:, o_ + 512:o_ + 768].rearrange("p (a t) -> p a t", a=2))
                    R["kd"].append(flat[:, o_ + 768:o_ + 1024].rearrange("p (c d) -> p c d", c=1))
                    R["v"].append(flat[:, o_ + 1024:o_ + 1536].rearrange("p (c d) -> p c d", c=1))
                    R["g"].append(flat[:, o_ + 1536:o_ + 2048].rearrange("p (c d) -> p c d", c=1))
                    R["qTb"].append([Buf("rqT", pending=pend) for _ in range(2)])
                    R["qdTb"].append([Buf("rqdT", pending=pend) for _ in range(2)])
                    R["kTb"].append([Buf("rkT", pending=pend) for _ in range(2)])
                    R["kdb"].append([Buf("rkd", pending=pend)])
                    R["vb"].append([Buf("rv", pending=pend)])
                    R["gb"].append([Buf("rg", pending=pend)])
                r_cos = rope[r_rope][:, 0, :NS]
                r_sin = rope[r_rope][:, 1, :NS]
                r_rpb = ropeb[r_rope]
            if pre is None:
                tmp = A.alloc("rt", [128, 4, NT], F32)
                tmpb = [A.buf(f"rt{i}") for i in range(4)]
                tq = A.alloc("tq", [128, 2, NT], F32)
                tqb = [A.buf("tq0"), A.buf("tq1")]
            uT = A.alloc("uT", [128, 16, NT], BF16)
            uTb = [[A.buf("uT") for _ in range(nch)] for _ in range(RH)]
            on = A.alloc("on", [128, 2, 512], F32)
            onb = [A.buf("on0"), A.buf("on1")]
            uu = A.alloc("u", [128, 2, 512], BF16)
            ub = [A.buf("u0"), A.buf("u1")]
            sTm = A.alloc("sTm", [128, 2, 128], BF16)
            sTmb = [A.buf("sTm0"), A.buf("sTm1")]
            if sample:
                NSR = 4
                S0 = [A.alloc(f"S0{i}", [128, 2, 512], F32) for i in range(NSR)]
                S0b = [A.buf("S0") for i in range(NSR)]
                S0bf = [A.alloc(f"S0bf{i}", [128, 2, 512], BF16) for i in range(3)]
                S0bfb = [A.buf("S0bf") for i in range(3)]
                Sn = [A.alloc(f"Sn{i}", [128, 2, 512], F32) for i in range(NSR)]
                Snb = [A.buf("Sn") for i in range(NSR)]
                Qz = A.alloc("Qz", [128, 2, NB, 128], BF16)
                Qzb = A.buf("Qz")
                KZ = A.alloc("KZ", [128, 2, NB, 128], BF16)
                KZb = A.buf("KZ")
            rp = rope[rope_i]
            rpb = ropeb[rope_i]
            cosv = rp[:, 0, :NT]
            sinv = rp[:, 1, :NT]
            mt_off = CF_MTS if sample else CF_MT
            qd_off = CF_QDS if sample else CF_QD
            kd_off = CF_KDS if sample else CF_KD
            gam = _gammas()
            sidx = [0]
            s0_issued = [0]

            def issue_s0(u):
                if u >= RH * NB or u < s0_issued[0]:
                    return
                assert u == s0_issued[0]
                s0_issued[0] += 1
                dma("sp", S0[u % 4][:, :, :], st_in[u % NB, u // NB].rearrange("(a p) e -> p a e", p=128), (), [S0b[u % 4]], f"s0{u % 4}")

            if sample:
                for u_ in range(4):
                    issue_s0(u_)
            wun = {}

            def p_qk(hd, which):
                def f():
                    p = hd % npar
                    if which == 0:
                        wun[hd] = wget(("qk", hd))
                    wu, wub = wun[hd]
                    base = which * 256
                    b1, bb1, _ = pb()
                    b2, bb2, _ = pb()
                    for kc in range(KC):
                        mm(b1[:, :NT], wu[:, kc, base:base + 128], cx.hT[:, kc, :NT], kc == 0, kc == KC - 1, [wub, cx.hb[kc]], [bb1])
                    for kc in range(KC):
                        mm(b2[:, :NT], wu[:, kc, base + 128:base + 256], cx.hT[:, kc, :NT], kc == 0, kc == KC - 1, [wub, cx.hb[kc]], [bb2])
                    tt(tmp[:, 0, :], b1[:, :NT], cosv, ALU.mult, [bb1, rpb], [tmpb[0]])
                    tt(tmp[:, 1, :], b2[:, :NT], sinv, ALU.mult, [bb2, rpb], [tmpb[1]])
                    tt(tmp[:, 2, :], b1[:, :NT], sinv, ALU.mult, [bb1, rpb], [tmpb[2]])
                    tt(tmp[:, 3, :], b2[:, :NT], cosv, ALU.mult, [bb2, rpb], [tmpb[3]])
                    if which == 0:
                        tt(tq[:, 0, :], tmp[:, 0, :], tmp[:, 1, :], ALU.subtract, [tmpb[0], tmpb[1]], [tqb[0]])
                        tt(tq[:, 1, :], tmp[:, 2, :], tmp[:, 3, :], ALU.add, [tmpb[2], tmpb[3]], [tqb[1]])
                        for a in range(2):
                            act(qT[p][:, a, :], tq[:, a, :], AF.Copy, [tqb[a]], [qTb[p][a]])
                            qdv = cf[:, qd_off + hd * 128:qd_off + (hd + 1) * 128].unsqueeze(1).broadcast_to([128, nch, 128])
                            tt(qdT[p][:, a, :].rearrange("p (c i) -> p c i", c=nch),
                               tq[:, a, :].rearrange("p (c i) -> p c i", c=nch), qdv, ALU.mult,
                               [tqb[a], cfb], [qdTb[p][a]])
                    else:
                        tt(kT[p][:, 0, :], tmp[:, 0, :], tmp[:, 1, :], ALU.subtract, [tmpb[0], tmpb[1]], [kTb[p][0]])
                        tt(kT[p][:, 1, :], tmp[:, 2, :], tmp[:, 3, :], ALU.add, [tmpb[2], tmpb[3]], [kTb[p][1]])
                    if R is not None:
                        b1, bb1, _ = pb()
                        b2, bb2, _ = pb()
                        for kc in range(KC):
                            mm(b1[:, :NS], wu[:, kc, base:base + 128], rcx.hT[:, kc, :NS], kc == 0, kc == KC - 1, [wub, rcx.hb[kc]], [bb1])
                        for kc in range(KC):
                            mm(b2[:, :NS], wu[:, kc, base + 128:base + 256], rcx.hT[:, kc, :NS], kc == 0, kc == KC - 1, [wub, rcx.hb[kc]], [bb2])
                        tt(tmp[:, 0, :NS], b1[:, :NS], r_cos, ALU.mult, [bb1, r_rpb], [tmpb[0]])
                        tt(tmp[:, 1, :NS], b2[:, :NS], r_sin, ALU.mult, [bb2, r_rpb], [tmpb[1]])
                        tt(tmp[:, 2, :NS], b1[:, :NS], r_sin, ALU.mult, [bb1, r_rpb], [tmpb[2]])
                        tt(tmp[:, 3, :NS], b2[:, :NS], r_cos, ALU.mult, [bb2, r_rpb], [tmpb[3]])
                        if which == 0:
                            tt(tq[:, 0, :NS], tmp[:, 0, :NS], tmp[:, 1, :NS], ALU.subtract, [tmpb[0], tmpb[1]], [tqb[0]])
                            tt(tq[:, 1, :NS], tmp[:, 2, :NS], tmp[:, 3, :NS], ALU.add, [tmpb[2], tmpb[3]], [tqb[1]])
                            for a in range(2):
                                act(R["qT"][hd][:, a, :], tq[:, a, :NS], AF.Copy, [tqb[a]], [R["qTb"][hd][a]])
                                tt(R["qdT"][hd][:, a, :], tq[:, a, :NS], cf[:, CF_QDS + hd * 128:CF_QDS + (hd + 1) * 128], ALU.mult,
                                   [tqb[a], cfb], [R["qdTb"][hd][a]])
                        else:
                            tt(R["kT"][hd][:, 0, :], tmp[:, 0, :NS], tmp[:, 1, :NS], ALU.subtract, [tmpb[0], tmpb[1]], [R["kTb"][hd][0]])
                            tt(R["kT"][hd][:, 1, :], tmp[:, 2, :NS], tmp[:, 3, :NS], ALU.add, [tmpb[2], tmpb[3]], [R["kTb"][hd][1]])
                return f

            def p_vg(hd, kind):
                def f():
                    p = hd % npar
                    wu_, wb_ = wget((kind, hd))
                    for c in range(nch):
                        bk, bb, _ = pb()
                        for kc in range(KC):
                            mm(bk[:, :], cx.hT[:, kc, c * 128:(c + 1) * 128], wu_[:, kc, :], kc == 0, kc == KC - 1, [wb_, cx.hb[kc]], [bb])
                        if kind == "v":
                            act(vv[p][:, c, :], bk[:, :], AF.Copy, [bb], [vb[p][c]])
                        else:
                            act(gg[p][:, c, :], bk[:, :], AF.Silu, [bb], [gb[p][c]])
                    if R is not None:
                        bk, bb, _ = pb()
                        for kc in range(KC):
                            mm(bk[:, :], rcx.hT[:, kc, 0:NS], wu_[:, kc, :], kc == 0, kc == KC - 1, [wb_, rcx.hb[kc]], [bb])
                        if kind == "v":
                            act(R["v"][hd][:, 0, :], bk[:, :], AF.Copy, [bb], [R["vb"][hd][0]])
                        else:
                            act(R["g"][hd][:, 0, :], bk[:, :], AF.Silu, [bb], [R["gb"][hd][0]])
                return f

            def p_kd(hd):
                def f():
                    p = hd % npar
                    for c in range(nch):
                        tb, tbb, _ = pb()
                        tbv = tb[:, :].bitcast(BF16)
                        for a in range(2):
                            tr(tbv[:, a * 128:(a + 1) * 128], kT[p][:, a, c * 128:(c + 1) * 128], ident_bf, [kTb[p][a], cbb], [tbb])
                        act(kd[p][:, c, :], tbv[:, 0:256], AF.Copy, [tbb, cfb], [kdb[p][c]],
                            scale=cf[:, kd_off + hd:kd_off + hd + 1])
                    if R is not None:
                        tb, tbb, _ = pb()
                        tbv = tb[:, :].bitcast(BF16)
                        for a in range(2):
                            tr(tbv[:, a * 128:(a + 1) * 128], R["kT"][hd][:, a, :], ident_bf, [R["kTb"][hd][a], cbb], [tbb])
                        act(R["kd"][hd][:, 0, :], tbv[:, 0:256], AF.Copy, [tbb, cfb], [R["kdb"][hd][0]],
                            scale=cf[:, CF_KDS + hd:CF_KDS + hd + 1])
                return f

            def proj_pieces(hd):
                if pre is not None:
                    return []
                return [p_qk(hd, 0), p_qk(hd, 1), p_vg(hd, "v"), p_vg(hd, "g"), p_kd(hd)]

            cst = {}

            def c_main(hd, c):
                p = hd % npar
                cd = float(_consts()[3][1 if sample else 0][hd])
                cs = slice(c * 128, (c + 1) * 128)
                if sample:
                    for a in range(2):
                        tt(Qz[:, a, :, :], qdT[p][:, a, :].unsqueeze(1).broadcast_to([128, NB, 128]),
                           cb[:, CB_BLK:CB_BLK + NB * 128].rearrange("p (b i) -> p b i", b=NB), ALU.mult,
                           [qdTb[p][a], cbb], [Qzb])
                        tt(KZ[:, a, :, :], kd[p][:, 0, a * 128:(a + 1) * 128].unsqueeze(1).broadcast_to([128, NB, 128]),
                           cf[:, CF_KZM:CF_KZM + NB].unsqueeze(2).broadcast_to([128, NB, 128]), ALU.mult,
                           [kdb[p][0], cfb], [KZb])
                sbk, sbb, _ = pb()
                for a in range(2):
                    mm(sbk[:, :128], kT[p][:, a, cs], qT[p][:, a, cs], a == 0, a == 1, [kTb[p][a], qTb[p][a]], [sbb])
                si = sidx[0] % 2
                sidx[0] += 1
                tt(sTm[:, si, :], sbk[:, :128], cf[:, mt_off + hd * 128:mt_off + (hd + 1) * 128], ALU.mult,
                   [sbb, cfb], [sTmb[si]])
                if not sample:
                    pbs = []
                    for a in range(2):
                        pk, pkb, _ = pb()
                        mm(pk[:, :], kd[p][:, c, a * 128:(a + 1) * 128], vv[p][:, c, :], True, True, [kdb[p][c], vb[p][c]], [pkb])
                        pbs.append((pk, pkb))
                    ob, obb, _ = pb()
                    mm(ob[:, :], sTm[:, si, :], vv[p][:, c, :], True, False, [sTmb[si], vb[p][c]], [obb])
                    for a in range(2):
                        mm(ob[:, :], qdT[p][:, a, cs], Sbf[:, hd, a, :], False, a == 1, [qdTb[p][a], Sbfb[hd][a]], [obb])
                    for a in range(2):
                        stt(Sst[:, hd, a, :], Sst[:, hd, a, :], cd, pbs[a][0][:, :], ALU.mult, ALU.add,
                            [Sb[hd][a], pbs[a][1]], [Sb[hd][a]])
                        act(Sbf[:, hd, a, :], Sst[:, hd, a, :], AF.Copy, [Sb[hd][a]], [Sbfb[hd][a]])
                else:
                    ob, obb, obi = pb(hold=True)
                    mm(ob[:, :], sTm[:, si, :], vv[p][:, c, :], True, False, [sTmb[si], vb[p][c]], [obb])
                    for b in range(NB):
                        u = hd * NB + b
                        ui = u % 4
                        u2 = u % 3
                        issue_s0(u)
                        issue_s0(u + 1)
                        issue_s0(u + 2)
                        issue_s0(u + 3)
                        for a in range(2):
                            act(S0bf[u2][:, a, :], S0[ui][:, a, :], AF.Copy, [S0b[ui]], [S0bfb[u2]])
                        for a in range(2):
                            mm(ob[:, :], Qz[:, a, b, :], S0bf[u2][:, a, :], False, (b == NB - 1 and a == 1), [Qzb, S0bfb[u2]], [obb])
                        for a in range(2):
                            pk, pkb, _ = pb()
                            mm(pk[:, :], KZ[:, a, b, :], vv[p][:, c, :], True, True, [KZb, vb[p][c]], [pkb])
                            stt(Sn[ui][:, a, :], S0[ui][:, a, :], cd, pk[:, :], ALU.mult, ALU.add, [S0b[ui], pkb], [Snb[ui]])
                        dma("pool", srs[b, hd].rearrange("(a p) e -> p a e", p=128), Sn[ui][:, :, :], [Snb[ui]], [], f"sn{ui}")
                    release(obi)
                gs = gidx[0] % 2
                gidx[0] += 1
                gn_gate(ob[:, :], obb, gg[p][:, c, :], gb[p][c], uu[:, gs, :], ub[gs], on[:, gs, :], onb[gs], (hd * nch + c) % 4)
                cst[(hd, c)] = gs

            def c_tail(hd, c):
                gs = cst[(hd, c)]
                cs = slice(c * 128, (c + 1) * 128)
                tb, tbb, _ = pb()
                tbv = tb[:, :].bitcast(BF16)
                for ec in range(4):
                    tr(tbv[:, ec * 128:(ec + 1) * 128], uu[:, gs, ec * 128:(ec + 1) * 128], ident_bf, [ub[gs], cbb], [tbb])
                act(uT[:, hd * 4:(hd + 1) * 4, cs], tbv[:, 0:512].rearrange("p (a b) -> p a b", a=4), AF.Copy, [tbb], [uTb[hd][c]])

            gidx = [0]
            for pc in proj_pieces(0):
                pc()
            for hd in range(RH):
                q_ = proj_pieces(hd + 1) if hd + 1 < RH else []
                for c in range(nch):
                    c_main(hd, c)
                    if q_:
                        q_.pop(0)()
                    if c > 0:
                        c_tail(hd, c - 1)
                while len(q_) > 1:
                    q_.pop(0)()
                c_tail(hd, nch - 1)
                while q_:
                    q_.pop(0)()
            for half in range(2):
                bs = [pb(hold=True) for _ in range(4)]
                for ecg in range(2):
                    wou, wob = wget(("wo", half, ecg))
                    for dm in range(4):
                        for ec in range(8):
                            e_ = ecg * 8 + ec
                            mm(bs[dm][0][:, :NT], wou[:, ec, dm * 128:(dm + 1) * 128], uT[:, e_, :NT], e_ == 0, e_ == 15,
                               [wob] + uTb[e_ // 4], [bs[dm][1]])
                for dm in range(4):
                    kc = half * 4 + dm
                    tt(cx.xT[:, kc, :NT], bs[dm][0][:, :NT], cx.xT[:, kc, :NT], ALU.add, [bs[dm][1], cx.xb[kc]], [cx.xb[kc]])
                    release(bs[dm][2])
            if (not sample) and tile_i == NPT - 1:
                dma("sp", srp.rearrange("h (a p) e -> p h a e", p=128), Sst[:, :, :, :],
                    [Sb[h][a] for h in range(RH) for a in range(2)], [], "srp")
            return R

        def layer1(cx, tile_i, sample):
            NT = cx.NT
            nblk = NT // 128
            A.reset()
            qT1 = A.alloc("qT1", [128, 8, NT], BF16)
            qT1b = [A.buf("qT1") for _ in range(8)]
            kvT = A.alloc("kvT", [128, 2, NT], F32)
            kvTb = [A.buf("kT1"), A.buf("vT1")]
            oT = A.alloc("oT", [128, 8, NT], BF16)
            oTb2 = [[A.buf("oT") for _ in range(nblk)] for _ in range(8)]
            ee = A.alloc("ee", [128, 8, 128], BF16)
            eeb = [A.buf("ee") for _ in range(8)]
            if sample:
                pT5 = A.alloc("pT", [128, 4, NB, 32], BF16)
                qS = A.alloc("qS", [128, 2, NB, 32], BF16)
                qSb = A.buf("qS")
            else:
                pT = A.alloc("pT", [128, 12, 128], BF16)
            pTb = [A.buf("pT") for _ in range(16 if sample else 12)]
            rec = A.alloc("rec", [128, 2, 512], F32)
            recb = [A.buf("rec0"), A.buf("rec1")]
            tok = A.alloc("tok", [128, 2, 128], F32)
            tokb = [A.buf("tok0"), A.buf("tok1")]
            if sample:
                Kc = [A.alloc(f"Kc{i}", [128, 2, 128], F32) for i in range(2)]
                Kcb = [A.buf("Kc0"), A.buf("Kc1")]
                Kcz = [A.alloc(f"Kcz{i}", [128, 4, 128], BF16) for i in range(2)]
                Kczb = [A.buf("Kcz0"), A.buf("Kcz1")]
                Vcz = [A.alloc(f"Vcz{i}", [128, 4, 128], BF16) for i in range(2)]
                Vczb = [A.buf("Vcz0"), A.buf("Vcz1")]
                kzc = [A.alloc(f"kzc{i}", [128, 4, 128], BF16) for i in range(2)]
                kzcb = [A.buf("kzc0"), A.buf("kzc1")]
                e32 = A.alloc("e32", [128, 4, 32], BF16)
                e32b = [A.buf("e32") for _ in range(4)]
                pTc = [A.alloc(f"pTc{i}", [128, 4, 32], BF16) for i in range(2)]
                pTcb = [A.buf("pTc0"), A.buf("pTc1")]

                def issue_kc(b):
                    if b < NB:
                        dma("sp", Kc[b % 2][:, 0, :], ck[b], [], [Kcb[b % 2]], f"kc{b % 2}")
                        dma("sp", Kc[b % 2][:, 1, :], cv[b], [], [Kcb[b % 2]], f"kc{b % 2}")
                issue_kc(0)
                issue_kc(1)
            for qu in range(2):
                wu, wub = wget(("q1", qu))
                for j in range(4):
                    hp = qu * 4 + j
                    bk, bb, _ = pb()
                    for kc in range(KC):
                        mm(bk[:, :NT], wu[:, kc, j * 128:(j + 1) * 128], cx.hT[:, kc, :NT], kc == 0, kc == KC - 1, [wub, cx.hb[kc]], [bb])
                    act(qT1[:, hp, :], bk[:, :NT], AF.Identity, [bb, spb], [qT1b[hp]], bias=sp[:, SP_BQ + hp:SP_BQ + hp + 1])
            wu, wub = wget(("kz",))
            for var in range(4):
                bk, bb, _ = pb()
                for kc in range(KC):
                    mm(bk[:, :NT], wu[:, kc, var * 128:(var + 1) * 128], cx.hT[:, kc, :NT], kc == 0, kc == KC - 1, [wub, cx.hb[kc]], [bb])
                act(kz[:, var, 128:128 + NT], bk[:, :NT], AF.Identity, [bb, spb], [kzb[1 + i] for i in range(nblk)],
                    bias=sp[:, SP_BKZ + var:SP_BKZ + var + 1])
            wu, wub = wget(("kv",))
            for j in range(2):
                bk, bb, _ = pb()
                for kc in range(KC):
                    mm(bk[:, :NT], wu[:, kc, j * 128:(j + 1) * 128], cx.hT[:, kc, :NT], kc == 0, kc == KC - 1, [wub, cx.hb[kc]], [bb])
                act(kvT[:, j, :], bk[:, :NT], AF.Identity, [bb, spb], [kvTb[j]], bias=sp[:, SP_BK + j:SP_BK + j + 1])
            for blk in range(nblk):
                cs = slice(blk * 128, (blk + 1) * 128)
                tb, tbb, _ = pb()
                tr(tb[:, 0:128], kvT[:, 1, cs], ident_f, [kvTb[1], cfb], [tbb])
                for var in range(4):
                    kvh, par = var // 2, var % 2
                    act(vz[:, 1 + blk, var, par * 64:(par + 1) * 64], tb[:, kvh * 64:(kvh + 1) * 64], AF.Copy, [tbb], [vzb[1 + blk]])
                last = sample or (tile_i == NPT - 1 and blk == nblk - 1)
                if last:
                    tt_i = 0
                    act(tok[:, 1, :], tb[:, 0:128], AF.Copy, [tbb], [tokb[1]])
                    tb2, tbb2, _ = pb()
                    tr(tb2[:, 0:128], kvT[:, 0, cs], ident_f, [kvTb[0], cfb], [tbb2])
                    act(tok[:, 0, :], tb2[:, 0:128], AF.Copy, [tbb2], [tokb[0]])
                    if sample:
                        for b in range(NB):
                            dma("sp", ks_o[b, 128 - DS:128, :], tok[b * DS:(b + 1) * DS, 0, :], [tokb[0]], [], "ko")
                            dma("sp", vs_o[b, 128 - DS:128, :], tok[b * DS:(b + 1) * DS, 1, :], [tokb[1]], [], "ko")
                        dma("sp", ks_o[:, 0:128 - DS, :], ck[:, DS:128, :], [], [], "ko")
                        dma("sp", vs_o[:, 0:128 - DS, :], cv[:, DS:128, :], [], [], "ko")
                    else:
                        dma("sp", kp_o[:, :], tok[:, 0, :], [tokb[0]], [], "ko")
                        dma("sp", vp_o[:, :], tok[:, 1, :], [tokb[1]], [], "ko")
            m_cur = cb[:, CB_MCUR:CB_MCUR + 128]
            m_prev = cb[:, CB_MPREV:CB_MPREV + 128]
            m_new = cb[:, CB_MNEW:CB_MNEW + 128]
            ei = [0]

            if not sample:
                its = [(blk, hp) for blk in range(nblk) for hp in range(8)]
                sres = {}

                def a_scores(i):
                    blk, hp = its[i]
                    gblk = tile_i * (TT // 128) + blk
                    qs = slice(blk * 128, (blk + 1) * 128)
                    kbs = [blk + 1] + ([blk] if gblk > 0 else [])
                    nk = len(kbs)
                    kvh = hp // 4
                    items = []
                    sbk, sbb, _ = pb()
                    col = 0
                    for par in range(2):
                        var = kvh * 2 + par
                        for kbi, kblk in enumerate(kbs):
                            pi = (i % 3) * 4 + par * 2 + kbi
                            mm(sbk[:, col * 128:(col + 1) * 128], kz[:, var, kblk * 128:(kblk + 1) * 128], qT1[:, hp, qs], True, True,
                               [kzb[kblk], qT1b[hp]], [sbb])
                            col += 1
                            items.append((par, var, kblk, pi))
                    w_ = 2 * nk * 128
                    e_i = i % 2
                    act(ee[:, e_i * 4:e_i * 4 + 2 * nk, :], sbk[:, :w_].rearrange("p (a c) -> p a c", c=128), AF.Exp, [sbb], [eeb[e_i]], scale=0.125)
                    base = (i % 3) * 4
                    dst = pT[:, base:base + 4, :].rearrange("p (a b) c -> p a b c", a=2)[:, :, 0:nk, :]
                    src = ee[:, e_i * 4:e_i * 4 + 2 * nk, :].rearrange("p (a b) c -> p a b c", a=2)
                    msk = cb[:, CB_MCUR:CB_MCUR + nk * 128].rearrange("p (b c) -> p b c", c=128).unsqueeze(1).broadcast_to([128, 2, nk, 128])
                    tt(dst, src, msk, ALU.mult, [eeb[e_i], cbb], [pTb[i % 3]])
                    sres[i] = items

                def a_nd(i):
                    blk, hp = its[i]
                    qs = slice(blk * 128, (blk + 1) * 128)
                    items = sres.pop(i)
                    nb_, nbb, _ = pb()
                    for n, (par, var, kblk, pi) in enumerate(items):
                        mm(nb_[:, :128], vz[:, kblk, var, :], pT[:, pi, :], n == 0, n == len(items) - 1, [vzb[kblk], pTb[i % 3]], [nbb])
                    db_, dbb, _ = pb()
                    for n, (par, var, kblk, pi) in enumerate(items):
                        mm(db_[:, :128], cb[:, CB_ONESZ + par * 128:CB_ONESZ + (par + 1) * 128], pT[:, pi, :], n == 0, n == len(items) - 1,
                           [cbb, pTb[i % 3]], [dbb])
                    ri = i % 2
                    act(rec[:, ri, :128], db_[:, :128], AF.Identity, [dbb, esinkb], [recb[ri]], bias=esink[:, hp:hp + 1])
                    recip(rec[:, ri, 128:256], rec[:, ri, :128], [recb[ri]], [recb[ri]])
                    tt(oT[:, hp, qs], nb_[:, :128], rec[:, ri, 128:256], ALU.mult, [nbb, recb[ri]], [oTb2[hp][blk]])

                a_scores(0)
                a_scores(1)
                for i in range(len(its)):
                    if i + 2 < len(its):
                        a_scores(i + 2)
                    a_nd(i)
                S.op("act", lambda e: e.activation(out=kz[:, :, 0:128], in_=kz[:, :, NT:NT + 128], func=AF.Copy), [kzb[nblk]], [kzb[0]])
                S.op("dve", lambda e: e.tensor_copy(out=vz[:, 0, :, :], in_=vz[:, nblk, :, :]), [vzb[nblk]], [vzb[0]])
            else:
                qs = slice(0, 128)
                for gi_ in range(4):
                    par, kvh = gi_ // 2, gi_ % 2
                    var = kvh * 2 + par
                    sbk, sbb, _ = pb()
                    for m in range(4):
                        hp = kvh * 4 + m
                        mm(sbk[:, m * 128:(m + 1) * 128], kz[:, var, 128:256], qT1[:, hp, qs], True, True, [kzb[1], qT1b[hp]], [sbb])
                    e_i = gi_ % 2
                    act(ee[:, e_i * 4:e_i * 4 + 4, :], sbk[:, :].rearrange("p (a c) -> p a c", c=128), AF.Exp, [sbb], [eeb[e_i]], scale=0.125)
                    tt(pT5[:, par * 2 + kvh, :, :].rearrange("p b (m i) -> p b m i", m=4),
                       ee[:, e_i * 4:e_i * 4 + 4, :].rearrange("p m (b i) -> p b m i", b=NB),
                       m_new.rearrange("p (b i) -> p b i", b=NB).unsqueeze(2).broadcast_to([128, NB, 4, DS]), ALU.mult,
                       [eeb[e_i], cbb], [pTb[gi_]])
                for kvh in range(2):
                    S.op("dve", (lambda o_, i_: (lambda e: e.tensor_copy(out=o_, in_=i_)))(
                        qS[:, kvh, :, :].rearrange("p b (m i) -> p b m i", m=4),
                        qT1[:, kvh * 4:(kvh + 1) * 4, :].rearrange("p m (b i) -> p b m i", b=NB)),
                        qT1b[kvh * 4:(kvh + 1) * 4], [qSb])
                nbk = [pb(hold=True) for _ in range(2)]
                dbk = [pb(hold=True) for _ in range(2)]
                m_cache = cb[:, CB_MCACHE:CB_MCACHE + 32].rearrange("p (m i) -> p m i", m=4)
                memset("dve", Kcz[0][:, :, :], 0.0, [Kczb[0]])
                memset("dve", Kcz[1][:, :, :], 0.0, [Kczb[1]])
                memset("dve", Vcz[0][:, :, :], 0.0, [Vczb[0]])
                memset("dve", Vcz[1][:, :, :], 0.0, [Vczb[1]])

                def s1(b):
                    ui = b % 2
                    for par in range(2):
                        kdst = Kcz[ui][:, par:4:2, par * 64:(par + 1) * 64] if False else \
                            Kcz[ui][:, :, :].rearrange("p (k r) c -> p k r c", k=2)[:, :, par, par * 64:(par + 1) * 64]
                        vdst = Vcz[ui][:, :, :].rearrange("p (k r) c -> p k r c", k=2)[:, :, par, par * 64:(par + 1) * 64]
                        ksrc = Kc[ui][:, 0, :].rearrange("p (k d) -> p k d", k=2)
                        vsrc = Kc[ui][:, 1, :].rearrange("p (k d) -> p k d", k=2)
                        S.op("dve", (lambda o_, i_: (lambda e: e.tensor_copy(out=o_, in_=i_)))(kdst, ksrc), [Kcb[ui]], [Kczb[ui]])
                        act(vdst, vsrc, AF.Copy, [Kcb[ui]], [Vczb[ui]])
                    tb, tbb, _ = pb()
                    tbv = tb[:, :].bitcast(BF16)
                    for var in range(4):
                        tr(tbv[:, var * 128:(var + 1) * 128], Kcz[ui][:, var, :], ident_bf, [Kczb[ui], cbb], [tbb])
                    act(kzc[ui][:, :, :], tbv[:, 0:512].rearrange("p (a b) -> p a b", a=4), AF.Copy, [tbb], [kzcb[ui]])

                def s2(b):
                    ui = b % 2
                    sbk, sbb, _ = pb()
                    for var in range(4):
                        kvh, par = var // 2, var % 2
                        mm(sbk[:, var * 32:(var + 1) * 32], kzc[ui][:, var, :], qS[:, kvh, b, :], True, True, [kzcb[ui], qSb], [sbb])
                    act(e32[:, :, :], sbk[:, 0:128].rearrange("p (v c) -> p v c", v=4), AF.Exp, [sbb], [e32b[0]], scale=0.125)
                    tt(pTc[ui][:, :, :].rearrange("p v (m i) -> p v m i", m=4), e32[:, :, :].rearrange("p v (m i) -> p v m i", m=4),
                       m_cache.unsqueeze(1).broadcast_to([128, 4, 4, DS]), ALU.mult, [e32b[0], cbb], [pTcb[ui]])

                def s3(b):
                    ui = b % 2
                    for kvh in range(2):
                        for (bk3, lhs_kind) in ((nbk[kvh], "v"), (dbk[kvh], "o")):
                            outv = bk3[0][:, b * 32:(b + 1) * 32]
                            n = 0
                            for par in range(2):
                                var = kvh * 2 + par
                                lhs = Vcz[ui][:, var, :] if lhs_kind == "v" else cb[:, CB_ONESZ + par * 128:CB_ONESZ + (par + 1) * 128]
                                rds = [Vczb[ui] if lhs_kind == "v" else cbb, pTcb[ui]]
                                mm(outv, lhs, pTc[ui][:, var, :], n == 0, False, rds, [bk3[1]])
                                n += 1
                            for par in range(2):
                                var = kvh * 2 + par
                                lhs = vz[:, 1, var, :] if lhs_kind == "v" else cb[:, CB_ONESZ + par * 128:CB_ONESZ + (par + 1) * 128]
                                rhs = pT5[:, par * 2 + kvh, b, :]
                                rds = [vzb[1] if lhs_kind == "v" else cbb, pTb[par * 2 + kvh]]
                                mm(outv, lhs, rhs, False, par == 1, rds, [bk3[1]])

                s1(0)
                for b in range(NB):
                    s2(b)
                    if b >= 1:
                        s3(b - 1)
                    issue_kc(b + 2)
                    if b + 1 < NB:
                        s1(b + 1)
                s3(NB - 1)
                for kvh in range(2):
                    dv = dbk[kvh][0][:, :].rearrange("p (b m i) -> p b m i", b=NB, m=4)
                    nv = nbk[kvh][0][:, :].rearrange("p (b m i) -> p b m i", b=NB, m=4)
                    r0 = rec[:, 0, :].rearrange("p (b m i) -> p b m i", b=NB, m=4)
                    r1 = rec[:, 1, :].rearrange("p (b m i) -> p b m i", b=NB, m=4)
                    for m in range(4):
                        hp = kvh * 4 + m
                        ts(r0[:, :, m, :], dv[:, :, m, :], esink[:, hp:hp + 1], None, ALU.add, ALU.bypass,
                           [dbk[kvh][1], esinkb], [recb[0]])
                    recip(rec[:, 1, :], rec[:, 0, :], [recb[0]], [recb[1]])
                    for m in range(4):
                        hp = kvh * 4 + m
                        tt(oT[:, hp, :].rearrange("p (b i) -> p b i", b=NB), nv[:, :, m, :], r1[:, :, m, :], ALU.mult,
                           [nbk[kvh][1], recb[1]], oTb2[hp])
                    release(nbk[kvh][2])
                    release(dbk[kvh][2])
            for half in range(2):
                wu, wub = wget(("wo1", half))
                for dm in range(4):
                    kc = half * 4 + dm
                    bk, bb, _ = pb()
                    for hp in range(8):
                        mm(bk[:, :NT], wu[:, hp, dm * 128:(dm + 1) * 128], oT[:, hp, :], hp == 0, hp == 7, [wub] + oTb2[hp], [bb])
                    stt(cx.xT[:, kc, :NT], bk[:, :NT], sp[:, SP_BO + kc:SP_BO + kc + 1], cx.xT[:, kc, :NT], ALU.add, ALU.add,
                        [bb, spb, cx.xb[kc]], [cx.xb[kc]])

        xin_p = xTp.rearrange("(kc p) t -> p kc t", p=128)
        yout_p = yTp.rearrange("(kc p) t -> p kc t", p=128)
        xin_s = xTs.rearrange("(kc p) t -> p kc t", p=128)
        yout_s = yTs.rearrange("(kc p) t -> p kc t", p=128)

        def load_rope(i, c0, n):
            dma("sp", rope[i][:, :, :n], rope_d[:, :, c0:c0 + n].rearrange("c p t -> p c t"), [], [ropeb[i]], f"rope{i}")

        dma("sp", pcx[0].xT[:, :, :], xin_p[:, :, 0:TT], [], pcx[0].xb, "xin0")
        load_rope(0, 0, TT)
        dma("sp", scx.xT[:, :, :], xin_s[:, :, :], [], scx.xb, "xins")
        for t in range(NPT):
            cx = pcx[t % 2]
            last = t == NPT - 1
            if not last:
                load_rope((t + 1) % 2, (t + 1) * TT, TT)
                dma("sp", pcx[(t + 1) % 2].xT[:, :, :], xin_p[:, :, (t + 1) * TT:(t + 2) * TT], [], pcx[(t + 1) % 2].xb, f"xin{(t + 1) % 2}")
            cxs = [cx, scx] if last else [cx]

            def dbg(i):
                if DEBUG and t == DEBUG_TILE:
                    dma("sp", dbg_o[i].rearrange("(kc p) t -> p kc t", p=128)[:, :, :TT], cx.xT[:, :, :TT], cx.xb, [], "dbg")
            if t == 0:
                rmsnorm(cx, 0)
            if last:
                load_rope((t + 1) % 2, SEQ, NS)
                rmsnorm(scx, 0)
                idle = pcx[(t + 1) % 2]
                R_ = layer0(cx, t, False, t % 2, rider=(scx, (t + 1) % 2, idle.xT, idle.xb))
                layer0(scx, NPT, True, (t + 1) % 2, pre=R_)
            else:
                layer0(cx, t, False, t % 2)
            dbg(0)
            for c_ in cxs:
                rmsnorm(c_, 1)
            ffn(cxs, 0)
            dbg(1)
            rmsnorm(cx, 2)
            layer1(cx, t, False)
            if last:
                rmsnorm(scx, 2)
                layer1(scx, NPT, True)
            dbg(2)
            for c_ in cxs:
                rmsnorm(c_, 3)
            ffn(cxs, 1)
            dbg(3)
            if not last:
                rmsnorm(pcx[(t + 1) % 2], 0)
            rmsnorm(cx, 4, to_x=True)
            dma("sp", yout_p[:, :, t * TT:(t + 1) * TT], cx.xT[:, :, :], cx.xb, [], "yout")
            if last:
                rmsnorm(scx, 4, to_x=True)
                dma("sp", yout_s[:, :, :], scx.xT[:, :, :], scx.xb, [], "yout")
        outs = Buf("outs")
        for name in list(S.dcount):
            if name in ("yout", "srp", "ko", "dbg") or name.startswith("sn"):
                outs.rs[("d", name)] = S.dcount[name]
        S.op("sp", lambda e: e.nop(), (), [outs])
        assert wstate["k"] == len(plan)
        S.emit(nc, es)
    return nc


_CACHE = {}


def _host_layout(inp):
    f = np.float32
    g = lambda k: np.ascontiguousarray(np.asarray(inp[k], dtype=f))
    x_prompt, x_sample = g("x_prompt"), g("x_sample")
    state_ret, ck, cv = g("state_ret")[0], g("cache_swa_k")[0], g("cache_swa_v")[0]
    wq, wk = g("ret_w_q")[0], g("ret_w_k")[0]
    wqk = np.concatenate([np.concatenate([wq[:, h * 256:(h + 1) * 256], wk[:, h * 256:(h + 1) * 256]], axis=1) for h in range(RH)], axis=1)
    wqkv = g("swa_w_qkv")[0]
    bqkv = g("swa_b_qkv")[0]
    wq1 = wqkv[:, :1024]
    wk1 = wqkv[:, 1024:1152]
    wv1 = wqkv[:, 1152:1280]
    wkz = np.zeros((D, 4, 128), f)
    bkz = np.zeros((4, 128), f)
    for kvh in range(2):
        for par in range(2):
            wkz[:, kvh * 2 + par, par * 64:(par + 1) * 64] = wk1[:, kvh * 64:(kvh + 1) * 64]
            bkz[kvh * 2 + par, par * 64:(par + 1) * 64] = bqkv[1024 + kvh * 64:1024 + (kvh + 1) * 64]
    wkz = wkz.reshape(D, 512)
    wkv = np.concatenate([wk1, wv1], axis=1)
    spar = np.zeros((128, SP_N), f)
    gains = [g("norm_mix")[0], g("norm_ffn")[0], g("norm_mix")[1], g("norm_ffn")[1], g("norm_final")]
    for i, v in enumerate(gains):
        spar[:, SP_G + i * 8:SP_G + (i + 1) * 8] = v.reshape(8, 128).T
    spar[:, SP_BQ:SP_BQ + 8] = bqkv[:1024].reshape(8, 128).T
    spar[:, SP_BKZ:SP_BKZ + 4] = bkz.T
    spar[:, SP_BK] = bqkv[1024:1152]
    spar[:, SP_BV] = bqkv[1152:1280]
    spar[:, SP_BO:SP_BO + 8] = g("swa_b_o")[0].reshape(8, 128).T
    sinks = g("swa_sinks")[0]
    spar[:, SP_SINK:SP_SINK + 8] = np.repeat(sinks.reshape(8, 2), 64, axis=1).T
    cf32, cb, rope, _ = _consts()
    shared = {
        "wqk": np.ascontiguousarray(wqk), "wv": g("ret_w_v")[0], "wg": g("ret_w_g")[0], "wo": g("ret_w_o")[0],
        "wq1": np.ascontiguousarray(wq1), "wkz": wkz, "wkv": np.ascontiguousarray(wkv), "wo1": g("swa_w_o")[0],
        "w1": g("ffn_w1"), "w3": g("ffn_w3"), "w2": g("ffn_w2"), "spar": spar, "cf32": cf32, "cb": cb, "rope": rope,
    }
    in_maps = []
    for c in range(NCORES):
        m = dict(shared)
        m["xTp"] = np.ascontiguousarray(x_prompt[c].T)
        m["xTs"] = np.ascontiguousarray(x_sample[c * NB:(c + 1) * NB].reshape(NS, D).T)
        m["st_in"] = np.ascontiguousarray(state_ret[c * NB:(c + 1) * NB])
        m["ck"] = np.ascontiguousarray(ck[c * NB:(c + 1) * NB].reshape(NB, 128, 128))
        m["cv"] = np.ascontiguousarray(cv[c * NB:(c + 1) * NB].reshape(NB, 128, 128))
        in_maps.append(m)
    return in_maps


def kernel(**inputs):
    if "nc" not in _CACHE:
        _CACHE["nc"] = build_program()
    nc = _CACHE["nc"]
    in_maps = _host_layout(inputs)
    res = run_bass_kernel_spmd(nc, in_maps, core_ids=list(range(NCORES)))
    R = res.results
    f = np.float32
    y_prompt = np.stack([R[c]["yTp"].T for c in range(NCORES)]).astype(f)
    y_sample = np.concatenate([R[c]["yTs"].T.reshape(NB, DS, D) for c in range(NCORES)]).astype(f)
    srp = np.stack([R[c]["srp"] for c in range(NCORES)])[None].astype(f)
    srs = np.concatenate([R[c]["srs"] for c in range(NCORES)])[None].astype(f)
    kp = np.stack([R[c]["kp"].reshape(128, 2, 64) for c in range(NCORES)])[None].astype(f)
    vp = np.stack([R[c]["vp"].reshape(128, 2, 64) for c in range(NCORES)])[None].astype(f)
    ks = np.concatenate([R[c]["ks"].reshape(NB, 128, 2, 64) for c in range(NCORES)])[None].astype(f)
    vs = np.concatenate([R[c]["vs"].reshape(NB, 128, 2, 64) for c in range(NCORES)])[None].astype(f)
    return (y_prompt, y_sample, srp, srs, kp, vp, ks, vs)
```

```python
import contextlib
import numpy as np
import concourse.bass as bass
import concourse.mybir as mybir
from concourse.bass_utils import run_bass_kernel_spmd

F32 = mybir.dt.float32
BF16 = mybir.dt.bfloat16
AF = mybir.ActivationFunctionType
ALU = mybir.AluOpType

NCORES = 8
D = 1024
KC = 8
SEQ = 2048
TT = 512
NPT = SEQ // TT
NS = 128
NB = 16
DS = 8
PAST = 16384
DFF = 2816
NF = DFF // 128
RH = 4
EPS = 1e-6
WSLOT = 4096
NW = 4
DEBUG = False
DEBUG_TILE = 0


class Buf:
    __slots__ = ("name", "excl", "const", "w", "rs")

    def __init__(self, name, excl=False, const=False, pending=None):
        self.name = name
        self.excl = excl
        self.const = const
        self.w = None
        self.rs = dict(pending) if pending else {}


class Sched:
    ENG = ("pe", "act", "dve", "pool", "sp")

    def __init__(self):
        self.ops = []
        self.cnt = {e: 0 for e in self.ENG}
        self.dcount = {}
        self.sig = set()

    def op(self, eng, fn, r=(), w=(), dsem=None):
        self.cnt[eng] += 1
        idx = self.cnt[eng]
        deps = {}

        def need(key, val):
            if deps.get(key, -1) < val:
                deps[key] = val

        for b in r:
            if b.w is not None:
                need(*b.w)
            if b.excl:
                for k, v in b.rs.items():
                    if k != ("e", eng):
                        need(k, v)
        for b in w:
            if b.w is not None:
                need(*b.w)
            for k, v in b.rs.items():
                need(k, v)
        if dsem is not None:
            self.dcount[dsem] = self.dcount.get(dsem, 0) + 16
            ev = (("d", dsem), self.dcount[dsem])
        else:
            ev = (("e", eng), idx)
        waits = []
        for key, val in deps.items():
            if key[0] == "e":
                if key[1] == "pe" and eng == "pe":
                    continue
                if key == ev[0] and val >= idx:
                    continue
                waits.append((key, val))
                self.sig.add((key[1], val))
            else:
                v = self.dcount[key[1]]
                if key == ev[0]:
                    v -= 16
                if v > 0:
                    waits.append((key, v))
        self.ops.append((eng, idx, fn, waits, dsem))
        for b in w:
            b.w = ev
            b.rs = {}
        for b in r:
            if b.const or b in w:
                continue
            b.rs[ev[0]] = ev[1]

    def emit(self, nc, es):
        handles = {"pe": nc.tensor, "act": nc.scalar, "dve": nc.vector, "pool": nc.gpsimd, "sp": nc.sync}
        sems = {e: es.enter_context(nc.semaphore("s_" + e)) for e in self.ENG}
        dsems = {n: es.enter_context(nc.semaphore("d_" + n)) for n in self.dcount}
        rank = {}
        for e in self.ENG:
            ids = sorted(i for (en, i) in self.sig if en == e)
            for k, i in enumerate(ids):
                rank[(e, i)] = k + 1
        seen = {e: {} for e in self.ENG}
        for (eng, idx, fn, waits, dsem) in self.ops:
            h = handles[eng]
            for key, val in waits:
                if key[0] == "e":
                    v = rank[(key[1], val)]
                    sem = sems[key[1]]
                else:
                    v = val
                    sem = dsems[key[1]]
                if seen[eng].get(key, 0) >= v:
                    continue
                seen[eng][key] = v
                h.wait_ge(sem, v)
            ins = fn(h)
            if dsem is not None:
                ins.then_inc(dsems[dsem], 16)
            elif (eng, idx) in self.sig:
                ins.then_inc(sems[eng], 1)


def _gammas():
    h = np.arange(RH, dtype=np.float64)
    return 1.0 - np.exp2(-(5.0 + h))


def _ref_decay(chunk):
    try:
        import jax
        import jax.numpy as jnp
        with jax.default_device(jax.devices("cpu")[0]):
            h = RH
            log_gamma = jnp.log1p(-jnp.exp2(-(5.0 + jnp.arange(h, dtype=jnp.float32))))
            idx = jnp.arange(chunk, dtype=jnp.float32)
            diff = idx[:, None] - idx[None, :]
            intra = jnp.where(diff >= 0, jnp.exp(log_gamma[:, None, None] * jnp.maximum(diff, 0.0)), 0.0)
            q_decay = jnp.exp(log_gamma[None, :] * (idx[:, None] + 1.0))
            k_decay = jnp.exp(log_gamma[None, :] * (chunk - 1.0 - idx[:, None]))
            chunk_decay = jnp.exp(log_gamma * chunk)
            return (np.asarray(intra, np.float32), np.asarray(q_decay, np.float32),
                    np.asarray(k_decay, np.float32), np.asarray(chunk_decay, np.float32))
    except Exception:
        f = np.float32
        log_gamma = np.log1p(-np.exp2(-(f(5.0) + np.arange(RH, dtype=f)))).astype(f)
        idx = np.arange(chunk, dtype=f)
        diff = idx[:, None] - idx[None, :]
        intra = np.where(diff >= 0, np.exp(log_gamma[:, None, None] * np.maximum(diff, f(0.0))), f(0.0)).astype(f)
        q_decay = np.exp(log_gamma[None, :] * (idx[:, None] + f(1.0))).astype(f)
        k_decay = np.exp(log_gamma[None, :] * (f(chunk) - f(1.0) - idx[:, None])).astype(f)
        chunk_decay = np.exp(log_gamma * f(chunk)).astype(f)
        return intra, q_decay, k_decay, chunk_decay


def _ref_rope():
    try:
        import jax
        import jax.numpy as jnp
        with jax.default_device(jax.devices("cpu")[0]):
            half = 128
            inv = 1.0 / (10000.0 ** (jnp.arange(half, dtype=jnp.float32) / half))
            pos = jnp.concatenate([jnp.arange(SEQ), PAST + (jnp.arange(NS) % DS)])
            ang = pos.astype(jnp.float32)[:, None] * inv[None, :]
            return np.asarray(jnp.cos(ang), np.float32).T, np.asarray(jnp.sin(ang), np.float32).T
    except Exception:
        f = np.float32
        half = 128
        inv = (f(1.0) / (f(10000.0) ** (np.arange(half, dtype=f) / f(half)))).astype(f)
        pos = np.concatenate([np.arange(SEQ), PAST + (np.arange(NS) % DS)]).astype(f)
        ang = (pos[:, None] * inv[None, :]).astype(f).astype(np.float64)
        return np.cos(ang).astype(f).T, np.sin(ang).astype(f).T


_CONST_CACHE = {}


def _consts():
    if "c" in _CONST_CACHE:
        return _CONST_CACHE["c"]
    i = np.arange(128)
    intra_p, qdec_p, kdec_p, cd_p = _ref_decay(128)
    intra_s, qdec_s, kdec_s, cd_s = _ref_decay(DS)
    sixteenth = np.float32(1.0 / 16.0)
    mt = np.zeros((128, RH, 128), np.float32)
    mts = np.zeros((128, RH, 128), np.float32)
    same = (i[:, None] // DS) == (i[None, :] // DS)
    difs = (i[None, :] % DS) - (i[:, None] % DS)
    for h in range(RH):
        mt[:, h, :] = intra_p[h].T * sixteenth
        blk = intra_s[h][(i[None, :] % DS), (i[:, None] % DS)]
        mts[:, h, :] = np.where(same, blk, 0.0) * sixteenth
    qd = np.zeros((128, RH, 128), np.float32)
    qds = np.zeros((128, RH, 128), np.float32)
    kd = np.zeros((128, RH), np.float32)
    kds = np.zeros((128, RH), np.float32)
    for h in range(RH):
        qd[:, h, :] = qdec_p[:, h][None, :]
        qds[:, h, :] = qdec_s[i % DS, h][None, :]
        kd[:, h] = kdec_p[:, h] * sixteenth
        kds[:, h] = kdec_s[i % DS, h] * sixteenth
    kzm = ((i[:, None] // DS) == np.arange(NB)[None, :]).astype(np.float32)
    ident = np.eye(128, dtype=np.float32)
    cf32 = np.concatenate([mt.reshape(128, -1), mts.reshape(128, -1), qd.reshape(128, -1), qds.reshape(128, -1),
                           kd, kds, kzm, ident], axis=1).astype(np.float32)
    ones = np.ones((128, 128))
    onesz = np.zeros((128, 2, 128))
    onesz[:, 0, :64] = 1.0
    onesz[:, 1, 64:] = 1.0
    m_cur = (i[:, None] <= i[None, :]).astype(np.float64)
    m_prev = (i[:, None] > i[None, :]).astype(np.float64)
    m_new = (same & (difs >= 0)).astype(np.float64)
    m_cache = np.zeros((128, 4, DS))
    for t in range(DS):
        m_cache[:, :, t] = (i > t)[:, None]
    blockmask = np.zeros((128, NB, 128))
    for b in range(NB):
        blockmask[:, b, b * DS:(b + 1) * DS] = 1.0
    cb = np.concatenate([ident, ones, onesz.reshape(128, -1), m_cur, m_prev, m_new, m_cache.reshape(128, -1),
                         blockmask.reshape(128, -1)], axis=1).astype(np.float32)
    cosT, sinT = _ref_rope()
    rope = np.stack([cosT, sinT]).astype(np.float32)
    _CONST_CACHE["c"] = (cf32, cb, rope, (cd_p, cd_s))
    return _CONST_CACHE["c"]


CF_MT = 0
CF_MTS = 512
CF_QD = 1024
CF_QDS = 1536
CF_KD = 2048
CF_KDS = 2052
CF_KZM = 2056
CF_ID = 2072
CF_N = 2200
CB_ID = 0
CB_ONES = 128
CB_ONESZ = 256
CB_MCUR = 512
CB_MPREV = 640
CB_MNEW = 768
CB_MCACHE = 896
CB_BLK = 928
CB_N = 928 + 2048
SP_G = 0
SP_BQ = 40
SP_BKZ = 48
SP_BK = 52
SP_BV = 53
SP_BO = 54
SP_SINK = 62
SP_N = 70


def build_program():
    nc = bass.Bass("TRN2", target_bir_lowering=False)
    S = Sched()

    def din(name, shape):
        return nc.dram_tensor(name, list(shape), F32, kind="ExternalInput").ap()

    def dout(name, shape):
        return nc.dram_tensor(name, list(shape), F32, kind="ExternalOutput").ap()

    xTp = din("xTp", [D, SEQ])
    xTs = din("xTs", [D, NS])
    st_in = din("st_in", [NB, RH, 256, 512])
    ck = din("ck", [NB, 128, 128])
    cv = din("cv", [NB, 128, 128])
    wqk = din("wqk", [D, RH * 512])
    wv = din("wv", [D, 2048])
    wg = din("wg", [D, 2048])
    wo = din("wo", [2048, D])
    wq1 = din("wq1", [D, 1024])
    wkz = din("wkz", [D, 512])
    wkv = din("wkv", [D, 256])
    wo1 = din("wo1", [D, D])
    w1 = din("w1", [2, D, DFF])
    w3 = din("w3", [2, D, DFF])
    w2 = din("w2", [2, DFF, D])
    spar = din("spar", [128, SP_N])
    cf_d = din("cf32", [128, CF_N])
    cb_d = din("cb", [128, CB_N])
    rope_d = din("rope", [2, 128, SEQ + NS])

    yTp = dout("yTp", [D, SEQ])
    yTs = dout("yTs", [D, NS])
    srp = dout("srp", [RH, 256, 512])
    srs = dout("srs", [NB, RH, 256, 512])
    kp_o = dout("kp", [128, 128])
    vp_o = dout("vp", [128, 128])
    ks_o = dout("ks", [NB, 128, 128])
    vs_o = dout("vs", [NB, 128, 128])
    if DEBUG:
        dbg_o = dout("dbg", [4, D, TT])

    es = contextlib.ExitStack()
    with es:
        def sb(name, shape, dt):
            return es.enter_context(nc.sbuf_tensor(name, list(shape), dt))

        class Ctx:
            pass

        hT_p = sb("hT", [128, KC, TT], BF16)
        hb_p = [Buf(f"h{k}") for k in range(KC)]
        pcx = []
        for i in range(2):
            c_ = Ctx()
            c_.xT = sb(f"xT{i}", [128, KC, TT], F32)
            c_.xb = [Buf(f"x{i}_{k}") for k in range(KC)]
            c_.hT = hT_p
            c_.hb = hb_p
            c_.NT = TT
            pcx.append(c_)
        scx = Ctx()
        scx.xT = sb("xTs_sb", [128, KC, NS], F32)
        scx.xb = [Buf(f"xs{k}") for k in range(KC)]
        scx.hT = sb("hTs_sb", [128, KC, NS], BF16)
        scx.hb = [Buf(f"hs{k}") for k in range(KC)]
        scx.NT = NS
        sq = sb("sq", [128, 2, TT], BF16)
        sqb = [Buf("sq0"), Buf("sq1")]
        rtmp = sb("rtmp", [128, TT], F32)
        rtmpb = Buf("rtmp")
        rstd = sb("rstd", [128, TT], F32)
        rstdb = Buf("rstd")
        wring = [sb(f"wr{i}", [128, WSLOT], BF16) for i in range(NW)]
        wrb = [Buf(f"wr{i}") for i in range(NW)]
        rope = [sb(f"rope{i}", [128, 2, TT], F32) for i in range(2)]
        ropeb = [Buf("rope0"), Buf("rope1")]
        cf = sb("cf", [128, CF_N], F32)
        cfb = Buf("cf", const=True)
        cb = sb("cbt", [128, CB_N], BF16)
        cbb = Buf("cb", const=True)
        sp = sb("sp", [128, SP_N], F32)
        spb = Buf("sp", const=True)
        esink = sb("esink", [128, 8], F32)
        esinkb = Buf("esink", const=True)
        Sst = sb("Sst", [128, RH, 2, 512], F32)
        Sbf = sb("Sbf", [128, RH, 2, 512], BF16)
        Sb = [[Buf(f"S{h}{a}") for a in range(2)] for h in range(RH)]
        Sbfb = [[Buf(f"Sbf{h}{a}") for a in range(2)] for h in range(RH)]
        ARENA_COLS = 34112
        arena = sb("arena", [128, ARENA_COLS], BF16)
        stat = sb("stat", [128, 4, 16], F32)
        statb = [Buf(f"stat{i}") for i in range(4)]
        kz = sb("kz", [128, 4, 128 + TT], BF16)
        kzb = [Buf(f"kz{i}") for i in range(5)]
        vz = sb("vz", [128, 5, 4, 128], BF16)
        vzb = [Buf(f"vz{i}") for i in range(5)]

        banks = [es.enter_context(nc.psum_tensor(f"pb{i}", [128, 512], F32)) for i in range(8)]
        bankb = [Buf(f"bank{i}", excl=True) for i in range(8)]
        held = [False] * 8
        rr = [0]

        def pb(hold=False):
            for _ in range(16):
                i = rr[0] % 8
                rr[0] += 1
                if not held[i]:
                    if hold:
                        held[i] = True
                    return banks[i], bankb[i], i
            raise RuntimeError("no psum bank")

        def release(i):
            held[i] = False

        class Arena:
            def __init__(self):
                self.off = 0
                self.cur = []
                self.pending = {}

            def reset(self):
                for b in self.cur:
                    if b.w is not None:
                        k, v = b.w
                        if self.pending.get(k, -1) < v:
                            self.pending[k] = v
                    for k, v in b.rs.items():
                        if self.pending.get(k, -1) < v:
                            self.pending[k] = v
                self.cur = []
                self.off = 0

            def buf(self, name):
                b = Buf(name, pending=self.pending)
                self.cur.append(b)
                return b

            def alloc(self, name, shape, dt):
                n = 1
                for s in shape[1:]:
                    n *= s
                nb = n * (2 if dt == F32 else 1)
                nb = (nb + 15) // 16 * 16
                assert self.off + nb <= ARENA_COLS, (name, self.off, nb)
                v = arena[:, self.off:self.off + nb]
                self.off += nb
                if dt == F32:
                    v = v.bitcast(F32)[:, :n]
                else:
                    v = v[:, :n]
                if len(shape) == 3:
                    v = v.rearrange("p (a b) -> p a b", a=shape[1])
                elif len(shape) == 4:
                    v = v.rearrange("p (a b c) -> p a b c", a=shape[1], b=shape[2])
                return v

        A = Arena()

        def mm(out, lhsT, rhs, start, stop, r, w):
            S.op("pe", lambda e: e.matmul(out, lhsT, rhs, start=start, stop=stop), r, w)

        def tr(out, in_, ident, r, w):
            S.op("pe", lambda e: e.transpose(out, in_, ident), r, w)

        def act(out, in_, func, r, w, bias=0.0, scale=1.0):
            S.op("act", lambda e: e.activation(out=out, in_=in_, func=func, bias=bias, scale=scale), r, w)

        def tt(out, in0, in1, op, r, w, eng="dve"):
            S.op(eng, lambda e: e.tensor_tensor(out=out, in0=in0, in1=in1, op=op), r, w)

        def stt(out, in0, scalar, in1, op0, op1, r, w):
            S.op("dve", lambda e: e.scalar_tensor_tensor(out=out, in0=in0, scalar=scalar, in1=in1, op0=op0, op1=op1), r, w)

        def ts(out, in0, s1, s2, op0, op1, r, w):
            S.op("dve", lambda e: e.tensor_scalar(out=out, in0=in0, scalar1=s1, scalar2=s2, op0=op0, op1=op1), r, w)

        def recip(out, in_, r, w):
            S.op("dve", lambda e: e.reciprocal(out=out, in_=in_), r, w)

        def dma(eng, out, in_, r, w, sem):
            S.op(eng, lambda e: e.dma_start(out=out, in_=in_), r, w, dsem=sem)

        def memset(eng, ap, val, w):
            S.op(eng, lambda e: e.memset(ap, val), (), w)

        def wview(ap2d, ncols):
            return ap2d.rearrange("(kc p) n -> p kc n", p=128), ncols

        units = {}
        wqk_v = wqk.rearrange("(kc p) n -> p kc n", p=128)
        wv_v = wv.rearrange("(kc p) n -> p kc n", p=128)
        wg_v = wg.rearrange("(kc p) n -> p kc n", p=128)
        wo_v = wo.rearrange("(kc p) n -> p kc n", p=128)
        wq1_v = wq1.rearrange("(kc p) n -> p kc n", p=128)
        wkz_v = wkz.rearrange("(kc p) n -> p kc n", p=128)
        wkv_v = wkv.rearrange("(kc p) n -> p kc n", p=128)
        wo1_v = wo1.rearrange("(kc p) n -> p kc n", p=128)
        for h in range(RH):
            units[("qk", h)] = (wqk_v[:, :, h * 512:(h + 1) * 512], (8, 512))
            units[("v", h)] = (wv_v[:, :, h * 512:(h + 1) * 512], (8, 512))
            units[("g", h)] = (wg_v[:, :, h * 512:(h + 1) * 512], (8, 512))
        for half in range(2):
            for ecg in range(2):
                units[("wo", half, ecg)] = (wo_v[:, ecg * 8:(ecg + 1) * 8, half * 512:(half + 1) * 512], (8, 512))
            units[("q1", half)] = (wq1_v[:, :, half * 512:(half + 1) * 512], (8, 512))
            units[("wo1", half)] = (wo1_v[:, :, half * 512:(half + 1) * 512], (8, 512))
        units[("kz",)] = (wkz_v, (8, 512))
        units[("kv",)] = (wkv_v, (8, 256))
        FG = [(0, 4), (4, 4), (8, 4), (12, 4), (16, 4), (20, 2)]
        UG = [(0, 8), (8, 8), (16, 6)]
        for l in range(2):
            w1_v = w1[l].rearrange("(kc p) n -> p kc n", p=128)
            w3_v = w3[l].rearrange("(kc p) n -> p kc n", p=128)
            w2_v = w2[l].rearrange("(f p) n -> p f n", p=128)
            for gi, (f0, nf) in enumerate(FG):
                units[("w1", l, gi)] = (w1_v[:, :, f0 * 128:(f0 + nf) * 128], (8, nf * 128))
                units[("w3", l, gi)] = (w3_v[:, :, f0 * 128:(f0 + nf) * 128], (8, nf * 128))
            for half in range(2):
                for ui, (f0, nf) in enumerate(UG):
                    units[("w2", l, half, ui)] = (w2_v[:, f0:f0 + nf, half * 512:(half + 1) * 512], (nf, 512))

        plan = []
        for t in range(NPT):
            nrep = 2 if t == NPT - 1 else 1
            for rep in range(nrep):
                if rep == 0:
                    for h in range(RH):
                        plan += [("qk", h), ("v", h), ("g", h)]
                plan += [("wo", 0, 0), ("wo", 0, 1), ("wo", 1, 0), ("wo", 1, 1)]
            for l in range(2):
                if l == 1:
                    for _ in range(nrep):
                        plan += [("q1", 0), ("q1", 1), ("kz",), ("kv",), ("wo1", 0), ("wo1", 1)]
                for gi in range(len(FG)):
                    plan += [("w1", l, gi), ("w3", l, gi)]
                for half in range(2):
                    for ui in range(len(UG)):
                        plan.append(("w2", l, half, ui))
        wstate = {"k": 0, "loaded": 0}

        def slot_view(slot, shp):
            a, b = shp
            return wring[slot][:, :a * b].rearrange("p (a b) -> p a b", a=a)

        def wget(key):
            u = wstate["k"]
            assert plan[u] == key, (u, plan[u], key)
            wstate["k"] += 1
            while wstate["loaded"] < min(len(plan), u + NW - 1):
                j = wstate["loaded"]
                src, shp = units[plan[j]]
                sl = j % NW
                dma("pool", slot_view(sl, shp), src, (), [wrb[sl]], f"w{sl}")
                wstate["loaded"] += 1
            src, shp = units[key]
            return slot_view(u % NW, shp), wrb[u % NW]

        dma("sp", cf[:, :], cf_d[:, :], (), [cfb], "c0")
        dma("sp", sp[:, :], spar[:, :], (), [spb], "c0")
        dma("pool", cb[:, :], cb_d[:, :], (), [cbb], "c1")
        act(esink[:, :], sp[:, SP_SINK:SP_SINK + 8], AF.Exp, [spb], [esinkb])
        for h in range(RH):
            for a in range(2):
                memset("dve", Sst[:, h, a, :], 0.0, [Sb[h][a]])
                memset("dve", Sbf[:, h, a, :], 0.0, [Sbfb[h][a]])
        for i in range(5):
            memset("dve", vz[:, i, :, :], 0.0, [vzb[i]])

        ident_bf = cb[:, CB_ID:CB_ID + 128]
        ones_bf = cb[:, CB_ONES:CB_ONES + 128]
        ident_f = cf[:, CF_ID:CF_ID + 128]

        def gain(gi, kc):
            return sp[:, SP_G + gi * 8 + kc:SP_G + gi * 8 + kc + 1]

        def rmsnorm(cx, gi, to_x=False):
            NT = cx.NT
            bk, bb, _ = pb()
            for kc in range(KC):
                s = kc % 2
                act(sq[:, s, :NT], cx.xT[:, kc, :NT], AF.Square, [cx.xb[kc]], [sqb[s]])
                mm(bk[:, :NT], ones_bf, sq[:, s, :NT], kc == 0, kc == KC - 1, [sqb[s], cbb], [bb])
            act(rtmp[:, :NT], bk[:, :NT], AF.Sqrt, [bb], [rtmpb], bias=EPS, scale=1.0 / D)
            recip(rstd[:, :NT], rtmp[:, :NT], [rtmpb], [rstdb])
            for kc in range(KC):
                if to_x:
                    stt(cx.xT[:, kc, :NT], cx.xT[:, kc, :NT], gain(gi, kc), rstd[:, :NT], ALU.mult, ALU.mult,
                        [cx.xb[kc], rstdb, spb], [cx.xb[kc]])
                else:
                    stt(cx.hT[:, kc, :NT], cx.xT[:, kc, :NT], gain(gi, kc), rstd[:, :NT], ALU.mult, ALU.mult,
                        [cx.xb[kc], rstdb, spb], [cx.hb[kc]])

        def ffn(cxs, l):
            A.reset()
            aTs, abs_, s1s, s1bs = [], [], [], []
            for ci, cx in enumerate(cxs):
                aTs.append(A.alloc(f"aT{ci}", [128, NF, cx.NT], BF16))
                abs_.append([A.buf(f"a{f}") for f in range(NF)])
                s1s.append(A.alloc(f"s1{ci}", [128, 2, cx.NT], F32))
                s1bs.append([A.buf("s1a"), A.buf("s1b")])
            for gi, (f0, nf) in enumerate(FG):
                w1u, w1b_ = wget(("w1", l, gi))
                w3u, w3b_ = wget(("w3", l, gi))
                for fi in range(nf):
                    f = f0 + fi
                    for ci, cx in enumerate(cxs):
                        NT = cx.NT
                        aT, ab, s1, s1b = aTs[ci], abs_[ci], s1s[ci], s1bs[ci]
                        b1, bb1, _ = pb()
                        b3, bb3, _ = pb()
                        for kc in range(KC):
                            mm(b1[:, :NT], w1u[:, kc, fi * 128:(fi + 1) * 128], cx.hT[:, kc, :NT], kc == 0, kc == KC - 1,
                               [w1b_, cx.hb[kc]], [bb1])
                        for kc in range(KC):
                            mm(b3[:, :NT], w3u[:, kc, fi * 128:(fi + 1) * 128], cx.hT[:, kc, :NT], kc == 0, kc == KC - 1,
                               [w3b_, cx.hb[kc]], [bb3])
                        act(s1[:, f % 2, :NT], b1[:, :NT], AF.Silu, [bb1], [s1b[f % 2]])
                        tt(aT[:, f, :NT], s1[:, f % 2, :NT], b3[:, :NT], ALU.mult, [s1b[f % 2], bb3], [ab[f]])
            for half in range(2):
                bss = [[pb(hold=True) for _ in range(4)] for _ in cxs]
                for ui, (f0, nf) in enumerate(UG):
                    w2u, w2b_ = wget(("w2", l, half, ui))
                    for dm in range(4):
                        for fi in range(nf):
                            f = f0 + fi
                            for ci, cx in enumerate(cxs):
                                mm(bss[ci][dm][0][:, :cx.NT], w2u[:, fi, dm * 128:(dm + 1) * 128], aTs[ci][:, f, :cx.NT],
                                   f == 0, f == NF - 1, [w2b_, abs_[ci][f]], [bss[ci][dm][1]])
                for ci, cx in enumerate(cxs):
                    NT = cx.NT
                    for dm in range(4):
                        kc = half * 4 + dm
                        tt(cx.xT[:, kc, :NT], bss[ci][dm][0][:, :NT], cx.xT[:, kc, :NT], ALU.add, [bss[ci][dm][1], cx.xb[kc]], [cx.xb[kc]])
                        release(bss[ci][dm][2])

        def gn_gate(ob, obb, g_ap, gbuf, u_ap, ubuf, on_ap, onbuf, si):
            st = stat[:, si, :]
            sbf = statb[si]
            S.op("dve", lambda e: e.bn_stats(out=st[:, 0:6], in_=ob), [obb], [sbf])
            S.op("dve", lambda e: e.bn_aggr(out=st[:, 6:8], in_=st[:, 0:6]), [sbf], [sbf])
            act(st[:, 8:9], st[:, 7:8], AF.Sqrt, [sbf], [sbf], bias=EPS, scale=1.0)
            recip(st[:, 9:10], st[:, 8:9], [sbf], [sbf])
            stt(st[:, 10:11], st[:, 6:7], -1.0, st[:, 9:10], ALU.mult, ALU.mult, [sbf], [sbf])
            act(on_ap, ob, AF.Identity, [obb, sbf], [onbuf], bias=st[:, 10:11], scale=st[:, 9:10])
            tt(u_ap, on_ap, g_ap, ALU.mult, [onbuf, gbuf], [ubuf])

        def layer0(cx, tile_i, sample, rope_i, rider=None, pre=None):
            NT = cx.NT
            nch = NT // 128
            A.reset()
            if pre is None:
                qT = [A.alloc(f"qT{i}", [128, 2, NT], BF16) for i in range(2)]
                qdT = [A.alloc(f"qdT{i}", [128, 2, NT], BF16) for i in range(2)]
                kT = [A.alloc(f"kT{i}", [128, 2, NT], BF16) for i in range(2)]
                qTb = [[A.buf("qT") for _ in range(2)] for _ in range(2)]
                qdTb = [[A.buf("qdT") for _ in range(2)] for _ in range(2)]
                kTb = [[A.buf("kT") for _ in range(2)] for _ in range(2)]
                kd = [A.alloc(f"kd{i}", [128, nch, 256], BF16) for i in range(2)]
                kdb = [[A.buf("kd") for _ in range(nch)] for _ in range(2)]
                vv = [A.alloc(f"v{i}", [128, nch, 512], BF16) for i in range(2)]
                vb = [[A.buf("v") for _ in range(nch)] for _ in range(2)]
                gg = [A.alloc(f"g{i}", [128, nch, 512], BF16) for i in range(2)]
                gb = [[A.buf("g") for _ in range(nch)] for _ in range(2)]
            else:
                qT, qdT, kT, kd, vv, gg = pre["qT"], pre["qdT"], pre["kT"], pre["kd"], pre["v"], pre["g"]
                qTb, qdTb, kTb, kdb, vb, gb = pre["qTb"], pre["qdTb"], pre["kTb"], pre["kdb"], pre["vb"], pre["gb"]
            npar = len(qT)
            R = None
            if rider is not None:
                rcx, r_rope, store, store_bufs = rider
                pend = {}
                for b_ in store_bufs:
                    if b_.w is not None and pend.get(b_.w[0], -1) < b_.w[1]:
                        pend[b_.w[0]] = b_.w[1]
                    for k_, v_ in b_.rs.items():
                        if pend.get(k_, -1) < v_:
                            pend[k_] = v_
                flat = store.rearrange("p a b -> p (a b)").bitcast(BF16)
                R = {k_: [] for k_ in ("qT", "qdT", "kT", "kd", "v", "g", "qTb", "qdTb", "kTb", "kdb", "vb", "gb")}
                for h_ in range(RH):
                    o_ = h_ * 2048
                    R["qT"].append(flat[:, o_:o_ + 256].rearrange("p (a t) -> p a t", a=2))
                    R["qdT"].append(flat[:, o_ + 256:o_ + 512].rearrange("p (a t) -> p a t", a=2))
                    R["kT"].append(flat[:, o_ + 512:o_ + 768].rearrange("p (a t) -> p a t", a=2))
                    R["kd"].append(flat[:, o_ + 768:o_ + 1024].rearrange("p (c d) -> p c d", c=1))
                    R["v"].append(flat[:, o_ + 1024:o_ + 1536].rearrange("p (c d) -> p c d", c=1))
                    R["g"].append(flat[:, o_ + 1536:o_ + 2048].rearrange("p (c d) -> p c d", c=1))
                    R["qTb"].append([Buf("rqT", pending=pend) for _ in range(2)])
                    R["qdTb"].append([Buf("rqdT", pending=pend) for _ in range(2)])
                    R["kTb"].append([Buf("rkT", pending=pend) for _ in range(2)])
                    R["kdb"].append([Buf("rkd", pending=pend)])
                    R["vb"].append([Buf("rv", pending=pend)])
                    R["gb"].append([Buf("rg", pending=pend)])
                r_cos = rope[r_rope][:, 0, :NS]
                r_sin = rope[r_rope][:, 1, :NS]
                r_rpb = ropeb[r_rope]
            if pre is None:
                tmp = A.alloc("rt", [128, 4, NT], F32)
                tmpb = [A.buf(f"rt{i}") for i in range(4)]
                tq = A.alloc("tq", [128, 2, NT], F32)
                tqb = [A.buf("tq0"), A.buf("tq1")]
            uT = A.alloc("uT", [128, 16, NT], BF16)
            uTb = [[A.buf("uT") for _ in range(nch)] for _ in range(RH)]
            on = A.alloc("on", [128, 2, 512], F32)
            onb = [A.buf("on0"), A.buf("on1")]
            uu = A.alloc("u", [128, 2, 512], BF16)
            ub = [A.buf("u0"), A.buf("u1")]
            sTm = A.alloc("sTm", [128, 2, 128], BF16)
            sTmb = [A.buf("sTm0"), A.buf("sTm1")]
            if sample:
                NSR = 4
                S0 = [A.alloc(f"S0{i}", [128, 2, 512], F32) for i in range(NSR)]
                S0b = [A.buf("S0") for i in range(NSR)]
                S0bf = [A.alloc(f"S0bf{i}", [128, 2, 512], BF16) for i in range(3)]
                S0bfb = [A.buf("S0bf") for i in range(3)]
                Sn = [A.alloc(f"Sn{i}", [128, 2, 512], F32) for i in range(NSR)]
                Snb = [A.buf("Sn") for i in range(NSR)]
                Qz = A.alloc("Qz", [128, 2, NB, 128], BF16)
                Qzb = A.buf("Qz")
                KZ = A.alloc("KZ", [128, 2, NB, 128], BF16)
                KZb = A.buf("KZ")
            rp = rope[rope_i]
            rpb = ropeb[rope_i]
            cosv = rp[:, 0, :NT]
            sinv = rp[:, 1, :NT]
            mt_off = CF_MTS if sample else CF_MT
            qd_off = CF_QDS if sample else CF_QD
            kd_off = CF_KDS if sample else CF_KD
            gam = _gammas()
            sidx = [0]
            s0_issued = [0]

            def issue_s0(u):
                if u >= RH * NB or u < s0_issued[0]:
                    return
                assert u == s0_issued[0]
                s0_issued[0] += 1
                dma("sp", S0[u % 4][:, :, :], st_in[u % NB, u // NB].rearrange("(a p) e -> p a e", p=128), (), [S0b[u % 4]], f"s0{u % 4}")

            if sample:
                for u_ in range(4):
                    issue_s0(u_)
            wun = {}

            def p_qk(hd, which):
                def f():
                    p = hd % npar
                    if which == 0:
                        wun[hd] = wget(("qk", hd))
                    wu, wub = wun[hd]
                    base = which * 256
                    b1, bb1, _ = pb()
                    b2, bb2, _ = pb()
                    for kc in range(KC):
                        mm(b1[:, :NT], wu[:, kc, base:base + 128], cx.hT[:, kc, :NT], kc == 0, kc == KC - 1, [wub, cx.hb[kc]], [bb1])
                    for kc in range(KC):
                        mm(b2[:, :NT], wu[:, kc, base + 128:base + 256], cx.hT[:, kc, :NT], kc == 0, kc == KC - 1, [wub, cx.hb[kc]], [bb2])
                    tt(tmp[:, 0, :], b1[:, :NT], cosv, ALU.mult, [bb1, rpb], [tmpb[0]])
                    tt(tmp[:, 1, :], b2[:, :NT], sinv, ALU.mult, [bb2, rpb], [tmpb[1]])
                    tt(tmp[:, 2, :], b1[:, :NT], sinv, ALU.mult, [bb1, rpb], [tmpb[2]])
                    tt(tmp[:, 3, :], b2[:, :NT], cosv, ALU.mult, [bb2, rpb], [tmpb[3]])
                    if which == 0:
                        tt(tq[:, 0, :], tmp[:, 0, :], tmp[:, 1, :], ALU.subtract, [tmpb[0], tmpb[1]], [tqb[0]])
                        tt(tq[:, 1, :], tmp[:, 2, :], tmp[:, 3, :], ALU.add, [tmpb[2], tmpb[3]], [tqb[1]])
                        for a in range(2):
                            act(qT[p][:, a, :], tq[:, a, :], AF.Copy, [tqb[a]], [qTb[p][a]])
                            qdv = cf[:, qd_off + hd * 128:qd_off + (hd + 1) * 128].unsqueeze(1).broadcast_to([128, nch, 128])
                            tt(qdT[p][:, a, :].rearrange("p (c i) -> p c i", c=nch),
                               tq[:, a, :].rearrange("p (c i) -> p c i", c=nch), qdv, ALU.mult,
                               [tqb[a], cfb], [qdTb[p][a]])
                    else:
                        tt(kT[p][:, 0, :], tmp[:, 0, :], tmp[:, 1, :], ALU.subtract, [tmpb[0], tmpb[1]], [kTb[p][0]])
                        tt(kT[p][:, 1, :], tmp[:, 2, :], tmp[:, 3, :], ALU.add, [tmpb[2], tmpb[3]], [kTb[p][1]])
                    if R is not None:
                        b1, bb1, _ = pb()
                        b2, bb2, _ = pb()
                        for kc in range(KC):
                            mm(b1[:, :NS], wu[:, kc, base:base + 128], rcx.hT[:, kc, :NS], kc == 0, kc == KC - 1, [wub, rcx.hb[kc]], [bb1])
                        for kc in range(KC):
                            mm(b2[:, :NS], wu[:, kc, base + 128:base + 256], rcx.hT[:, kc, :NS], kc == 0, kc == KC - 1, [wub, rcx.hb[kc]], [bb2])
                        tt(tmp[:, 0, :NS], b1[:, :NS], r_cos, ALU.mult, [bb1, r_rpb], [tmpb[0]])
                        tt(tmp[:, 1, :NS], b2[:, :NS], r_sin, ALU.mult, [bb2, r_rpb], [tmpb[1]])
                        tt(tmp[:, 2, :NS], b1[:, :NS], r_sin, ALU.mult, [bb1, r_rpb], [tmpb[2]])
                        tt(tmp[:, 3, :NS], b2[:, :NS], r_cos, ALU.mult, [bb2, r_rpb], [tmpb[3]])
                        if which == 0:
                            tt(tq[:, 0, :NS], tmp[:, 0, :NS], tmp[:, 1, :NS], ALU.subtract, [tmpb[0], tmpb[1]], [tqb[0]])
                            tt(tq[:, 1, :NS], tmp[:, 2, :NS], tmp[:, 3, :NS], ALU.add, [tmpb[2], tmpb[3]], [tqb[1]])
                            for a in range(2):
                                act(R["qT"][hd][:, a, :], tq[:, a, :NS], AF.Copy, [tqb[a]], [R["qTb"][hd][a]])
                                tt(R["qdT"][hd][:, a, :], tq[:, a, :NS], cf[:, CF_QDS + hd * 128:CF_QDS + (hd + 1) * 128], ALU.mult,
                                   [tqb[a], cfb], [R["qdTb"][hd][a]])
                        else:
                            tt(R["kT"][hd][:, 0, :], tmp[:, 0, :NS], tmp[:, 1, :NS], ALU.subtract, [tmpb[0], tmpb[1]], [R["kTb"][hd][0]])
                            tt(R["kT"][hd][:, 1, :], tmp[:, 2, :NS], tmp[:, 3, :NS], ALU.add, [tmpb[2], tmpb[3]], [R["kTb"][hd][1]])
                return f

            def p_vg(hd, kind):
                def f():
                    p = hd % npar
                    wu_, wb_ = wget((kind, hd))
                    for c in range(nch):
                        bk, bb, _ = pb()
                        for kc in range(KC):
                            mm(bk[:, :], cx.hT[:, kc, c * 128:(c + 1) * 128], wu_[:, kc, :], kc == 0, kc == KC - 1, [wb_, cx.hb[kc]], [bb])
                        if kind == "v":
                            act(vv[p][:, c, :], bk[:, :], AF.Copy, [bb], [vb[p][c]])
                        else:
                            act(gg[p][:, c, :], bk[:, :], AF.Silu, [bb], [gb[p][c]])
                    if R is not None:
                        bk, bb, _ = pb()
                        for kc in range(KC):
                            mm(bk[:, :], rcx.hT[:, kc, 0:NS], wu_[:, kc, :], kc == 0, kc == KC - 1, [wb_, rcx.hb[kc]], [bb])
                        if kind == "v":
                            act(R["v"][hd][:, 0, :], bk[:, :], AF.Copy, [bb], [R["vb"][hd][0]])
                        else:
                            act(R["g"][hd][:, 0, :], bk[:, :], AF.Silu, [bb], [R["gb"][hd][0]])
                return f

            def p_kd(hd):
                def f():
                    p = hd % npar
                    for c in range(nch):
                        tb, tbb, _ = pb()
                        tbv = tb[:, :].bitcast(BF16)
                        for a in range(2):
                            tr(tbv[:, a * 128:(a + 1) * 128], kT[p][:, a, c * 128:(c + 1) * 128], ident_bf, [kTb[p][a], cbb], [tbb])
                        act(kd[p][:, c, :], tbv[:, 0:256], AF.Copy, [tbb, cfb], [kdb[p][c]],
                            scale=cf[:, kd_off + hd:kd_off + hd + 1])
                    if R is not None:
                        tb, tbb, _ = pb()
                        tbv = tb[:, :].bitcast(BF16)
                        for a in range(2):
                            tr(tbv[:, a * 128:(a + 1) * 128], R["kT"][hd][:, a, :], ident_bf, [R["kTb"][hd][a], cbb], [tbb])
                        act(R["kd"][hd][:, 0, :], tbv[:, 0:256], AF.Copy, [tbb, cfb], [R["kdb"][hd][0]],
                            scale=cf[:, CF_KDS + hd:CF_KDS + hd + 1])
                return f

            def proj_pieces(hd):
                if pre is not None:
                    return []
                return [p_qk(hd, 0), p_qk(hd, 1), p_vg(hd, "v"), p_vg(hd, "g"), p_kd(hd)]

            cst = {}

            def c_main(hd, c):
                p = hd % npar
                cd = float(_consts()[3][1 if sample else 0][hd])
                cs = slice(c * 128, (c + 1) * 128)
                if sample:
                    for a in range(2):
                        tt(Qz[:, a, :, :], qdT[p][:, a, :].unsqueeze(1).broadcast_to([128, NB, 128]),
                           cb[:, CB_BLK:CB_BLK + NB * 128].rearrange("p (b i) -> p b i", b=NB), ALU.mult,
                           [qdTb[p][a], cbb], [Qzb])
                        tt(KZ[:, a, :, :], kd[p][:, 0, a * 128:(a + 1) * 128].unsqueeze(1).broadcast_to([128, NB, 128]),
                           cf[:, CF_KZM:CF_KZM + NB].unsqueeze(2).broadcast_to([128, NB, 128]), ALU.mult,
                           [kdb[p][0], cfb], [KZb])
                sbk, sbb, _ = pb()
                for a in range(2):
                    mm(sbk[:, :128], kT[p][:, a, cs], qT[p][:, a, cs], a == 0, a == 1, [kTb[p][a], qTb[p][a]], [sbb])
                si = sidx[0] % 2
                sidx[0] += 1
                tt(sTm[:, si, :], sbk[:, :128], cf[:, mt_off + hd * 128:mt_off + (hd + 1) * 128], ALU.mult,
                   [sbb, cfb], [sTmb[si]])
                if not sample:
                    pbs = []
                    for a in range(2):
                        pk, pkb, _ = pb()
                        mm(pk[:, :], kd[p][:, c, a * 128:(a + 1) * 128], vv[p][:, c, :], True, True, [kdb[p][c], vb[p][c]], [pkb])
                        pbs.append((pk, pkb))
                    ob, obb, _ = pb()
                    mm(ob[:, :], sTm[:, si, :], vv[p][:, c, :], True, False, [sTmb[si], vb[p][c]], [obb])
                    for a in range(2):
                        mm(ob[:, :], qdT[p][:, a, cs], Sbf[:, hd, a, :], False, a == 1, [qdTb[p][a], Sbfb[hd][a]], [obb])
                    for a in range(2):
                        stt(Sst[:, hd, a, :], Sst[:, hd, a, :], cd, pbs[a][0][:, :], ALU.mult, ALU.add,
                            [Sb[hd][a], pbs[a][1]], [Sb[hd][a]])
                        act(Sbf[:, hd, a, :], Sst[:, hd, a, :], AF.Copy, [Sb[hd][a]], [Sbfb[hd][a]])
                else:
                    ob, obb, obi = pb(hold=True)
                    mm(ob[:, :], sTm[:, si, :], vv[p][:, c, :], True, False, [sTmb[si], vb[p][c]], [obb])
                    for b in range(NB):
                        u = hd * NB + b
                        ui = u % 4
                        u2 = u % 3
                        issue_s0(u)
                        issue_s0(u + 1)
                        issue_s0(u + 2)
                        issue_s0(u + 3)
                        for a in range(2):
                            act(S0bf[u2][:, a, :], S0[ui][:, a, :], AF.Copy, [S0b[ui]], [S0bfb[u2]])
                        for a in range(2):
                            mm(ob[:, :], Qz[:, a, b, :], S0bf[u2][:, a, :], False, (b == NB - 1 and a == 1), [Qzb, S0bfb[u2]], [obb])
                        for a in range(2):
                            pk, pkb, _ = pb()
                            mm(pk[:, :], KZ[:, a, b, :], vv[p][:, c, :], True, True, [KZb, vb[p][c]], [pkb])
                            stt(Sn[ui][:, a, :], S0[ui][:, a, :], cd, pk[:, :], ALU.mult, ALU.add, [S0b[ui], pkb], [Snb[ui]])
                        dma("pool", srs[b, hd].rearrange("(a p) e -> p a e", p=128), Sn[ui][:, :, :], [Snb[ui]], [], f"sn{ui}")
                    release(obi)
                gs = gidx[0] % 2
                gidx[0] += 1
                gn_gate(ob[:, :], obb, gg[p][:, c, :], gb[p][c], uu[:, gs, :], ub[gs], on[:, gs, :], onb[gs], (hd * nch + c) % 4)
                cst[(hd, c)] = gs

            def c_tail(hd, c):
                gs = cst[(hd, c)]
                cs = slice(c * 128, (c + 1) * 128)
                tb, tbb, _ = pb()
                tbv = tb[:, :].bitcast(BF16)
                for ec in range(4):
                    tr(tbv[:, ec * 128:(ec + 1) * 128], uu[:, gs, ec * 128:(ec + 1) * 128], ident_bf, [ub[gs], cbb], [tbb])
                act(uT[:, hd * 4:(hd + 1) * 4, cs], tbv[:, 0:512].rearrange("p (a b) -> p a b", a=4), AF.Copy, [tbb], [uTb[hd][c]])

            gidx = [0]
            for pc in proj_pieces(0):
                pc()
            for hd in range(RH):
                q_ = proj_pieces(hd + 1) if hd + 1 < RH else []
                for c in range(nch):
                    c_main(hd, c)
                    if q_:
                        q_.pop(0)()
                    if c > 0:
                        c_tail(hd, c - 1)
                while len(q_) > 1:
                    q_.pop(0)()
                c_tail(hd, nch - 1)
                while q_:
                    q_.pop(0)()
            for half in range(2):
                bs = [pb(hold=True) for _ in range(4)]
                for ecg in range(2):
                    wou, wob = wget(("wo", half, ecg))
                    for dm in range(4):
                        for ec in range(8):
                            e_ = ecg * 8 + ec
                            mm(bs[dm][0][:, :NT], wou[:, ec, dm * 128:(dm + 1) * 128], uT[:, e_, :NT], e_ == 0, e_ == 15,
                               [wob] + uTb[e_ // 4], [bs[dm][1]])
                for dm in range(4):
                    kc = half * 4 + dm
                    tt(cx.xT[:, kc, :NT], bs[dm][0][:, :NT], cx.xT[:, kc, :NT], ALU.add, [bs[dm][1], cx.xb[kc]], [cx.xb[kc]])
                    release(bs[dm][2])
            if (not sample) and tile_i == NPT - 1:
                dma("sp", srp.rearrange("h (a p) e -> p h a e", p=128), Sst[:, :, :, :],
                    [Sb[h][a] for h in range(RH) for a in range(2)], [], "srp")
            return R

        def layer1(cx, tile_i, sample):
            NT = cx.NT
            nblk = NT // 128
            A.reset()
            qT1 = A.alloc("qT1", [128, 8, NT], BF16)
            qT1b = [A.buf("qT1") for _ in range(8)]
            kvT = A.alloc("kvT", [128, 2, NT], F32)
            kvTb = [A.buf("kT1"), A.buf("vT1")]
            oT = A.alloc("oT", [128, 8, NT], BF16)
            oTb2 = [[A.buf("oT") for _ in range(nblk)] for _ in range(8)]
            ee = A.alloc("ee", [128, 8, 128], BF16)
            eeb = [A.buf("ee") for _ in range(8)]
            if sample:
                pT5 = A.alloc("pT", [128, 4, NB, 32], BF16)
                qS = A.alloc("qS", [128, 2, NB, 32], BF16)
                qSb = A.buf("qS")
            else:
                pT = A.alloc("pT", [128, 12, 128], BF16)
            pTb = [A.buf("pT") for _ in range(16 if sample else 12)]
            rec = A.alloc("rec", [128, 2, 512], F32)
            recb = [A.buf("rec0"), A.buf("rec1")]
            tok = A.alloc("tok", [128, 2, 128], F32)
            tokb = [A.buf("tok0"), A.buf("tok1")]
            if sample:
                Kc = [A.alloc(f"Kc{i}", [128, 2, 128], F32) for i in range(2)]
                Kcb = [A.buf("Kc0"), A.buf("Kc1")]
                Kcz = [A.alloc(f"Kcz{i}", [128, 4, 128], BF16) for i in range(2)]
                Kczb = [A.buf("Kcz0"), A.buf("Kcz1")]
                Vcz = [A.alloc(f"Vcz{i}", [128, 4, 128], BF16) for i in range(2)]
                Vczb = [A.buf("Vcz0"), A.buf("Vcz1")]
                kzc = [A.alloc(f"kzc{i}", [128, 4, 128], BF16) for i in range(2)]
                kzcb = [A.buf("kzc0"), A.buf("kzc1")]
                e32 = A.alloc("e32", [128, 4, 32], BF16)
                e32b = [A.buf("e32") for _ in range(4)]
                pTc = [A.alloc(f"pTc{i}", [128, 4, 32], BF16) for i in range(2)]
                pTcb = [A.buf("pTc0"), A.buf("pTc1")]

                def issue_kc(b):
                    if b < NB:
                        dma("sp", Kc[b % 2][:, 0, :], ck[b], [], [Kcb[b % 2]], f"kc{b % 2}")
                        dma("sp", Kc[b % 2][:, 1, :], cv[b], [], [Kcb[b % 2]], f"kc{b % 2}")
                issue_kc(0)
                issue_kc(1)
            for qu in range(2):
                wu, wub = wget(("q1", qu))
                for j in range(4):
                    hp = qu * 4 + j
                    bk, bb, _ = pb()
                    for kc in range(KC):
                        mm(bk[:, :NT], wu[:, kc, j * 128:(j + 1) * 128], cx.hT[:, kc, :NT], kc == 0, kc == KC - 1, [wub, cx.hb[kc]], [bb])
                    act(qT1[:, hp, :], bk[:, :NT], AF.Identity, [bb, spb], [qT1b[hp]], bias=sp[:, SP_BQ + hp:SP_BQ + hp + 1])
            wu, wub = wget(("kz",))
            for var in range(4):
                bk, bb, _ = pb()
                for kc in range(KC):
                    mm(bk[:, :NT], wu[:, kc, var * 128:(var + 1) * 128], cx.hT[:, kc, :NT], kc == 0, kc == KC - 1, [wub, cx.hb[kc]], [bb])
                act(kz[:, var, 128:128 + NT], bk[:, :NT], AF.Identity, [bb, spb], [kzb[1 + i] for i in range(nblk)],
                    bias=sp[:, SP_BKZ + var:SP_BKZ + var + 1])
            wu, wub = wget(("kv",))
            for j in range(2):
                bk, bb, _ = pb()
                for kc in range(KC):
                    mm(bk[:, :NT], wu[:, kc, j * 128:(j + 1) * 128], cx.hT[:, kc, :NT], kc == 0, kc == KC - 1, [wub, cx.hb[kc]], [bb])
                act(kvT[:, j, :], bk[:, :NT], AF.Identity, [bb, spb], [kvTb[j]], bias=sp[:, SP_BK + j:SP_BK + j + 1])
            for blk in range(nblk):
                cs = slice(blk * 128, (blk + 1) * 128)
                tb, tbb, _ = pb()
                tr(tb[:, 0:128], kvT[:, 1, cs], ident_f, [kvTb[1], cfb], [tbb])
                for var in range(4):
                    kvh, par = var // 2, var % 2
                    act(vz[:, 1 + blk, var, par * 64:(par + 1) * 64], tb[:, kvh * 64:(kvh + 1) * 64], AF.Copy, [tbb], [vzb[1 + blk]])
                last = sample or (tile_i == NPT - 1 and blk == nblk - 1)
                if last:
                    tt_i = 0
                    act(tok[:, 1, :], tb[:, 0:128], AF.Copy, [tbb], [tokb[1]])
                    tb2, tbb2, _ = pb()
                    tr(tb2[:, 0:128], kvT[:, 0, cs], ident_f, [kvTb[0], cfb], [tbb2])
                    act(tok[:, 0, :], tb2[:, 0:128], AF.Copy, [tbb2], [tokb[0]])
                    if sample:
                        for b in range(NB):
                            dma("sp", ks_o[b, 128 - DS:128, :], tok[b * DS:(b + 1) * DS, 0, :], [tokb[0]], [], "ko")
                            dma("sp", vs_o[b, 128 - DS:128, :], tok[b * DS:(b + 1) * DS, 1, :], [tokb[1]], [], "ko")
                        dma("sp", ks_o[:, 0:128 - DS, :], ck[:, DS:128, :], [], [], "ko")
                        dma("sp", vs_o[:, 0:128 - DS, :], cv[:, DS:128, :], [], [], "ko")
                    else:
                        dma("sp", kp_o[:, :], tok[:, 0, :], [tokb[0]], [], "ko")
                        dma("sp", vp_o[:, :], tok[:, 1, :], [tokb[1]], [], "ko")
            m_cur = cb[:, CB_MCUR:CB_MCUR + 128]
            m_prev = cb[:, CB_MPREV:CB_MPREV + 128]
            m_new = cb[:, CB_MNEW:CB_MNEW + 128]
            ei = [0]

            if not sample:
                its = [(blk, hp) for blk in range(nblk) for hp in range(8)]
                sres = {}

                def a_scores(i):
                    blk, hp = its[i]
                    gblk = tile_i * (TT // 128) + blk
                    qs = slice(blk * 128, (blk + 1) * 128)
                    kbs = [blk + 1] + ([blk] if gblk > 0 else [])
                    nk = len(kbs)
                    kvh = hp // 4
                    items = []
                    sbk, sbb, _ = pb()
                    col = 0
                    for par in range(2):
                        var = kvh * 2 + par
                        for kbi, kblk in enumerate(kbs):
                            pi = (i % 3) * 4 + par * 2 + kbi
                            mm(sbk[:, col * 128:(col + 1) * 128], kz[:, var, kblk * 128:(kblk + 1) * 128], qT1[:, hp, qs], True, True,
                               [kzb[kblk], qT1b[hp]], [sbb])
                            col += 1
                            items.append((par, var, kblk, pi))
                    w_ = 2 * nk * 128
                    e_i = i % 2
                    act(ee[:, e_i * 4:e_i * 4 + 2 * nk, :], sbk[:, :w_].rearrange("p (a c) -> p a c", c=128), AF.Exp, [sbb], [eeb[e_i]], scale=0.125)
                    base = (i % 3) * 4
                    dst = pT[:, base:base + 4, :].rearrange("p (a b) c -> p a b c", a=2)[:, :, 0:nk, :]
                    src = ee[:, e_i * 4:e_i * 4 + 2 * nk, :].rearrange("p (a b) c -> p a b c", a=2)
                    msk = cb[:, CB_MCUR:CB_MCUR + nk * 128].rearrange("p (b c) -> p b c", c=128).unsqueeze(1).broadcast_to([128, 2, nk, 128])
                    tt(dst, src, msk, ALU.mult, [eeb[e_i], cbb], [pTb[i % 3]])
                    sres[i] = items

                def a_nd(i):
                    blk, hp = its[i]
                    qs = slice(blk * 128, (blk + 1) * 128)
                    items = sres.pop(i)
                    nb_, nbb, _ = pb()
                    for n, (par, var, kblk, pi) in enumerate(items):
                        mm(nb_[:, :128], vz[:, kblk, var, :], pT[:, pi, :], n == 0, n == len(items) - 1, [vzb[kblk], pTb[i % 3]], [nbb])
                    db_, dbb, _ = pb()
                    for n, (par, var, kblk, pi) in enumerate(items):
                        mm(db_[:, :128], cb[:, CB_ONESZ + par * 128:CB_ONESZ + (par + 1) * 128], pT[:, pi, :], n == 0, n == len(items) - 1,
                           [cbb, pTb[i % 3]], [dbb])
                    ri = i % 2
                    act(rec[:, ri, :128], db_[:, :128], AF.Identity, [dbb, esinkb], [recb[ri]], bias=esink[:, hp:hp + 1])
                    recip(rec[:, ri, 128:256], rec[:, ri, :128], [recb[ri]], [recb[ri]])
                    tt(oT[:, hp, qs], nb_[:, :128], rec[:, ri, 128:256], ALU.mult, [nbb, recb[ri]], [oTb2[hp][blk]])

                a_scores(0)
                a_scores(1)
                for i in range(len(its)):
                    if i + 2 < len(its):
                        a_scores(i + 2)
                    a_nd(i)
                S.op("act", lambda e: e.activation(out=kz[:, :, 0:128], in_=kz[:, :, NT:NT + 128], func=AF.Copy), [kzb[nblk]], [kzb[0]])
                S.op("dve", lambda e: e.tensor_copy(out=vz[:, 0, :, :], in_=vz[:, nblk, :, :]), [vzb[nblk]], [vzb[0]])
            else:
                qs = slice(0, 128)
                for gi_ in range(4):
                    par, kvh = gi_ // 2, gi_ % 2
                    var = kvh * 2 + par
                    sbk, sbb, _ = pb()
                    for m in range(4):
                        hp = kvh * 4 + m
                        mm(sbk[:, m * 128:(m + 1) * 128], kz[:, var, 128:256], qT1[:, hp, qs], True, True, [kzb[1], qT1b[hp]], [sbb])
                    e_i = gi_ % 2
                    act(ee[:, e_i * 4:e_i * 4 + 4, :], sbk[:, :].rearrange("p (a c) -> p a c", c=128), AF.Exp, [sbb], [eeb[e_i]], scale=0.125)
                    tt(pT5[:, par * 2 + kvh, :, :].rearrange("p b (m i) -> p b m i", m=4),
                       ee[:, e_i * 4:e_i * 4 + 4, :].rearrange("p m (b i) -> p b m i", b=NB),
                       m_new.rearrange("p (b i) -> p b i", b=NB).unsqueeze(2).broadcast_to([128, NB, 4, DS]), ALU.mult,
                       [eeb[e_i], cbb], [pTb[gi_]])
                for kvh in range(2):
                    S.op("dve", (lambda o_, i_: (lambda e: e.tensor_copy(out=o_, in_=i_)))(
                        qS[:, kvh, :, :].rearrange("p b (m i) -> p b m i", m=4),
                        qT1[:, kvh * 4:(kvh + 1) * 4, :].rearrange("p m (b i) -> p b m i", b=NB)),
                        qT1b[kvh * 4:(kvh + 1) * 4], [qSb])
                nbk = [pb(hold=True) for _ in range(2)]
                dbk = [pb(hold=True) for _ in range(2)]
                m_cache = cb[:, CB_MCACHE:CB_MCACHE + 32].rearrange("p (m i) -> p m i", m=4)
                memset("dve", Kcz[0][:, :, :], 0.0, [Kczb[0]])
                memset("dve", Kcz[1][:, :, :], 0.0, [Kczb[1]])
                memset("dve", Vcz[0][:, :, :], 0.0, [Vczb[0]])
                memset("dve", Vcz[1][:, :, :], 0.0, [Vczb[1]])

                def s1(b):
                    ui = b % 2
                    for par in range(2):
                        kdst = Kcz[ui][:, par:4:2, par * 64:(par + 1) * 64] if False else \
                            Kcz[ui][:, :, :].rearrange("p (k r) c -> p k r c", k=2)[:, :, par, par * 64:(par + 1) * 64]
                        vdst = Vcz[ui][:, :, :].rearrange("p (k r) c -> p k r c", k=2)[:, :, par, par * 64:(par + 1) * 64]
                        ksrc = Kc[ui][:, 0, :].rearrange("p (k d) -> p k d", k=2)
                        vsrc = Kc[ui][:, 1, :].rearrange("p (k d) -> p k d", k=2)
                        S.op("dve", (lambda o_, i_: (lambda e: e.tensor_copy(out=o_, in_=i_)))(kdst, ksrc), [Kcb[ui]], [Kczb[ui]])
                        act(vdst, vsrc, AF.Copy, [Kcb[ui]], [Vczb[ui]])
                    tb, tbb, _ = pb()
                    tbv = tb[:, :].bitcast(BF16)
                    for var in range(4):
                        tr(tbv[:, var * 128:(var + 1) * 128], Kcz[ui][:, var, :], ident_bf, [Kczb[ui], cbb], [tbb])
                    act(kzc[ui][:, :, :], tbv[:, 0:512].rearrange("p (a b) -> p a b", a=4), AF.Copy, [tbb], [kzcb[ui]])

                def s2(b):
                    ui = b % 2
                    sbk, sbb, _ = pb()
                    for var in range(4):
                        kvh, par = var // 2, var % 2
                        mm(sbk[:, var * 32:(var + 1) * 32], kzc[ui][:, var, :], qS[:, kvh, b, :], True, True, [kzcb[ui], qSb], [sbb])
                    act(e32[:, :, :], sbk[:, 0:128].rearrange("p (v c) -> p v c", v=4), AF.Exp, [sbb], [e32b[0]], scale=0.125)
                    tt(pTc[ui][:, :, :].rearrange("p v (m i) -> p v m i", m=4), e32[:, :, :].rearrange("p v (m i) -> p v m i", m=4),
                       m_cache.unsqueeze(1).broadcast_to([128, 4, 4, DS]), ALU.mult, [e32b[0], cbb], [pTcb[ui]])

                def s3(b):
                    ui = b % 2
                    for kvh in range(2):
                        for (bk3, lhs_kind) in ((nbk[kvh], "v"), (dbk[kvh], "o")):
                            outv = bk3[0][:, b * 32:(b + 1) * 32]
                            n = 0
                            for par in range(2):
                                var = kvh * 2 + par
                                lhs = Vcz[ui][:, var, :] if lhs_kind == "v" else cb[:, CB_ONESZ + par * 128:CB_ONESZ + (par + 1) * 128]
                                rds = [Vczb[ui] if lhs_kind == "v" else cbb, pTcb[ui]]
                                mm(outv, lhs, pTc[ui][:, var, :], n == 0, False, rds, [bk3[1]])
                                n += 1
                            for par in range(2):
                                var = kvh * 2 + par
                                lhs = vz[:, 1, var, :] if lhs_kind == "v" else cb[:, CB_ONESZ + par * 128:CB_ONESZ + (par + 1) * 128]
                                rhs = pT5[:, par * 2 + kvh, b, :]
                                rds = [vzb[1] if lhs_kind == "v" else cbb, pTb[par * 2 + kvh]]
                                mm(outv, lhs, rhs, False, par == 1, rds, [bk3[1]])

                s1(0)
                for b in range(NB):
                    s2(b)
                    if b >= 1:
                        s3(b - 1)
                    issue_kc(b + 2)
                    if b + 1 < NB:
                        s1(b + 1)
                s3(NB - 1)
                for kvh in range(2):
                    dv = dbk[kvh][0][:, :].rearrange("p (b m i) -> p b m i", b=NB, m=4)
                    nv = nbk[kvh][0][:, :].rearrange("p (b m i) -> p b m i", b=NB, m=4)
                    r0 = rec[:, 0, :].rearrange("p (b m i) -> p b m i", b=NB, m=4)
                    r1 = rec[:, 1, :].rearrange("p (b m i) -> p b m i", b=NB, m=4)
                    for m in range(4):
                        hp = kvh * 4 + m
                        ts(r0[:, :, m, :], dv[:, :, m, :], esink[:, hp:hp + 1], None, ALU.add, ALU.bypass,
                           [dbk[kvh][1], esinkb], [recb[0]])
                    recip(rec[:, 1, :], rec[:, 0, :], [recb[0]], [recb[1]])
                    for m in range(4):
                        hp = kvh * 4 + m
                        tt(oT[:, hp, :].rearrange("p (b i) -> p b i", b=NB), nv[:, :, m, :], r1[:, :, m, :], ALU.mult,
                           [nbk[kvh][1], recb[1]], oTb2[hp])
                    release(nbk[kvh][2])
                    release(dbk[kvh][2])
            for half in range(2):
                wu, wub = wget(("wo1", half))
                for dm in range(4):
                    kc = half * 4 + dm
                    bk, bb, _ = pb()
                    for hp in range(8):
                        mm(bk[:, :NT], wu[:, hp, dm * 128:(dm + 1) * 128], oT[:, hp, :], hp == 0, hp == 7, [wub] + oTb2[hp], [bb])
                    stt(cx.xT[:, kc, :NT], bk[:, :NT], sp[:, SP_BO + kc:SP_BO + kc + 1], cx.xT[:, kc, :NT], ALU.add, ALU.add,
                        [bb, spb, cx.xb[kc]], [cx.xb[kc]])

        xin_p = xTp.rearrange("(kc p) t -> p kc t", p=128)
        yout_p = yTp.rearrange("(kc p) t -> p kc t", p=128)
        xin_s = xTs.rearrange("(kc p) t -> p kc t", p=128)
        yout_s = yTs.rearrange("(kc p) t -> p kc t", p=128)

        def load_rope(i, c0, n):
            dma("sp", rope[i][:, :, :n], rope_d[:, :, c0:c0 + n].rearrange("c p t -> p c t"), [], [ropeb[i]], f"rope{i}")

        dma("sp", pcx[0].xT[:, :, :], xin_p[:, :, 0:TT], [], pcx[0].xb, "xin0")
        load_rope(0, 0, TT)
        dma("sp", scx.xT[:, :, :], xin_s[:, :, :], [], scx.xb, "xins")
        for t in range(NPT):
            cx = pcx[t % 2]
            last = t == NPT - 1
            if not last:
                load_rope((t + 1) % 2, (t + 1) * TT, TT)
                dma("sp", pcx[(t + 1) % 2].xT[:, :, :], xin_p[:, :, (t + 1) * TT:(t + 2) * TT], [], pcx[(t + 1) % 2].xb, f"xin{(t + 1) % 2}")
            cxs = [cx, scx] if last else [cx]

            def dbg(i):
                if DEBUG and t == DEBUG_TILE:
                    dma("sp", dbg_o[i].rearrange("(kc p) t -> p kc t", p=128)[:, :, :TT], cx.xT[:, :, :TT], cx.xb, [], "dbg")
            if t == 0:
                rmsnorm(cx, 0)
            if last:
                load_rope((t + 1) % 2, SEQ, NS)
                rmsnorm(scx, 0)
                idle = pcx[(t + 1) % 2]
                R_ = layer0(cx, t, False, t % 2, rider=(scx, (t + 1) % 2, idle.xT, idle.xb))
                layer0(scx, NPT, True, (t + 1) % 2, pre=R_)
            else:
                layer0(cx, t, False, t % 2)
            dbg(0)
            for c_ in cxs:
                rmsnorm(c_, 1)
            ffn(cxs, 0)
            dbg(1)
            if last:
                rmsnorm(scx, 2)
            rmsnorm(cx, 2)
            layer1(cx, t, False)
            if last:
                layer1(scx, NPT, True)
            dbg(2)
            for c_ in cxs:
                rmsnorm(c_, 3)
            ffn(cxs, 1)
            dbg(3)
            if not last:
                rmsnorm(pcx[(t + 1) % 2], 0)
            rmsnorm(cx, 4, to_x=True)
            dma("sp", yout_p[:, :, t * TT:(t + 1) * TT], cx.xT[:, :, :], cx.xb, [], "yout")
            if last:
                rmsnorm(scx, 4, to_x=True)
                dma("sp", yout_s[:, :, :], scx.xT[:, :, :], scx.xb, [], "yout")
        outs = Buf("outs")
        for name in list(S.dcount):
            if name in ("yout", "srp", "ko", "dbg") or name.startswith("sn"):
                outs.rs[("d", name)] = S.dcount[name]
        S.op("sp", lambda e: e.nop(), (), [outs])
        assert wstate["k"] == len(plan)
        S.emit(nc, es)
    return nc


_CACHE = {}


def _host_layout(inp):
    f = np.float32
    g = lambda k: np.ascontiguousarray(np.asarray(inp[k], dtype=f))
    x_prompt, x_sample = g("x_prompt"), g("x_sample")
    state_ret, ck, cv = g("state_ret")[0], g("cache_swa_k")[0], g("cache_swa_v")[0]
    wq, wk = g("ret_w_q")[0], g("ret_w_k")[0]
    wqk = np.concatenate([np.concatenate([wq[:, h * 256:(h + 1) * 256], wk[:, h * 256:(h + 1) * 256]], axis=1) for h in range(RH)], axis=1)
    wqkv = g("swa_w_qkv")[0]
    bqkv = g("swa_b_qkv")[0]
    wq1 = wqkv[:, :1024]
    wk1 = wqkv[:, 1024:1152]
    wv1 = wqkv[:, 1152:1280]
    wkz = np.zeros((D, 4, 128), f)
    bkz = np.zeros((4, 128), f)
    for kvh in range(2):
        for par in range(2):
            wkz[:, kvh * 2 + par, par * 64:(par + 1) * 64] = wk1[:, kvh * 64:(kvh + 1) * 64]
            bkz[kvh * 2 + par, par * 64:(par + 1) * 64] = bqkv[1024 + kvh * 64:1024 + (kvh + 1) * 64]
    wkz = wkz.reshape(D, 512)
    wkv = np.concatenate([wk1, wv1], axis=1)
    spar = np.zeros((128, SP_N), f)
    gains = [g("norm_mix")[0], g("norm_ffn")[0], g("norm_mix")[1], g("norm_ffn")[1], g("norm_final")]
    for i, v in enumerate(gains):
        spar[:, SP_G + i * 8:SP_G + (i + 1) * 8] = v.reshape(8, 128).T
    spar[:, SP_BQ:SP_BQ + 8] = bqkv[:1024].reshape(8, 128).T
    spar[:, SP_BKZ:SP_BKZ + 4] = bkz.T
    spar[:, SP_BK] = bqkv[1024:1152]
    spar[:, SP_BV] = bqkv[1152:1280]
    spar[:, SP_BO:SP_BO + 8] = g("swa_b_o")[0].reshape(8, 128).T
    sinks = g("swa_sinks")[0]
    spar[:, SP_SINK:SP_SINK + 8] = np.repeat(sinks.reshape(8, 2), 64, axis=1).T
    cf32, cb, rope, _ = _consts()
    shared = {
        "wqk": np.ascontiguousarray(wqk), "wv": g("ret_w_v")[0], "wg": g("ret_w_g")[0], "wo": g("ret_w_o")[0],
        "wq1": np.ascontiguousarray(wq1), "wkz": wkz, "wkv": np.ascontiguousarray(wkv), "wo1": g("swa_w_o")[0],
        "w1": g("ffn_w1"), "w3": g("ffn_w3"), "w2": g("ffn_w2"), "spar": spar, "cf32": cf32, "cb": cb, "rope": rope,
    }
    in_maps = []
    for c in range(NCORES):
        m = dict(shared)
        m["xTp"] = np.ascontiguousarray(x_prompt[c].T)
        m["xTs"] = np.ascontiguousarray(x_sample[c * NB:(c + 1) * NB].reshape(NS, D).T)
        m["st_in"] = np.ascontiguousarray(state_ret[c * NB:(c + 1) * NB])
        m["ck"] = np.ascontiguousarray(ck[c * NB:(c + 1) * NB].reshape(NB, 128, 128))
        m["cv"] = np.ascontiguousarray(cv[c * NB:(c + 1) * NB].reshape(NB, 128, 128))
        in_maps.append(m)
    return in_maps


def kernel(**inputs):
    if "nc" not in _CACHE:
        _CACHE["nc"] = build_program()
    nc = _CACHE["nc"]
    in_maps = _host_layout(inputs)
    res = run_bass_kernel_spmd(nc, in_maps, core_ids=list(range(NCORES)))
    R = res.results
    f = np.float32
    y_prompt = np.stack([R[c]["yTp"].T for c in range(NCORES)]).astype(f)
    y_sample = np.concatenate([R[c]["yTs"].T.reshape(NB, DS, D) for c in range(NCORES)]).astype(f)
    srp = np.stack([R[c]["srp"] for c in range(NCORES)])[None].astype(f)
    srs = np.concatenate([R[c]["srs"] for c in range(NCORES)])[None].astype(f)
    kp = np.stack([R[c]["kp"].reshape(128, 2, 64) for c in range(NCORES)])[None].astype(f)
    vp = np.stack([R[c]["vp"].reshape(128, 2, 64) for c in range(NCORES)])[None].astype(f)
    ks = np.concatenate([R[c]["ks"].reshape(NB, 128, 2, 64) for c in range(NCORES)])[None].astype(f)
    vs = np.concatenate([R[c]["vs"].reshape(NB, 128, 2, 64) for c in range(NCORES)])[None].astype(f)
    return (y_prompt, y_sample, srp, srs, kp, vp, ks, vs)
```
